# Optimizing a Trainium2 kernel written in Bass

```python
import math
import jax, jax.numpy as jnp
from jax import lax
import numpy as np

D_MODEL = 2048
BATCH = 16
SEQ = 2048
DEPTH = 4

N_META = 16
GRID_W = 64
Q_BLOCK = 128
ROPE_THETA = 10000.0
NORM_EPS = 1e-6
N_BRANCH = 3

C_CONV = D_MODEL // 2
CONV_K = 31

GQA_HEADS = 8
GQA_KV_HEADS = 2
GQA_HEAD_DIM = 128

MLA_HEADS = 8
MLA_Q_RANK = D_MODEL // 4
MLA_KV_RANK = D_MODEL // 4
MLA_NOPE_DIM = 128
MLA_ROPE_DIM = 64
MLA_V_DIM = 128
MLA_QK_DIM = MLA_NOPE_DIM + MLA_ROPE_DIM

D_FF = -(-8 * D_MODEL // (3 * 256)) * 256

IN_SIZES = (
    2 * C_CONV,
    GQA_HEADS * GQA_HEAD_DIM,
    GQA_KV_HEADS * GQA_HEAD_DIM,
    GQA_KV_HEADS * GQA_HEAD_DIM,
    MLA_Q_RANK,
    MLA_KV_RANK,
    MLA_ROPE_DIM,
    N_BRANCH * D_MODEL,
)
D_IN = sum(IN_SIZES)

kernel_name = 'hybrid_conv_gqa_mla_encoder'


def rmsnorm(x, g):
    xf = x.astype(jnp.float32)
    y = xf * lax.rsqrt(jnp.mean(xf * xf, axis=-1, keepdims=True) + NORM_EPS)
    return (y * g.astype(jnp.float32)).astype(x.dtype)


def layernorm(x, g, b):
    xf = x.astype(jnp.float32)
    mu = jnp.mean(xf, axis=-1, keepdims=True)
    xc = xf - mu
    y = xc * lax.rsqrt(jnp.mean(xc * xc, axis=-1, keepdims=True) + NORM_EPS)
    return (y * g.astype(jnp.float32) + b.astype(jnp.float32)).astype(x.dtype)


def grid_positions(n_tok):
    rows = n_tok // GRID_W
    row = jnp.repeat(jnp.arange(rows, dtype=jnp.float32), GRID_W)
    col = jnp.tile(jnp.arange(GRID_W, dtype=jnp.float32), rows)
    meta = jnp.zeros((N_META,), jnp.float32)
    return jnp.concatenate([meta, row]), jnp.concatenate([meta, col])


def apply_rope(x, pos):
    dim = x.shape[-1]
    half = dim // 2
    inv = ROPE_THETA ** (-jnp.arange(half, dtype=jnp.float32) / half)
    ang = pos[:, None] * inv[None, :]
    cos = jnp.cos(ang)[None, :, None, :].astype(x.dtype)
    sin = jnp.sin(ang)[None, :, None, :].astype(x.dtype)
    x1, x2 = x[..., :half], x[..., half:]
    return jnp.concatenate([x1 * cos - x2 * sin, x2 * cos + x1 * sin], axis=-1)


def axial_rope(x, row_pos, col_pos):
    half = x.shape[-1] // 2
    return jnp.concatenate([apply_rope(x[..., :half], row_pos),
                            apply_rope(x[..., half:], col_pos)], axis=-1)


def _attend(qb, k, v, scale):
    s = jnp.einsum('bqhgd,bkhd->bhgqk', qb, k, preferred_element_type=jnp.float32) * scale
    p = jax.nn.softmax(s, axis=-1).astype(v.dtype)
    return jnp.einsum('bhgqk,bkhe->bqhge', p, v)


def bidir_block_attention(q, k, v, scale):
    b, l, hq, dk = q.shape
    hkv, dv = k.shape[2], v.shape[-1]
    g = hq // hkv
    q = q.reshape(b, l, hkv, g, dk)
    out_meta = _attend(q[:, :N_META], k, v, scale)
    n_tok = l - N_META
    nblk = n_tok // Q_BLOCK
    qr = q[:, N_META:].reshape(b, nblk, Q_BLOCK, hkv, g, dk).transpose(1, 0, 2, 3, 4, 5)
    out_r = lax.map(lambda qb: _attend(qb, k, v, scale), qr)
    out_r = out_r.transpose(1, 0, 2, 3, 4, 5).reshape(b, n_tok, hkv, g, dv)
    out = jnp.concatenate([out_meta, out_r], axis=1)
    return out.reshape(b, l, hq * dv)


def conv_branch(u2, dw, cb, ln_g, ln_b, w_pw):
    a, gte = jnp.split(u2, 2, axis=-1)
    z = a * jax.nn.sigmoid(gte)
    z = lax.conv_general_dilated(z, dw, window_strides=(1,),
                                 padding=[(CONV_K // 2, CONV_K // 2)],
                                 dimension_numbers=('NWC', 'WIO', 'NWC'),
                                 feature_group_count=C_CONV) + cb
    z = jax.nn.silu(layernorm(z, ln_g, ln_b))
    return z @ w_pw


def gqa_branch(q, k, v, qn_g, kn_g, w_o, row_pos, col_pos):
    b, l, _ = q.shape
    q = rmsnorm(q.reshape(b, l, GQA_HEADS, GQA_HEAD_DIM), qn_g)
    k = rmsnorm(k.reshape(b, l, GQA_KV_HEADS, GQA_HEAD_DIM), kn_g)
    v = v.reshape(b, l, GQA_KV_HEADS, GQA_HEAD_DIM)
    q = axial_rope(q, row_pos, col_pos)
    k = axial_rope(k, row_pos, col_pos)
    o = bidir_block_attention(q, k, v, 1.0 / math.sqrt(GQA_HEAD_DIM))
    return o @ w_o


def mla_branch(cq, ckv, kpe, qn_g, w_uq, kvn_g, w_ukv, w_o, row_pos, col_pos):
    b, l, _ = cq.shape
    q = (rmsnorm(cq, qn_g) @ w_uq).reshape(b, l, MLA_HEADS, MLA_QK_DIM)
    q_nope, q_pe = q[..., :MLA_NOPE_DIM], q[..., MLA_NOPE_DIM:]
    q_pe = axial_rope(q_pe, row_pos, col_pos)
    kv = (rmsnorm(ckv, kvn_g) @ w_ukv).reshape(b, l, MLA_HEADS, MLA_NOPE_DIM + MLA_V_DIM)
    k_nope, v = kv[..., :MLA_NOPE_DIM], kv[..., MLA_NOPE_DIM:]
    k_pe = axial_rope(kpe.reshape(b, l, 1, MLA_ROPE_DIM), row_pos, col_pos)
    k = jnp.concatenate([k_nope, jnp.broadcast_to(k_pe, (b, l, MLA_HEADS, MLA_ROPE_DIM))], axis=-1)
    q = jnp.concatenate([q_nope, q_pe], axis=-1)
    o = bidir_block_attention(q, k, v, 1.0 / math.sqrt(MLA_QK_DIM))
    return o @ w_o


def setup_inputs(seed: int = 0) -> dict:
    key = jax.random.key(seed)
    ks = jax.random.split(key, 24)

    def nrm(k, shape, scale):
        return jax.random.normal(k, shape, jnp.float32) * scale

    def gain(k, shape):
        return 1.0 + 0.02 * jax.random.normal(k, shape, jnp.float32)

    L = DEPTH
    return {
        'x': nrm(ks[0], (BATCH, SEQ, D_MODEL), 1.0),
        'meta_tokens': nrm(ks[1], (N_META, D_MODEL), 1.0),
        'mix_norm_g': gain(ks[2], (L, D_MODEL)),
        'w_in': nrm(ks[3], (L, D_MODEL, D_IN), D_MODEL ** -0.5),
        'conv_dw': nrm(ks[4], (L, CONV_K, 1, C_CONV), CONV_K ** -0.5),
        'conv_b': nrm(ks[5], (L, C_CONV), 0.02),
        'conv_ln_g': gain(ks[6], (L, C_CONV)),
        'conv_ln_b': nrm(ks[7], (L, C_CONV), 0.02),
        'w_conv_out': nrm(ks[8], (L, C_CONV, D_MODEL), C_CONV ** -0.5),
        'gqa_q_norm_g': gain(ks[9], (L, GQA_HEAD_DIM)),
        'gqa_k_norm_g': gain(ks[10], (L, GQA_HEAD_DIM)),
        'w_gqa_out': nrm(ks[11], (L, GQA_HEADS * GQA_HEAD_DIM, D_MODEL), (GQA_HEADS * GQA_HEAD_DIM) ** -0.5),
        'mla_q_norm_g': gain(ks[12], (L, MLA_Q_RANK)),
        'w_mla_uq': nrm(ks[13], (L, MLA_Q_RANK, MLA_HEADS * MLA_QK_DIM), MLA_Q_RANK ** -0.5),
        'mla_kv_norm_g': gain(ks[14], (L, MLA_KV_RANK)),
        'w_mla_ukv': nrm(ks[15], (L, MLA_KV_RANK, MLA_HEADS * (MLA_NOPE_DIM + MLA_V_DIM)), MLA_KV_RANK ** -0.5),
        'w_mla_out': nrm(ks[16], (L, MLA_HEADS * MLA_V_DIM, D_MODEL), (MLA_HEADS * MLA_V_DIM) ** -0.5),
        'gate_b': nrm(ks[17], (L, N_BRANCH * D_MODEL), 0.1),
        'w_out': nrm(ks[18], (L, D_MODEL, D_MODEL), 0.5 * D_MODEL ** -0.5),
        'ffn_norm_g': gain(ks[19], (L, D_MODEL)),
        'w_ffn_gate': nrm(ks[20], (L, D_MODEL, D_FF), D_MODEL ** -0.5),
        'w_ffn_up': nrm(ks[21], (L, D_MODEL, D_FF), D_MODEL ** -0.5),
        'w_ffn_down': nrm(ks[22], (L, D_FF, D_MODEL), 0.5 * D_FF ** -0.5),
        'final_norm_g': gain(ks[23], (D_MODEL,)),
    }


def reference(x, meta_tokens, mix_norm_g, w_in, conv_dw, conv_b, conv_ln_g, conv_ln_b, w_conv_out,
              gqa_q_norm_g, gqa_k_norm_g, w_gqa_out, mla_q_norm_g, w_mla_uq, mla_kv_norm_g,
              w_mla_ukv, w_mla_out, gate_b, w_out, ffn_norm_g, w_ffn_gate, w_ffn_up, w_ffn_down,
              final_norm_g):
    b, n_tok, d = x.shape
    meta = jnp.broadcast_to(meta_tokens.astype(x.dtype)[None], (b, N_META, d))
    h = jnp.concatenate([meta, x], axis=1)
    row_pos, col_pos = grid_positions(n_tok)

    split_points = []
    acc = 0
    for s in IN_SIZES[:-1]:
        acc += s
        split_points.append(acc)

    for i in range(DEPTH):
        u = rmsnorm(h, mix_norm_g[i])
        proj = u @ w_in[i]
        (u_conv, q_g, k_g, v_g, c_q, c_kv, k_pe, gate_logits) = jnp.split(proj, split_points, axis=-1)

        y_a = conv_branch(u_conv, conv_dw[i], conv_b[i], conv_ln_g[i], conv_ln_b[i], w_conv_out[i])
        y_b = gqa_branch(q_g, k_g, v_g, gqa_q_norm_g[i], gqa_k_norm_g[i], w_gqa_out[i], row_pos, col_pos)
        y_c = mla_branch(c_q, c_kv, k_pe, mla_q_norm_g[i], w_mla_uq[i], mla_kv_norm_g[i],
                         w_mla_ukv[i], w_mla_out[i], row_pos, col_pos)

        gates = jax.nn.sigmoid(gate_logits + gate_b[i]).reshape(b, -1, N_BRANCH, d)
        merged = gates[:, :, 0] * y_a + gates[:, :, 1] * y_b + gates[:, :, 2] * y_c
        h = h + merged @ w_out[i]

        v = rmsnorm(h, ffn_norm_g[i])
        h = h + (jax.nn.silu(v @ w_ffn_gate[i]) * (v @ w_ffn_up[i])) @ w_ffn_down[i]

    h = rmsnorm(h, final_norm_g)
    return h[:, N_META:]
```

```python
import contextlib
import numpy as np
import concourse.bass as bass
import concourse.mybir as mybir
from concourse.bass_utils import run_bass_kernel_spmd

F32, BF16, U8 = mybir.dt.float32, mybir.dt.bfloat16, mybir.dt.uint8
AF = mybir.ActivationFunctionType
ALU = mybir.AluOpType

D = 2048; SEQ = 2048; NM = 16; L = SEQ + NM; DIN = 10816; DFF = 5632; CC = 1024
EPS = 1e-6
T512 = [(i * 512, min(512, L - i * 512)) for i in range((L + 511) // 512)]
T256 = [(i * 256, min(256, L - i * 256)) for i in range((L + 255) // 256)]
KCH = [(i * 128, min(128, L - i * 128)) for i in range((L + 127) // 128)]
VC = {}
_o = 0
for _n, _w in [("mixg", 16), ("ffng", 16), ("gateb", 48), ("convb", 8), ("lng", 8), ("lnb", 8), ("dw", 248),
               ("gqg", 1), ("gkg", 1), ("mqg", 4), ("mkvg", 4), ("fing", 16)]:
    VC[_n] = _o; _o += _w
NV = _o
ND = 24


class Sched:
    def __init__(self):
        self.ops = []
        self.lastw = {}
        self.readers = {}
        self.lastop = {}
        self.dmas_since_bar = []
        self.nbar = 0

    def add(self, eng, meth, args=(), kw=None, reads=(), writes=(), dma=False):
        deps = set()
        for k in reads:
            if k in self.lastw: deps.add(self.lastw[k])
        for k in writes:
            if k in self.lastw: deps.add(self.lastw[k])
            deps.update(self.readers.get(k, {}).values())
        idx = len(self.ops)
        self.ops.append(dict(eng=eng, meth=meth, args=args, kw=kw or {}, deps=deps, dma=dma, inc=False, kind="op"))
        ek = ("dma", idx) if dma else eng
        for k in reads:
            self.readers.setdefault(k, {})[ek] = idx
        for k in writes:
            self.lastw[k] = idx; self.readers[k] = {}
        if dma: self.dmas_since_bar.append(idx)
        else: self.lastop[eng] = idx
        return idx

    def barrier(self):
        deps = set(self.lastop.values()) | set(self.dmas_since_bar)
        self.nbar += 1
        self.ops.append(dict(eng="sp", kind="bar", deps=deps, dma=False, inc=False, val=self.nbar))
        for e in ("pe", "act", "dve", "pool"):
            self.ops.append(dict(eng=e, kind="barwait", deps=set(), dma=False, inc=False, val=self.nbar))
        self.lastw = {}; self.readers = {}; self.lastop = {}; self.dmas_since_bar = []

    def emit(self, nc, stack):
        ops = self.ops
        for op in ops:
            for d in op["deps"]:
                dop = ops[d]
                if dop["dma"]: continue
                if dop["eng"] == "pe" and op["eng"] == "pe" and not op["dma"] and op["kind"] == "op": continue
                dop["inc"] = True
        esem = {e: stack.enter_context(nc.semaphore("c_" + e)) for e in ("pe", "act", "dve", "pool")}
        bsem = stack.enter_context(nc.semaphore("bar"))
        dsem = {q: [stack.enter_context(nc.semaphore("d_%s%d" % (q, i))) for i in range(ND)] for q in ("sp", "pool")}
        cnt = {e: 0 for e in esem}
        dcnt = {"sp": 0, "pool": 0}
        for op in ops:
            if op["kind"] != "op": continue
            if op["dma"]:
                q = op["eng"]; j = dcnt[q]; dcnt[q] += 1
                op["dslot"] = (q, j % ND); op["dval"] = 16 * (j // ND + 1)
            elif op["inc"]:
                cnt[op["eng"]] += 1; op["cnt"] = cnt[op["eng"]]
        streams = {e: [] for e in ("pe", "act", "dve", "pool", "sp")}
        for op in ops: streams[op["eng"]].append(op)
        engs = {"pe": nc.tensor, "act": nc.scalar, "dve": nc.vector, "pool": nc.gpsimd, "sp": nc.sync}
        final_d = {}
        for op in ops:
            if op["kind"] == "op" and op["dma"]: final_d[op["dslot"]] = op["dval"]

        def run_stream(name, E):
            known = {}
            def need(key, sem, val):
                if known.get(key, 0) >= val: return
                known[key] = val
                E.wait_ge(sem, val)
            for op in streams[name]:
                if op["kind"] == "barwait":
                    E.wait_ge(bsem, op["val"]); continue
                for d in sorted(op["deps"]):
                    dop = ops[d]
                    if dop["dma"]:
                        q, s = dop["dslot"]; need(("d", q, s), dsem[q][s], dop["dval"])
                    elif dop["inc"]:
                        if dop["eng"] == name and name == "pe" and not op["dma"] and op["kind"] == "op": continue
                        need(("e", dop["eng"]), esem[dop["eng"]], dop["cnt"])
                if op["kind"] == "bar":
                    E.sem_inc(bsem, 1); continue
                if op["dma"]:
                    q, s = op["dslot"]
                    if op["dval"] > 16: need(("d", q, s), dsem[q][s], op["dval"] - 16)
                    ins = getattr(E, op["meth"])(*op["args"], **op["kw"])
                    ins.then_inc(dsem[q][s], 16)
                else:
                    ins = getattr(E, op["meth"])(*op["args"], **op["kw"])
                    if op["inc"]: ins.then_inc(esem[op["eng"]], 1)
            if name == "sp":
                for (q, s), v in final_d.items(): need(("d", q, s), dsem[q][s], v)

        with nc.Block() as block:
            @block.tensor
            def _(e): run_stream("pe", e)
            @block.scalar
            def _(e): run_stream("act", e)
            @block.vector
            def _(e): run_stream("dve", e)
            @block.gpsimd
            def _(e): run_stream("pool", e)
            @block.sync
            def _(e): run_stream("sp", e)


def build(nl, nseq):
    nc = bass.Bass("TRN2", target_bir_lowering=False)
    def din(name, shape, dt=F32): return nc.dram_tensor(name, list(shape), dt, kind="ExternalInput").ap()
    def scr(name, shape, dt): return nc.dram_tensor(name, list(shape), dt, kind="Internal").ap()
    x = din("x", [nseq, SEQ, D]); meta = din("meta", [NM, D]); vecs = din("vecs", [nl, 128, NV])
    w_in = din("w_in", [nl, D, DIN]); w_co = din("w_co", [nl, CC, D]); w_go = din("w_go", [nl, 1024, D])
    w_mo = din("w_mo", [nl, 1024, D]); w_uq = din("w_uq", [nl, 512, 1536]); w_ukv = din("w_ukv", [nl, 512, 2048])
    w_o = din("w_o", [nl, D, D]); w_fg = din("w_fg", [nl, D, DFF]); w_fu = din("w_fu", [nl, D, DFF])
    w_fd = din("w_fd", [nl, DFF, D])
    cident = din("cident", [128, 128]); cp128 = din("cp128", [128, 128]); cp64 = din("cp64", [64, 64])
    ccs128 = din("ccs128", [2, 128, L]); ccs64 = din("ccs64", [2, 64, L])
    out = nc.dram_tensor("out", [nseq, SEQ, D], F32, kind="ExternalOutput").ap()
    hT = scr("hT", [D, L], F32); zT = scr("zT", [CC, L + 30], BF16)
    qraw = scr("qraw", [1024, L], F32); kraw = scr("kraw", [256, L], F32); vtm = scr("vtm", [L, 256], BF16)
    cqraw = scr("cqraw", [512, L], F32); ckvraw = scr("ckvraw", [512, L], F32); kperaw = scr("kperaw", [64, L], F32)
    gT = scr("gT", [6144, L], BF16); sT = scr("sT", [1024, L], BF16); ogT = scr("ogT", [1024, L], BF16)
    omT = scr("omT", [1024, L], BF16); aT = scr("aT", [DFF, L], BF16)

    S = Sched()
    stack = contextlib.ExitStack()
    SBYTES = 212000
    SB = stack.enter_context(nc.sbuf_tensor("sb", [128, SBYTES], U8))
    PS = [stack.enter_context(nc.psum_tensor("ps%d" % i, [128, 512], F32)) for i in range(8)]
    top = [0]
    def alloc(shape, dt):
        n = int(np.prod(shape)) * (4 if dt == F32 else 2)
        off = (top[0] + 63) // 64 * 64
        assert off + n <= SBYTES, ("sbuf overflow", off, n)
        top[0] = off + n
        ap = SB[:, off:off + n].bitcast(dt)
        if len(shape) == 2:
            ap = ap.rearrange("p (a b) -> p a b", a=shape[0])
        return ap

    ones_f = alloc([128], F32); ones_b = alloc([128], BF16); ident = alloc([128], F32)
    epsc = alloc([1], F32); vec = alloc([NV], F32)
    XT = alloc([16, L], BF16)
    gtop = top[0]

    def mm(o, l, r, st, sp, reads, writes): S.add("pe", "matmul", (o, l, r), dict(start=st, stop=sp), reads, writes)
    def act(o, i, f, reads, writes, bias=None, scale=None):
        kw = {}
        if bias is not None: kw["bias"] = bias
        if scale is not None: kw["scale"] = scale
        S.add("act", "activation", (), dict(out=o, in_=i, func=f, **kw), reads, writes)
    def tt(o, a, b, op, reads, writes): S.add("dve", "tensor_tensor", (), dict(out=o, in0=a, in1=b, op=op), reads, writes)
    def stt(o, a, sc, b, op0, op1, reads, writes):
        S.add("dve", "scalar_tensor_tensor", (), dict(out=o, in0=a, scalar=sc, in1=b, op0=op0, op1=op1), reads, writes)
    def recip(o, i, reads, writes): S.add("dve", "reciprocal", (), dict(out=o, in_=i), reads, writes)
    def dma(o, i, reads, writes, q="sp"): S.add(q, "dma_start", (), dict(out=o, in_=i), reads, writes, dma=True)
    def phase_end():
        S.barrier(); top[0] = gtop

    S.add("dve", "memset", (ones_f, 1.0), {}, (), ("ones_f",))
    S.add("dve", "memset", (ones_b, 1.0), {}, (), ("ones_b",))
    S.add("dve", "memset", (epsc, EPS), {}, (), ("epsc",))
    dma(ident, cident, (), ("ident",))
    zpad = alloc([8, 16], BF16)
    S.add("dve", "memset", (zpad, 0.0), {}, (), ("zpad",))
    zTv = zT.rearrange("(c p) t -> p c t", p=128)
    dma(zTv[:, :, 0:15], zpad[:, :, 0:15], ("zpad",), ())
    dma(zTv[:, :, L + 15:L + 30], zpad[:, :, 0:15], ("zpad",), ())
    phase_end()

    def wview(w, l):
        return w[l].rearrange("(c p) m -> p c m", p=128)

    def norm_phase(src, KC, gcol, dst, dkey):
        srcv = src.rearrange("(c p) t -> p c t", p=128)
        hin = [alloc([KC, 256], F32) for _ in range(2)]
        sq = alloc([KC, 256], F32)
        sd = alloc([256], F32); rs = alloc([256], F32)
        def load(i):
            t0, n = T256[i]
            dma(hin[i % 2][:, :, :n], srcv[:, :, t0:t0 + n], (), (("hin", i % 2),))
        load(0)
        for i, (t0, n) in enumerate(T256):
            if i + 1 < len(T256): load(i + 1)
            b = i % 2
            act(sq[:, :, :n], hin[b][:, :, :n], AF.Square, (("hin", b),), ("sq",))
            pb = PS[i % 2]
            for c in range(KC):
                mm(pb[:, :n], ones_f, sq[:, c, :n], c == 0, c == KC - 1, ("sq",), (("ps", i % 2),))
            act(sd[:, :n], pb[:, :n], AF.Sqrt, (("ps", i % 2),), ("sd",), bias=epsc, scale=1.0 / (KC * 128))
            recip(rs[:, :n], sd[:, :n], ("sd",), ("rs",))
            for c in range(KC):
                stt(dst[:, c, t0:t0 + n], hin[b][:, c, :n], vec[:, gcol + c:gcol + c + 1], rs[:, :n], ALU.mult, ALU.mult,
                    (("hin", b), "rs"), ((dkey, c, t0 // 512),))

    def rope(dim, xn, outap, t0, n, cs, perm, psb, tmp1, tmp2, rkeys, wkeys):
        rkeys = tuple(rkeys) + ("cs", "perm")
        mm(PS[psb][:dim, :n], perm, xn, True, True, rkeys, (("ps", psb),))
        tt(tmp1[:dim, :n], xn, cs[:dim, 0, t0:t0 + n], ALU.mult, rkeys, ("rt1",))
        tt(tmp2[:dim, :n], PS[psb][:dim, :n], cs[:dim, 1, t0:t0 + n], ALU.mult, (("ps", psb), "cs"), ("rt2",))
        tt(outap, tmp1[:dim, :n], tmp2[:dim, :n], ALU.add, ("rt1", "rt2"), wkeys)

    def headnorm_rope(raw, rkey, gcol, outap_fn, okey, cs, perm, tmps, psa, psb):
        sq, sd, rs, xn, t1, t2 = tmps
        for ti, (t0, n) in enumerate(T512):
            act(sq[:, :n], raw[:, t0:t0 + n], AF.Square, (rkey,), ("hsq",))
            mm(PS[psa][:, :n], ones_f, sq[:, :n], True, True, ("hsq",), (("ps", psa),))
            act(sd[:, :n], PS[psa][:, :n], AF.Sqrt, (("ps", psa),), ("hsd",), bias=epsc, scale=1.0 / 128)
            recip(rs[:, :n], sd[:, :n], ("hsd",), ("hrs",))
            stt(xn[:, :n], raw[:, t0:t0 + n], vec[:, gcol:gcol + 1], rs[:, :n], ALU.mult, ALU.mult, (rkey, "hrs"), ("hxn",))
            rope(128, xn[:, :n], outap_fn(t0, n), t0, n, cs, perm, psb, t1, t2, ("hxn",), (okey,))

    def attention(qparts, kparts, vfn, scale, rkeys, dst, pt, rd, ot):
        for ti, (t0, n) in enumerate(T512):
            bo = 2 + ti % 2; bd = 4 + ti % 2
            def smm(kc):
                k0, nk = KCH[kc]
                for pi, ((qa, dq), (ka, dk)) in enumerate(zip(qparts, kparts)):
                    mm(PS[kc % 2][:nk, :n], ka[:dq, k0:k0 + nk], qa[:dq, t0:t0 + n], pi == 0, pi == len(qparts) - 1,
                       rkeys, (("ps", kc % 2),))
            smm(0)
            for kc, (k0, nk) in enumerate(KCH):
                if kc + 1 < len(KCH): smm(kc + 1)
                pb = kc % 3
                act(pt[pb][:nk, :n], PS[kc % 2][:nk, :n], AF.Exp, (("ps", kc % 2),), (("pt", pb),), scale=scale)
                mm(PS[bo][:, :n], vfn(kc, nk), pt[pb][:nk, :n], kc == 0, kc == len(KCH) - 1, (("pt", pb),) + rkeys, (("ps", bo),))
                mm(PS[bd][:, :n], ones_b[:nk, :], pt[pb][:nk, :n], kc == 0, kc == len(KCH) - 1, (("pt", pb),), (("ps", bd),))
            recip(rd[:, :n], PS[bd][:, :n], (("ps", bd),), ("ard",))
            tt(ot[ti % 2][:, :n], PS[bo][:, :n], rd[:, :n], ALU.mult, (("ps", bo), "ard"), (("aot", ti % 2),))
            dma(dst[:, t0:t0 + n], ot[ti % 2][:, :n], (("aot", ti % 2),), ())

    for sq_i in range(nseq):
        xin = [alloc([D], F32) for _ in range(2)]
        hst = [alloc([16, 128], F32) for _ in range(2)]
        hTv = hT.rearrange("(c p) t -> p c t", p=128)
        def p0load(k):
            k0, n = KCH[k]; b = k % 2
            if k == 0:
                dma(xin[b][0:NM, :], meta[:, :], (), (("xin", b),))
                dma(xin[b][NM:128, :], x[sq_i, 0:128 - NM, :], (), (("xin", b),))
            else:
                dma(xin[b][:n, :], x[sq_i, k0 - NM:k0 - NM + n, :], (), (("xin", b),))
        p0load(0)
        for k, (k0, n) in enumerate(KCH):
            if k + 1 < len(KCH): p0load(k + 1)
            b = k % 2
            for c4 in range(4):
                pb = c4 % 2
                for j in range(4):
                    c = c4 * 4 + j
                    S.add("pe", "transpose", (PS[pb][:, j * 128:j * 128 + n], xin[b][:n, c * 128:(c + 1) * 128], ident[:n, :n]), {},
                          (("xin", b), "ident"), (("ps", pb),))
                src = PS[pb][:, :].rearrange("p (j t) -> p j t", j=4)[:, :, :n]
                S.add("dve", "tensor_copy", (), dict(out=hst[b][:, c4 * 4:c4 * 4 + 4, :n], in_=src), (("ps", pb),), (("hst", b),))
            dma(hTv[:, :, k0:k0 + n], hst[b][:, :, :n], (("hst", b),), ())
        phase_end()

        for l in range(nl):
            dma(vec, vecs[l], (), ("vec",))
            S.barrier()
            norm_phase(hT, 16, VC["mixg"], XT, "XT")
            phase_end()
            wv = wview(w_in, l)
            groups = []
            for j in range(8):
                groups.append(("glu", [(128 * j, 128), (1024 + 128 * j, 128)], j))
            for j in range(4): groups.append(("raw", [(2048 + 256 * j, 256)], (qraw, 256 * j)))
            groups.append(("raw", [(3072, 256)], (kraw, 0)))
            groups.append(("vtm", [(3328, 256)], None))
            for j in range(2): groups.append(("raw", [(3584 + 256 * j, 256)], (cqraw, 256 * j)))
            for j in range(2): groups.append(("raw", [(4096 + 256 * j, 256)], (ckvraw, 256 * j)))
            groups.append(("raw64", [(4608, 64)], (kperaw, 0)))
            for j in range(24): groups.append(("gate", [(4672 + 256 * j, 256)], j))
            wb = [alloc([16, 256], BF16) for _ in range(2)]
            tmpf = [alloc([512], F32) for _ in range(2)]
            stb = [alloc([512], BF16) for _ in range(4)]
            stf = [alloc([512], F32) for _ in range(4)]
            vst = [alloc([256], BF16) for _ in range(2)]
            def wload(gi):
                kind, cols, info = groups[gi]; b = gi % 2; o = 0
                for (c0, w) in cols:
                    dma(wb[b][:, :, o:o + w], wv[:, :, c0:c0 + w], (), (("wb", b),), q="pool"); o += w
            wload(0)
            cnt = [0]
            for gi, (kind, cols, info) in enumerate(groups):
                if gi + 1 < len(groups): wload(gi + 1)
                b = gi % 2; W = wb[b]
                if kind == "vtm":
                    for kc, (k0, nk) in enumerate(KCH):
                        pb = kc % 2
                        for c in range(16):
                            mm(PS[pb][:nk, :256], XT[:, c, k0:k0 + nk], W[:, c, 0:256], c == 0, c == 15,
                               (("XT", c, k0 // 512), ("XT", c, (k0 + nk - 1) // 512), ("wb", b)), (("ps", pb),))
                        S.add("act", "activation", (), dict(out=vst[pb][:nk, :], in_=PS[pb][:nk, :256], func=AF.Copy),
                              (("ps", pb),), (("vst", pb),))
                        dma(vtm[k0:k0 + nk, :], vst[pb][:nk, :], (("vst", pb),), ())
                    continue
                nch = 1 if kind == "raw64" else 2
                wd = 64 if kind == "raw64" else 128
                for ti, (t0, n) in enumerate(T512):
                    par = ti % 2
                    for mi in range(nch):
                        pb = par * 2 + mi
                        for c in range(16):
                            mm(PS[pb][:wd, :n], W[:, c, mi * 128:mi * 128 + wd], XT[:, c, t0:t0 + n], c == 0, c == 15,
                               (("XT", c, ti), ("wb", b)), (("ps", pb),))
                    if kind == "glu":
                        j = info; sb_ = cnt[0] % 4; cnt[0] += 1
                        act(tmpf[par][:, :n], PS[par * 2 + 1][:, :n], AF.Sigmoid, (("ps", par * 2 + 1),), (("tmpf", par),))
                        tt(stb[sb_][:, :n], PS[par * 2][:, :n], tmpf[par][:, :n], ALU.mult, (("ps", par * 2), ("tmpf", par)), (("stb", sb_),))
                        dma(zT[j * 128:(j + 1) * 128, 15 + t0:15 + t0 + n], stb[sb_][:, :n], (("stb", sb_),), ())
                    elif kind in ("raw", "raw64"):
                        dst, r0 = info
                        for mi in range(nch):
                            sb_ = cnt[0] % 4; cnt[0] += 1; pb = par * 2 + mi
                            if mi == 0:
                                S.add("act", "activation", (), dict(out=stf[sb_][:wd, :n], in_=PS[pb][:wd, :n], func=AF.Copy), (("ps", pb),), (("stf", sb_),))
                            else:
                                S.add("dve", "tensor_copy", (), dict(out=stf[sb_][:wd, :n], in_=PS[pb][:wd, :n]), (("ps", pb),), (("stf", sb_),))
                            dma(dst[r0 + mi * 128:r0 + mi * 128 + wd, t0:t0 + n], stf[sb_][:wd, :n], (("stf", sb_),), ())
                    elif kind == "gate":
                        j = info
                        for mi in range(2):
                            sb_ = cnt[0] % 4; cnt[0] += 1; pb = par * 2 + mi; ch = 2 * j + mi
                            act(stb[sb_][:, :n], PS[pb][:, :n], AF.Sigmoid, (("ps", pb),), (("stb", sb_),),
                                bias=vec[:, VC["gateb"] + ch:VC["gateb"] + ch + 1])
                            dma(gT[ch * 128:(ch + 1) * 128, t0:t0 + n], stb[sb_][:, :n], (("stb", sb_),), ())
            phase_end()

            identb = alloc([128], BF16)
            S.add("dve", "tensor_copy", (), dict(out=identb, in_=ident), (), ("identb",))
            diag = alloc([248, 128], BF16)
            for i in range(248):
                S.add("dve", "tensor_scalar", (), dict(out=diag[:, i, :], in0=identb, scalar1=vec[:, VC["dw"] + i:VC["dw"] + i + 1],
                                                      scalar2=None, op0=ALU.mult), ("identb",), (("diag", i % 8),))
            zt = [alloc([8, 542], BF16) for _ in range(2)]
            acc = alloc([8, 512], F32); sqr = [alloc([512], F32) for _ in range(2)]
            mu = alloc([512], F32); m2 = alloc([512], F32); var = alloc([512], F32); sd = alloc([512], F32); rs = alloc([512], F32)
            xc = [alloc([512], F32) for _ in range(2)]
            st8 = [alloc([8, 512], BF16) for _ in range(1)]
            sTv = sT.rearrange("(c p) t -> p c t", p=128)
            def p2load(ti):
                t0, n = T512[ti]
                dma(zt[ti % 2][:, :, :n + 30], zTv[:, :, t0:t0 + n + 30], (), (("zt", ti % 2),))
            p2load(0)
            for ti, (t0, n) in enumerate(T512):
                if ti + 1 < len(T512): p2load(ti + 1)
                zb = ti % 2
                for c in range(8):
                    pb = c % 4
                    for k in range(31):
                        mm(PS[pb][:, :n], diag[:, k * 8 + c, :], zt[zb][:, c, k:k + n], k == 0, k == 30,
                           (("zt", zb), ("diag", c)), (("ps", pb),))
                    act(acc[:, c, :n], PS[pb][:, :n], AF.Identity, (("ps", pb),), (("acc", c),),
                        bias=vec[:, VC["convb"] + c:VC["convb"] + c + 1])
                    act(sqr[c % 2][:, :n], acc[:, c, :n], AF.Square, (("acc", c),), (("sqr", c % 2),))
                    mm(PS[4][:, :n], ones_f, acc[:, c, :n], c == 0, c == 7, (("acc", c),), (("ps", 4),))
                    mm(PS[5][:, :n], ones_f, sqr[c % 2][:, :n], c == 0, c == 7, (("sqr", c % 2),), (("ps", 5),))
                S.add("dve", "tensor_scalar", (), dict(out=mu[:, :n], in0=PS[4][:, :n], scalar1=1.0 / CC, scalar2=None, op0=ALU.mult), (("ps", 4),), ("mu",))
                tt(m2[:, :n], mu[:, :n], mu[:, :n], ALU.mult, ("mu",), ("m2",))
                stt(var[:, :n], PS[5][:, :n], 1.0 / CC, m2[:, :n], ALU.mult, ALU.subtract, (("ps", 5), "m2"), ("var",))
                act(sd[:, :n], var[:, :n], AF.Sqrt, ("var",), ("sd",), bias=epsc)
                recip(rs[:, :n], sd[:, :n], ("sd",), ("rs",))
                for c in range(8):
                    xb = c % 2
                    tt(xc[xb][:, :n], acc[:, c, :n], mu[:, :n], ALU.subtract, (("acc", c), "mu"), (("xc", xb),))
                    tt(xc[xb][:, :n], xc[xb][:, :n], rs[:, :n], ALU.mult, (("xc", xb), "rs"), (("xc", xb),))
                    act(st8[0][:, c, :n], xc[xb][:, :n], AF.Silu, (("xc", xb),), (("st8", 0),),
                        bias=vec[:, VC["lnb"] + c:VC["lnb"] + c + 1], scale=vec[:, VC["lng"] + c:VC["lng"] + c + 1])
                dma(sTv[:, :, t0:t0 + n], st8[0][:, :, :n], (("st8", 0),), ())
            phase_end()

            cs = alloc([2, L], F32); perm = alloc([128], F32)
            dma(cs[:, 0, :], ccs128[0], (), ("cs",)); dma(cs[:, 1, :], ccs128[1], (), ("cs",))
            dma(perm, cp128, (), ("perm",))
            kr = alloc([2, L], F32); kb = alloc([2, L], BF16); vb = alloc([17, 256], BF16)
            dma(kr, kraw.rearrange("(c p) t -> p c t", p=128), (), ("kr",))
            dma(vb[:, 0:16, :], vtm[0:2048, :].rearrange("(k p) d -> p k d", p=128), (), ("vb",))
            dma(vb[0:16, 16, :], vtm[2048:L, :], (), ("vb",))
            tmps = (alloc([512], F32), alloc([512], F32), alloc([512], F32), alloc([512], F32), alloc([512], F32), alloc([512], F32))
            for kvh in range(2):
                headnorm_rope(kr[:, kvh, :], "kr", VC["gkg"], lambda t0, n, kvh=kvh: kb[:, kvh, t0:t0 + n], "kb", cs, perm, tmps, 6, 7)
            qr = [alloc([L], F32) for _ in range(2)]; qb = [alloc([L], BF16) for _ in range(2)]
            pt = [alloc([512], BF16) for _ in range(3)]; rd = alloc([512], F32); ot = [alloc([512], BF16) for _ in range(2)]
            dma(qr[0], qraw[0:128, :], (), (("qr", 0),))
            for h in range(8):
                if h + 1 < 8: dma(qr[(h + 1) % 2], qraw[(h + 1) * 128:(h + 2) * 128, :], (), (("qr", (h + 1) % 2),))
                hb = h % 2; kvh = h // 4
                headnorm_rope(qr[hb], ("qr", hb), VC["gqg"], lambda t0, n, hb=hb: qb[hb][:, t0:t0 + n], ("qb", hb), cs, perm, tmps, 6, 7)
                attention([(qb[hb], 128)], [(kb[:, kvh, :], 128)], lambda kc, nk, kvh=kvh: vb[:nk, kc, kvh * 128:(kvh + 1) * 128],
                          1.0 / np.sqrt(128.0), (("qb", hb), "kb", "vb"), ogT[h * 128:(h + 1) * 128, :], pt, rd, ot)
            phase_end()

            cqn = alloc([4, L], BF16); ckvn = alloc([4, L], BF16)
            mark = top[0]
            norm_phase(cqraw, 4, VC["mqg"], cqn, "cqn")
            norm_phase(ckvraw, 4, VC["mkvg"], ckvn, "ckvn")
            S.barrier(); top[0] = mark
            cs = alloc([2, L], F32); perm = alloc([64], F32)
            dma(cs[0:64, 0, :], ccs64[0], (), ("cs",)); dma(cs[0:64, 1, :], ccs64[1], (), ("cs",))
            dma(perm[0:64, :], cp64, (), ("perm",))
            wuq = alloc([4, 1536], BF16); wukv = alloc([4, 2048], BF16)
            dma(wuq, wview(w_uq, l), (), ("wuq",), q="pool"); dma(wukv, wview(w_ukv, l), (), ("wukv",), q="pool")
            kpr = alloc([L], F32); kpb = alloc([L], BF16)
            dma(kpr[0:64, :], kperaw, (), ("kpr",))
            t1 = alloc([512], F32); t2 = alloc([512], F32); xq = alloc([512], F32)
            for ti, (t0, n) in enumerate(T512):
                rope(64, kpr[0:64, t0:t0 + n], kpb[0:64, t0:t0 + n], t0, n, cs, perm[0:64, :], 7, t1, t2, ("kpr", "cs", "perm"), ("kpb",))
            qnb = [alloc([L], BF16) for _ in range(2)]; qpb = [alloc([L], BF16) for _ in range(2)]
            knb = [alloc([L], BF16) for _ in range(2)]; vhb = [alloc([17, 128], BF16) for _ in range(2)]
            pt = [alloc([512], BF16) for _ in range(3)]; rd = alloc([512], F32); ot = [alloc([512], BF16) for _ in range(2)]
            for h in range(8):
                hb = h % 2
                for ti, (t0, n) in enumerate(T512):
                    tk = tuple(("cqn", c, ti) for c in range(4)); tk2 = tuple(("ckvn", c, ti) for c in range(4))
                    for c in range(4):
                        mm(PS[6][:, :n], wuq[:, c, h * 192:h * 192 + 128], cqn[:, c, t0:t0 + n], c == 0, c == 3, tk + ("wuq",), (("ps", 6),))
                    S.add("act", "activation", (), dict(out=qnb[hb][:, t0:t0 + n], in_=PS[6][:, :n], func=AF.Copy), (("ps", 6),), (("qnb", hb),))
                    for c in range(4):
                        mm(PS[7][:64, :n], wuq[:, c, h * 192 + 128:h * 192 + 192], cqn[:, c, t0:t0 + n], c == 0, c == 3, tk + ("wuq",), (("ps", 7),))
                    S.add("dve", "tensor_copy", (), dict(out=xq[0:64, :n], in_=PS[7][:64, :n]), (("ps", 7),), ("xq",))
                    rope(64, xq[0:64, :n], qpb[hb][0:64, t0:t0 + n], t0, n, cs, perm[0:64, :], 7, t1, t2, ("xq", "cs", "perm"), (("qpb", hb),))
                    for c in range(4):
                        mm(PS[6][:, :n], wukv[:, c, h * 256:h * 256 + 128], ckvn[:, c, t0:t0 + n], c == 0, c == 3, tk2 + ("wukv",), (("ps", 6),))
                    S.add("act", "activation", (), dict(out=knb[hb][:, t0:t0 + n], in_=PS[6][:, :n], func=AF.Copy), (("ps", 6),), (("knb", hb),))
                for kc, (k0, nk) in enumerate(KCH):
                    tk2 = tuple(("ckvn", c, tt_) for c in range(4) for tt_ in {k0 // 512, (k0 + nk - 1) // 512})
                    for c in range(4):
                        mm(PS[7][:nk, :128], ckvn[:, c, k0:k0 + nk], wukv[:, c, h * 256 + 128:h * 256 + 256], c == 0, c == 3, tk2 + ("wukv",), (("ps", 7),))
                    S.add("dve", "tensor_copy", (), dict(out=vhb[hb][:nk, kc, :], in_=PS[7][:nk, :128]), (("ps", 7),), (("vhb", hb),))
                attention([(qnb[hb], 128), (qpb[hb], 64)], [(knb[hb], 128), (kpb, 64)], lambda kc, nk, hb=hb: vhb[hb][:nk, kc, :],
                          1.0 / np.sqrt(192.0), (("qnb", hb), ("qpb", hb), ("knb", hb), "kpb", ("vhb", hb)), omT[h * 128:(h + 1) * 128, :], pt, rd, ot)
            phase_end()

            br = [alloc([8, L], BF16) for _ in range(3)]
            for i, src in enumerate((sT, ogT, omT)):
                dma(br[i], src.rearrange("(c p) t -> p c t", p=128), (), (("br", i),))
            wm = [alloc([24, 128], BF16) for _ in range(2)]
            gt = [alloc([3, 512], BF16) for _ in range(2)]
            f0 = alloc([512], F32); f1 = alloc([512], F32)
            wsrc = [wview(w_co, l), wview(w_go, l), wview(w_mo, l)]
            gTv = gT.rearrange("(b f) t -> f b t", b=3)
            def p5w(m):
                for i in range(3):
                    dma(wm[m % 2][:, i * 8:(i + 1) * 8, :], wsrc[i][:, :, m * 128:(m + 1) * 128], (), (("wm", m % 2),), q="pool")
            p5w(0)
            it = 0
            its = [(m, ti) for m in range(16) for ti in range(len(T512))]
            def p5g(i):
                m_, ti_ = its[i]; t0_, n_ = T512[ti_]
                dma(gt[i % 2][:, :, :n_], gTv[m_ * 128:(m_ + 1) * 128, :, t0_:t0_ + n_], (), (("gt", i % 2),))
            p5g(0)
            for m in range(16):
                if m + 1 < 16: p5w(m + 1)
                for ti, (t0, n) in enumerate(T512):
                    par = it % 2; it += 1
                    if it < len(its): p5g(it)
                    for i in range(3):
                        pb = par * 3 + i
                        for c in range(8):
                            mm(PS[pb][:, :n], wm[m % 2][:, i * 8 + c, :], br[i][:, c, t0:t0 + n], c == 0, c == 7, (("br", i), ("wm", m % 2)), (("ps", pb),))
                    tt(f0[:, :n], PS[par * 3][:, :n], gt[par][:, 0, :n], ALU.mult, (("ps", par * 3), ("gt", par)), ("f0",))
                    tt(f1[:, :n], PS[par * 3 + 1][:, :n], gt[par][:, 1, :n], ALU.mult, (("ps", par * 3 + 1), ("gt", par)), ("f1",))
                    tt(f0[:, :n], f0[:, :n], f1[:, :n], ALU.add, ("f0", "f1"), ("f0",))
                    tt(f1[:, :n], PS[par * 3 + 2][:, :n], gt[par][:, 2, :n], ALU.mult, (("ps", par * 3 + 2), ("gt", par)), ("f1",))
                    tt(XT[:, m, t0:t0 + n], f0[:, :n], f1[:, :n], ALU.add, ("f0", "f1"), (("XT", m, ti),))
            phase_end()

            def resid_gemm(wl, KCt, xfn, xkeys_fn, ncol_chunks=1):
                pass

            wo = [alloc([16, 128], BF16) for _ in range(2)]
            hr = [alloc([512], F32) for _ in range(2)]; ho = [alloc([512], F32) for _ in range(2)]
            wov = wview(w_o, l)
            dma(wo[0], wov[:, :, 0:128], (), (("wo", 0),), q="pool")
            it = 0
            its = [(m, ti) for m in range(16) for ti in range(len(T512))]
            def p5h(i):
                m_, ti_ = its[i]; t0_, n_ = T512[ti_]
                dma(hr[i % 2][:, :n_], hT[m_ * 128:(m_ + 1) * 128, t0_:t0_ + n_], (), (("hr", i % 2),))
            p5h(0)
            for m in range(16):
                if m + 1 < 16: dma(wo[(m + 1) % 2], wov[:, :, (m + 1) * 128:(m + 2) * 128], (), (("wo", (m + 1) % 2),), q="pool")
                for ti, (t0, n) in enumerate(T512):
                    par = it % 2; it += 1
                    if it < len(its): p5h(it)
                    for c in range(16):
                        mm(PS[par][:, :n], wo[m % 2][:, c, :], XT[:, c, t0:t0 + n], c == 0, c == 15, (("XT", c, ti), ("wo", m % 2)), (("ps", par),))
                    tt(ho[par][:, :n], PS[par][:, :n], hr[par][:, :n], ALU.add, (("ps", par), ("hr", par)), (("ho", par),))
                    dma(hT[m * 128:(m + 1) * 128, t0:t0 + n], ho[par][:, :n], (("ho", par),), ())
            phase_end()

            norm_phase(hT, 16, VC["ffng"], XT, "XT")
            phase_end()
            wb = [alloc([16, 256], BF16) for _ in range(2)]
            tmpf = [alloc([512], F32) for _ in range(2)]; stb = [alloc([512], BF16) for _ in range(2)]
            wgv = wview(w_fg, l); wuv = wview(w_fu, l)
            def p6w(j):
                dma(wb[j % 2][:, :, 0:128], wgv[:, :, j * 128:(j + 1) * 128], (), (("wb", j % 2),), q="pool")
                dma(wb[j % 2][:, :, 128:256], wuv[:, :, j * 128:(j + 1) * 128], (), (("wb", j % 2),), q="pool")
            p6w(0)
            it = 0
            for j in range(44):
                if j + 1 < 44: p6w(j + 1)
                for ti, (t0, n) in enumerate(T512):
                    par = it % 2; it += 1
                    for mi in range(2):
                        pb = par * 2 + mi
                        for c in range(16):
                            mm(PS[pb][:, :n], wb[j % 2][:, c, mi * 128:(mi + 1) * 128], XT[:, c, t0:t0 + n], c == 0, c == 15,
                               (("XT", c, ti), ("wb", j % 2)), (("ps", pb),))
                    act(tmpf[par][:, :n], PS[par * 2][:, :n], AF.Silu, (("ps", par * 2),), (("tmpf", par),))
                    tt(stb[par][:, :n], PS[par * 2 + 1][:, :n], tmpf[par][:, :n], ALU.mult, (("ps", par * 2 + 1), ("tmpf", par)), (("stb", par),))
                    dma(aT[j * 128:(j + 1) * 128, t0:t0 + n], stb[par][:, :n], (("stb", par),), ())
            phase_end()
            wd = [alloc([44, 256], BF16) for _ in range(2)]
            at = [alloc([22, 512], BF16) for _ in range(2)]
            hr = [alloc([2, 512], F32) for _ in range(2)]; ho = [alloc([2, 512], F32) for _ in range(2)]
            aTv = aT.rearrange("(c p) t -> p c t", p=128); wdv = wview(w_fd, l)
            hTv2 = hT.rearrange("(c p) t -> p c t", p=128)
            dma(wd[0], wdv[:, :, 0:256], (), (("wd", 0),), q="pool")
            it = 0; ia = 0
            steps = [(me, ti, kh) for me in range(8) for ti in range(len(T512)) for kh in range(2)]
            def p6l(i):
                me_, ti_, kh_ = steps[i]; t0_, n_ = T512[ti_]
                if kh_ == 0:
                    dma(hr[(i // 2) % 2][:, :, :n_], hTv2[:, me_ * 2:me_ * 2 + 2, t0_:t0_ + n_], (), (("hr", (i // 2) % 2),))
                dma(at[i % 2][:, :, :n_], aTv[:, kh_ * 22:(kh_ + 1) * 22, t0_:t0_ + n_], (), (("at", i % 2),))
            p6l(0)
            for me in range(8):
                if me + 1 < 8: dma(wd[(me + 1) % 2], wdv[:, :, (me + 1) * 256:(me + 2) * 256], (), (("wd", (me + 1) % 2),), q="pool")
                for ti, (t0, n) in enumerate(T512):
                    par = it % 2; it += 1
                    for kh in range(2):
                        ab = ia % 2; ia += 1
                        if ia < len(steps): p6l(ia)
                        for mi in range(2):
                            pb = par * 2 + mi
                            for c in range(22):
                                mm(PS[pb][:, :n], wd[me % 2][:, kh * 22 + c, mi * 128:(mi + 1) * 128], at[ab][:, c, :n],
                                   kh == 0 and c == 0, kh == 1 and c == 21, (("at", ab), ("wd", me % 2)), (("ps", pb),))
                    for mi in range(2):
                        pb = par * 2 + mi
                        tt(ho[par][:, mi, :n], PS[pb][:, :n], hr[par][:, mi, :n], ALU.add, (("ps", pb), ("hr", par)), (("ho", par),))
                    dma(hTv2[:, me * 2:me * 2 + 2, t0:t0 + n], ho[par][:, :, :n], (("ho", par),), ())
            phase_end()

        hin = [alloc([16, 128], F32) for _ in range(2)]; sq = alloc([16, 128], F32)
        sd = alloc([128], F32); rs = alloc([128], F32); un = alloc([16, 128], F32)
        orow = [alloc([D], F32) for _ in range(2)]
        hTv = hT.rearrange("(c p) t -> p c t", p=128)
        dma(hin[0], hTv[:, :, NM:NM + 128], (), (("hin", 0),))
        for kk in range(16):
            if kk + 1 < 16: dma(hin[(kk + 1) % 2], hTv[:, :, NM + (kk + 1) * 128:NM + (kk + 2) * 128], (), (("hin", (kk + 1) % 2),))
            b = kk % 2
            act(sq, hin[b], AF.Square, (("hin", b),), ("sq",))
            for c in range(16):
                mm(PS[7][:, :128], ones_f, sq[:, c, :], c == 0, c == 15, ("sq",), (("ps", 7),))
            act(sd, PS[7][:, :128], AF.Sqrt, (("ps", 7),), ("sd",), bias=epsc, scale=1.0 / D)
            recip(rs, sd, ("sd",), ("rs",))
            for c in range(16):
                stt(un[:, c, :], hin[b][:, c, :], vec[:, VC["fing"] + c:VC["fing"] + c + 1], rs, ALU.mult, ALU.mult, (("hin", b), "rs"), (("un", c),))
            for c4 in range(4):
                pb = c4 % 2
                for j in range(4):
                    c = c4 * 4 + j
                    S.add("pe", "transpose", (PS[pb][:, j * 128:(j + 1) * 128], un[:, c, :], ident), {}, (("un", c),), (("ps", pb),))
                S.add("dve", "tensor_copy", (), dict(out=orow[b][:, c4 * 512:(c4 + 1) * 512], in_=PS[pb][:, :]), (("ps", pb),), (("orow", b),))
            dma(out[sq_i, kk * 128:(kk + 1) * 128, :], orow[b], (("orow", b),), ())
        phase_end()

    S.emit(nc, stack)
    stack.close()
    return nc


def _consts():
    def tables(dim):
        half_blk = dim // 2
        hh = half_blk // 2
        inv = (10000.0 ** (-np.arange(hh, dtype=np.float32) / np.float32(hh))).astype(np.float32)
        rows = np.concatenate([np.zeros(NM, np.float32), np.repeat(np.arange(SEQ // 64, dtype=np.float32), 64)])
        cols = np.concatenate([np.zeros(NM, np.float32), np.tile(np.arange(64, dtype=np.float32), SEQ // 64)])
        cs = np.zeros((2, dim, L), np.float32)
        perm = np.zeros((dim, dim), np.float32)
        for d in range(dim):
            blk = d // half_blk; dd = d % half_blk; j = dd % hh
            pos = rows if blk == 0 else cols
            ang = (pos * inv[j]).astype(np.float32)
            cs[0, d] = np.cos(ang)
            s = np.sin(ang)
            if dd < hh:
                cs[1, d] = -s; src = d + hh
            else:
                cs[1, d] = s; src = d - hh
            perm[src, d] = 1.0
        return cs, perm
    cs128, p128 = tables(128)
    cs64, p64 = tables(64)
    return dict(cident=np.eye(128, dtype=np.float32), cp128=p128, cp64=p64, ccs128=cs128, ccs64=cs64)


def _vecs(p, nl):
    v = np.zeros((nl, 128, NV), np.float32)
    def put(name, arr, l):
        a = np.asarray(arr, np.float32).reshape(-1, 128).T
        v[l, :, VC[name]:VC[name] + a.shape[1]] = a
    for l in range(nl):
        put("mixg", p["mix_norm_g"][l], l); put("ffng", p["ffn_norm_g"][l], l); put("gateb", p["gate_b"][l], l)
        put("convb", p["conv_b"][l], l); put("lng", p["conv_ln_g"][l], l); put("lnb", p["conv_ln_b"][l], l)
        dw = np.asarray(p["conv_dw"][l], np.float32).reshape(31, 8, 128)
        v[l, :, VC["dw"]:VC["dw"] + 248] = dw.reshape(248, 128).T
        put("gqg", p["gqa_q_norm_g"][l], l); put("gkg", p["gqa_k_norm_g"][l], l)
        put("mqg", p["mla_q_norm_g"][l], l); put("mkvg", p["mla_kv_norm_g"][l], l)
        put("fing", p["final_norm_g"], l)
    return v


_NC_CACHE = {}


def run(p, nl, nseq, ncores):
    key = (nl, nseq)
    if key not in _NC_CACHE: _NC_CACHE[key] = build(nl, nseq)
    nc = _NC_CACHE[key]
    consts = _consts()
    f = lambda a: np.ascontiguousarray(np.asarray(a, np.float32))
    shared = dict(meta=f(p["meta_tokens"]), vecs=_vecs(p, nl), w_in=f(p["w_in"][:nl]), w_co=f(p["w_conv_out"][:nl]),
                  w_go=f(p["w_gqa_out"][:nl]), w_mo=f(p["w_mla_out"][:nl]), w_uq=f(p["w_mla_uq"][:nl]),
                  w_ukv=f(p["w_mla_ukv"][:nl]), w_o=f(p["w_out"][:nl]), w_fg=f(p["w_ffn_gate"][:nl]),
                  w_fu=f(p["w_ffn_up"][:nl]), w_fd=f(p["w_ffn_down"][:nl]), **consts)
    xs = f(p["x"])
    in_maps = [dict(shared, x=xs[i * nseq:(i + 1) * nseq]) for i in range(ncores)]
    res = run_bass_kernel_spmd(nc, in_maps, core_ids=list(range(ncores)))
    return np.concatenate([r["out"] for r in res.results], axis=0)


def kernel(**inputs):
    return run(inputs, 4, 2, 8)
```

```python
import contextlib
import numpy as np
import concourse.bass as bass
import concourse.mybir as mybir
from concourse.bass_utils import run_bass_kernel_spmd

F32, BF16, U8 = mybir.dt.float32, mybir.dt.bfloat16, mybir.dt.uint8
AF = mybir.ActivationFunctionType
ALU = mybir.AluOpType

D = 2048; SEQ = 2048; NM = 16; L = SEQ + NM; DIN = 10816; DFF = 5632; CC = 1024
EPS = 1e-6
T512 = [(i * 512, min(512, L - i * 512)) for i in range((L + 511) // 512)]
T256 = [(i * 256, min(256, L - i * 256)) for i in range((L + 255) // 256)]
KCH = [(i * 128, min(128, L - i * 128)) for i in range((L + 127) // 128)]
VC = {}
_o = 0
for _n, _w in [("mixg", 16), ("ffng", 16), ("gateb", 48), ("convb", 8), ("lng", 8), ("lnb", 8), ("dw", 248),
               ("gqg", 1), ("gkg", 1), ("mqg", 4), ("mkvg", 4), ("fing", 16)]:
    VC[_n] = _o; _o += _w
NV = _o
ND = 24


class Sched:
    def __init__(self):
        self.ops = []
        self.lastw = {}
        self.readers = {}
        self.lastop = {}
        self.dmas_since_bar = []
        self.nbar = 0

    def add(self, eng, meth, args=(), kw=None, reads=(), writes=(), dma=False):
        deps = set()
        for k in reads:
            if k in self.lastw: deps.add(self.lastw[k])
        for k in writes:
            if k in self.lastw: deps.add(self.lastw[k])
            deps.update(self.readers.get(k, {}).values())
        idx = len(self.ops)
        self.ops.append(dict(eng=eng, meth=meth, args=args, kw=kw or {}, deps=deps, dma=dma, inc=False, kind="op"))
        ek = ("dma", idx) if dma else eng
        for k in reads:
            self.readers.setdefault(k, {})[ek] = idx
        for k in writes:
            self.lastw[k] = idx; self.readers[k] = {}
        if dma: self.dmas_since_bar.append(idx)
        else: self.lastop[eng] = idx
        return idx

    def barrier(self):
        deps = set(self.lastop.values()) | set(self.dmas_since_bar)
        self.nbar += 1
        self.ops.append(dict(eng="sp", kind="bar", deps=deps, dma=False, inc=False, val=self.nbar))
        for e in ("pe", "act", "dve", "pool"):
            self.ops.append(dict(eng=e, kind="barwait", deps=set(), dma=False, inc=False, val=self.nbar))
        self.lastw = {}; self.readers = {}; self.lastop = {}; self.dmas_since_bar = []

    def emit(self, nc, stack):
        ops = self.ops
        for op in ops:
            for d in op["deps"]:
                dop = ops[d]
                if dop["dma"]: continue
                if dop["eng"] == "pe" and op["eng"] == "pe" and not op["dma"] and op["kind"] == "op": continue
                dop["inc"] = True
        esem = {e: stack.enter_context(nc.semaphore("c_" + e)) for e in ("pe", "act", "dve", "pool")}
        bsem = stack.enter_context(nc.semaphore("bar"))
        dsem = {q: [stack.enter_context(nc.semaphore("d_%s%d" % (q, i))) for i in range(ND)] for q in ("sp", "pool")}
        cnt = {e: 0 for e in esem}
        dcnt = {"sp": 0, "pool": 0}
        for op in ops:
            if op["kind"] != "op": continue
            if op["dma"]:
                q = op["eng"]; j = dcnt[q]; dcnt[q] += 1
                op["dslot"] = (q, j % ND); op["dval"] = 16 * (j // ND + 1)
            elif op["inc"]:
                cnt[op["eng"]] += 1; op["cnt"] = cnt[op["eng"]]
        streams = {e: [] for e in ("pe", "act", "dve", "pool", "sp")}
        for op in ops: streams[op["eng"]].append(op)
        engs = {"pe": nc.tensor, "act": nc.scalar, "dve": nc.vector, "pool": nc.gpsimd, "sp": nc.sync}
        final_d = {}
        for op in ops:
            if op["kind"] == "op" and op["dma"]: final_d[op["dslot"]] = op["dval"]

        def run_stream(name, E):
            known = {}
            def need(key, sem, val):
                if known.get(key, 0) >= val: return
                known[key] = val
                E.wait_ge(sem, val)
            for op in streams[name]:
                if op["kind"] == "barwait":
                    E.wait_ge(bsem, op["val"]); continue
                for d in sorted(op["deps"]):
                    dop = ops[d]
                    if dop["dma"]:
                        q, s = dop["dslot"]; need(("d", q, s), dsem[q][s], dop["dval"])
                    elif dop["inc"]:
                        if dop["eng"] == name and name == "pe" and not op["dma"] and op["kind"] == "op": continue
                        need(("e", dop["eng"]), esem[dop["eng"]], dop["cnt"])
                if op["kind"] == "bar":
                    E.sem_inc(bsem, 1); continue
                if op["dma"]:
                    q, s = op["dslot"]
                    if op["dval"] > 16: need(("d", q, s), dsem[q][s], op["dval"] - 16)
                    ins = getattr(E, op["meth"])(*op["args"], **op["kw"])
                    ins.then_inc(dsem[q][s], 16)
                else:
                    ins = getattr(E, op["meth"])(*op["args"], **op["kw"])
                    if op["inc"]: ins.then_inc(esem[op["eng"]], 1)
            if name == "sp":
                for (q, s), v in final_d.items(): need(("d", q, s), dsem[q][s], v)

        with nc.Block() as block:
            @block.tensor
            def _(e): run_stream("pe", e)
            @block.scalar
            def _(e): run_stream("act", e)
            @block.vector
            def _(e): run_stream("dve", e)
            @block.gpsimd
            def _(e): run_stream("pool", e)
            @block.sync
            def _(e): run_stream("sp", e)


def build(nl, nseq):
    nc = bass.Bass("TRN2", target_bir_lowering=False)
    def din(name, shape, dt=F32): return nc.dram_tensor(name, list(shape), dt, kind="ExternalInput").ap()
    def scr(name, shape, dt): return nc.dram_tensor(name, list(shape), dt, kind="Internal").ap()
    x = din("x", [nseq, SEQ, D]); meta = din("meta", [NM, D]); vecs = din("vecs", [nl, 128, NV])
    w_in = din("w_in", [nl, D, DIN]); w_co = din("w_co", [nl, CC, D]); w_go = din("w_go", [nl, 1024, D])
    w_mo = din("w_mo", [nl, 1024, D]); w_uq = din("w_uq", [nl, 512, 1536]); w_ukv = din("w_ukv", [nl, 512, 2048])
    w_o = din("w_o", [nl, D, D]); w_fg = din("w_fg", [nl, D, DFF]); w_fu = din("w_fu", [nl, D, DFF])
    w_fd = din("w_fd", [nl, DFF, D])
    cident = din("cident", [128, 128]); cp128 = din("cp128", [128, 128]); cp64 = din("cp64", [64, 64])
    ccs128 = din("ccs128", [2, 128, L]); ccs64 = din("ccs64", [2, 64, L])
    out = nc.dram_tensor("out", [nseq, SEQ, D], F32, kind="ExternalOutput").ap()
    hT = scr("hT", [D, L], F32); zT = scr("zT", [CC, L + 30], BF16)
    qraw = scr("qraw", [1024, L], F32); kraw = scr("kraw", [256, L], F32); vtm = scr("vtm", [L, 256], BF16)
    cqraw = scr("cqraw", [512, L], F32); ckvraw = scr("ckvraw", [512, L], F32); kperaw = scr("kperaw", [64, L], F32)
    gT = scr("gT", [6144, L], BF16); sT = scr("sT", [1024, L], BF16); ogT = scr("ogT", [1024, L], BF16)
    omT = scr("omT", [1024, L], BF16); aT = scr("aT", [128, 5 * 44 * 512], BF16)
    aTv = aT.rearrange("p (i c t) -> p i c t", i=5, c=44)

    S = Sched()
    stack = contextlib.ExitStack()
    SBYTES = 212000
    SB = stack.enter_context(nc.sbuf_tensor("sb", [128, SBYTES], U8))
    PS = [stack.enter_context(nc.psum_tensor("ps%d" % i, [128, 512], F32)) for i in range(8)]
    top = [0]
    def alloc(shape, dt):
        n = int(np.prod(shape)) * (4 if dt == F32 else 2)
        off = (top[0] + 63) // 64 * 64
        assert off + n <= SBYTES, ("sbuf overflow", off, n)
        top[0] = off + n
        ap = SB[:, off:off + n].bitcast(dt)
        if len(shape) == 2:
            ap = ap.rearrange("p (a b) -> p a b", a=shape[0])
        return ap

    ones_f = alloc([128], F32); ones_b = alloc([128], BF16); ident = alloc([128], F32)
    epsc = alloc([1], F32); vec = alloc([NV], F32)
    xt_off = (top[0] + 63) // 64 * 64
    XT = alloc([16, L], BF16)
    gtop = top[0]

    def mm(o, l, r, st, sp, reads, writes): S.add("pe", "matmul", (o, l, r), dict(start=st, stop=sp), reads, writes)
    def act(o, i, f, reads, writes, bias=None, scale=None):
        kw = {}
        if bias is not None: kw["bias"] = bias
        if scale is not None: kw["scale"] = scale
        S.add("act", "activation", (), dict(out=o, in_=i, func=f, **kw), reads, writes)
    def tt(o, a, b, op, reads, writes): S.add("dve", "tensor_tensor", (), dict(out=o, in0=a, in1=b, op=op), reads, writes)
    def stt(o, a, sc, b, op0, op1, reads, writes):
        S.add("dve", "scalar_tensor_tensor", (), dict(out=o, in0=a, scalar=sc, in1=b, op0=op0, op1=op1), reads, writes)
    def recip(o, i, reads, writes): S.add("dve", "reciprocal", (), dict(out=o, in_=i), reads, writes)
    def dma(o, i, reads, writes, q="sp"): S.add(q, "dma_start", (), dict(out=o, in_=i), reads, writes, dma=True)
    def phase_end():
        S.barrier(); top[0] = gtop

    S.add("dve", "memset", (ones_f, 1.0), {}, (), ("ones_f",))
    S.add("dve", "memset", (ones_b, 1.0), {}, (), ("ones_b",))
    S.add("dve", "memset", (epsc, EPS), {}, (), ("epsc",))
    dma(ident, cident, (), ("ident",))
    zpad = alloc([8, 16], BF16)
    S.add("dve", "memset", (zpad, 0.0), {}, (), ("zpad",))
    zTv = zT.rearrange("(c p) t -> p c t", p=128)
    dma(zTv[:, :, 0:15], zpad[:, :, 0:15], ("zpad",), ())
    dma(zTv[:, :, L + 15:L + 30], zpad[:, :, 0:15], ("zpad",), ())
    phase_end()

    def wview(w, l):
        return w[l].rearrange("(c p) m -> p c m", p=128)

    def norm_phase(src, KC, gcol, dst, dkey):
        srcv = src.rearrange("(c p) t -> p c t", p=128)
        hin = [alloc([KC, 256], F32) for _ in range(2)]
        sq = alloc([KC, 256], F32)
        sd = alloc([256], F32); rs = alloc([256], F32)
        def load(i):
            t0, n = T256[i]
            dma(hin[i % 2][:, :, :n], srcv[:, :, t0:t0 + n], (), (("hin", i % 2),))
        load(0)
        for i, (t0, n) in enumerate(T256):
            if i + 1 < len(T256): load(i + 1)
            b = i % 2
            act(sq[:, :, :n], hin[b][:, :, :n], AF.Square, (("hin", b),), ("sq",))
            pb = PS[i % 2]
            for c in range(KC):
                mm(pb[:, :n], ones_f, sq[:, c, :n], c == 0, c == KC - 1, ("sq",), (("ps", i % 2),))
            act(sd[:, :n], pb[:, :n], AF.Ln, (("ps", i % 2),), ("sd",), bias=epsc, scale=1.0 / (KC * 128))
            act(rs[:, :n], sd[:, :n], AF.Exp, ("sd",), ("rs",), scale=-0.5)
            for c in range(KC):
                stt(dst[:, c, t0:t0 + n], hin[b][:, c, :n], vec[:, gcol + c:gcol + c + 1], rs[:, :n], ALU.mult, ALU.mult,
                    (("hin", b), "rs"), ((dkey, c, t0 // 512),))

    def rope(dim, xn, outap, t0, n, cs, perm, psb, tmp1, tmp2, rkeys, wkeys):
        rkeys = tuple(rkeys) + ("cs", "perm")
        mm(PS[psb][:dim, :n], perm, xn, True, True, rkeys, (("ps", psb),))
        tt(tmp1[:dim, :n], xn, cs[:dim, 0, t0:t0 + n], ALU.mult, rkeys, ("rt1",))
        tt(tmp2[:dim, :n], PS[psb][:dim, :n], cs[:dim, 1, t0:t0 + n], ALU.mult, (("ps", psb), "cs"), ("rt2",))
        tt(outap, tmp1[:dim, :n], tmp2[:dim, :n], ALU.add, ("rt1", "rt2"), wkeys)

    def rope_gen(dim, xn, outap, t0, n, cs, perm, psb, tmp1, tmp2, rkeys, wkeys):
        rkeys = tuple(rkeys) + ("cs", "perm")
        mm(PS[psb][:dim, :n], perm, xn, True, True, rkeys, (("ps", psb),))
        tt(tmp1[:dim, :n], xn, cs[:dim, 0, t0:t0 + n], ALU.mult, rkeys, ("rt1",))
        yield
        yield
        tt(tmp2[:dim, :n], PS[psb][:dim, :n], cs[:dim, 1, t0:t0 + n], ALU.mult, (("ps", psb), "cs"), ("rt2",))
        tt(outap, tmp1[:dim, :n], tmp2[:dim, :n], ALU.add, ("rt1", "rt2"), wkeys)
        yield

    def headnorm_rope_gen(raw, rkey, gcol, outap_fn, okey, cs, perm, tmps, psa, psb):
        sq, sd, rs, xn, t1, t2 = tmps
        for ti, (t0, n) in enumerate(T512):
            act(sq[:, :n], raw[:, t0:t0 + n], AF.Square, (rkey,), ("hsq",))
            yield
            yield
            mm(PS[psa][:, :n], ones_f, sq[:, :n], True, True, ("hsq",), (("ps", psa),))
            yield
            act(sd[:, :n], PS[psa][:, :n], AF.Ln, (("ps", psa),), ("hsd",), bias=epsc, scale=1.0 / 128)
            act(rs[:, :n], sd[:, :n], AF.Exp, ("hsd",), ("hrs",), scale=-0.5)
            yield
            stt(xn[:, :n], raw[:, t0:t0 + n], vec[:, gcol:gcol + 1], rs[:, :n], ALU.mult, ALU.mult, (rkey, "hrs"), ("hxn",))
            yield
            yield
            yield from rope_gen(128, xn[:, :n], outap_fn(t0, n), t0, n, cs, perm, psb, t1, t2, ("hxn",), (okey,))

    def headnorm_rope(*a):
        for _ in headnorm_rope_gen(*a): pass

    SBK = (0, 1, 7)
    def attention(qparts, kparts, vfn, scale, rkeys, dst, pt, rd, ot, pacc, bg=None):
        for ti, (t0, n) in enumerate(T512):
            bo = 2 + ti % 2; bd = 4 + ti % 2; ab = ti % 2
            def smm(kc):
                k0, nk = KCH[kc]
                for pi, ((qa, dq), (ka, dk)) in enumerate(zip(qparts, kparts)):
                    mm(PS[SBK[kc % 3]][:nk, :n], ka[:dq, k0:k0 + nk], qa[:dq, t0:t0 + n], pi == 0, pi == len(qparts) - 1,
                       rkeys, (("ps", SBK[kc % 3]),))
            smm(0); smm(1)
            for kc, (k0, nk) in enumerate(KCH):
                if kc + 2 < len(KCH): smm(kc + 2)
                pb = kc % 3; sbk = SBK[kc % 3]
                act(pt[pb][:nk, :n], PS[sbk][:nk, :n], AF.Exp, (("ps", sbk),), (("pt", pb),), scale=scale)
                mm(PS[bo][:, :n], vfn(kc, nk), pt[pb][:nk, :n], kc == 0, kc == len(KCH) - 1, (("pt", pb),) + rkeys, (("ps", bo),))
                mm(PS[bd][:, :n], ones_b[:nk, :], pt[pb][:nk, :n], kc == 0, kc == len(KCH) - 1, (("pt", pb),), (("ps", bd),))
                if bg is not None: next(bg, None)
            act(rd[:, :n], PS[bd][:, :n], AF.Ln, (("ps", bd),), ("ard0",))
            act(rd[:, :n], rd[:, :n], AF.Exp, ("ard0",), ("ard",), scale=-1.0)
            tt(ot[ti % 2][:, :n], PS[bo][:, :n], rd[:, :n], ALU.mult, (("ps", bo), "ard"), (("aot", ti % 2),))
            dma(dst[:, t0:t0 + n], ot[ti % 2][:, :n], (("aot", ti % 2),), ())

    for sq_i in range(nseq):
        xin = [alloc([D], F32) for _ in range(2)]
        hst = [alloc([16, 128], F32) for _ in range(2)]
        hTv = hT.rearrange("(c p) t -> p c t", p=128)
        def p0load(k):
            k0, n = KCH[k]; b = k % 2
            if k == 0:
                dma(xin[b][0:NM, :], meta[:, :], (), (("xin", b),))
                dma(xin[b][NM:128, :], x[sq_i, 0:128 - NM, :], (), (("xin", b),))
            else:
                dma(xin[b][:n, :], x[sq_i, k0 - NM:k0 - NM + n, :], (), (("xin", b),))
        p0load(0)
        for k, (k0, n) in enumerate(KCH):
            if k + 1 < len(KCH): p0load(k + 1)
            b = k % 2
            for c4 in range(4):
                pb = c4 % 2
                for j in range(4):
                    c = c4 * 4 + j
                    S.add("pe", "transpose", (PS[pb][:, j * 128:j * 128 + n], xin[b][:n, c * 128:(c + 1) * 128], ident[:n, :n]), {},
                          (("xin", b), "ident"), (("ps", pb),))
                src = PS[pb][:, :].rearrange("p (j t) -> p j t", j=4)[:, :, :n]
                S.add("dve", "tensor_copy", (), dict(out=hst[b][:, c4 * 4:c4 * 4 + 4, :n], in_=src), (("ps", pb),), (("hst", b),))
            dma(hTv[:, :, k0:k0 + n], hst[b][:, :, :n], (("hst", b),), ())
        phase_end()

        for l in range(nl):
            dma(vec, vecs[l], (), ("vec",))
            S.barrier()
            norm_phase(hT, 16, VC["mixg"], XT, "XT")
            phase_end()
            wv = wview(w_in, l)
            groups = []
            for j in range(8):
                groups.append(("glu", [(128 * j, 128), (1024 + 128 * j, 128)], j))
            for j in range(4): groups.append(("raw", [(2048 + 256 * j, 256)], (qraw, 256 * j)))
            groups.append(("raw", [(3072, 256)], (kraw, 0)))
            groups.append(("vtm", [(3328, 256)], None))
            for j in range(2): groups.append(("raw", [(3584 + 256 * j, 256)], (cqraw, 256 * j)))
            for j in range(2): groups.append(("raw", [(4096 + 256 * j, 256)], (ckvraw, 256 * j)))
            groups.append(("raw64", [(4608, 64)], (kperaw, 0)))
            for j in range(24): groups.append(("gate", [(4672 + 256 * j, 256)], j))
            wb = [alloc([16, 256], BF16) for _ in range(2)]
            tmpf = [alloc([512], F32) for _ in range(2)]
            stb = [alloc([512], BF16) for _ in range(4)]
            stf = [alloc([512], F32) for _ in range(4)]
            vst = [alloc([256], BF16) for _ in range(2)]
            def wload(gi):
                kind, cols, info = groups[gi]; b = gi % 2; o = 0
                for (c0, w) in cols:
                    dma(wb[b][:, :, o:o + w], wv[:, :, c0:c0 + w], (), (("wb", b),), q="pool"); o += w
            wload(0)
            cnt = [0]
            for gi, (kind, cols, info) in enumerate(groups):
                if gi + 1 < len(groups): wload(gi + 1)
                b = gi % 2; W = wb[b]
                if kind == "vtm":
                    for kc, (k0, nk) in enumerate(KCH):
                        pb = kc % 2
                        for c in range(16):
                            mm(PS[pb][:nk, :256], XT[:, c, k0:k0 + nk], W[:, c, 0:256], c == 0, c == 15,
                               (("XT", c, k0 // 512), ("XT", c, (k0 + nk - 1) // 512), ("wb", b)), (("ps", pb),))
                        S.add("act", "activation", (), dict(out=vst[pb][:nk, :], in_=PS[pb][:nk, :256], func=AF.Copy),
                              (("ps", pb),), (("vst", pb),))
                        dma(vtm[k0:k0 + nk, :], vst[pb][:nk, :], (("vst", pb),), ())
                    continue
                nch = 1 if kind == "raw64" else 2
                wd = 64 if kind == "raw64" else 128
                for ti, (t0, n) in enumerate(T512):
                    par = ti % 2
                    for mi in range(nch):
                        pb = par * 2 + mi
                        for c in range(16):
                            mm(PS[pb][:wd, :n], W[:, c, mi * 128:mi * 128 + wd], XT[:, c, t0:t0 + n], c == 0, c == 15,
                               (("XT", c, ti), ("wb", b)), (("ps", pb),))
                    if kind == "glu":
                        j = info; sb_ = cnt[0] % 4; cnt[0] += 1
                        act(tmpf[par][:, :n], PS[par * 2 + 1][:, :n], AF.Sigmoid, (("ps", par * 2 + 1),), (("tmpf", par),))
                        tt(stb[sb_][:, :n], PS[par * 2][:, :n], tmpf[par][:, :n], ALU.mult, (("ps", par * 2), ("tmpf", par)), (("stb", sb_),))
                        dma(zT[j * 128:(j + 1) * 128, 15 + t0:15 + t0 + n], stb[sb_][:, :n], (("stb", sb_),), ())
                    elif kind in ("raw", "raw64"):
                        dst, r0 = info
                        for mi in range(nch):
                            sb_ = cnt[0] % 4; cnt[0] += 1; pb = par * 2 + mi
                            if mi == 0:
                                S.add("act", "activation", (), dict(out=stf[sb_][:wd, :n], in_=PS[pb][:wd, :n], func=AF.Copy), (("ps", pb),), (("stf", sb_),))
                            else:
                                S.add("dve", "tensor_copy", (), dict(out=stf[sb_][:wd, :n], in_=PS[pb][:wd, :n]), (("ps", pb),), (("stf", sb_),))
                            dma(dst[r0 + mi * 128:r0 + mi * 128 + wd, t0:t0 + n], stf[sb_][:wd, :n], (("stf", sb_),), ())
                    elif kind == "gate":
                        j = info
                        for mi in range(2):
                            sb_ = cnt[0] % 4; cnt[0] += 1; pb = par * 2 + mi; ch = 2 * j + mi
                            act(stb[sb_][:, :n], PS[pb][:, :n], AF.Sigmoid, (("ps", pb),), (("stb", sb_),),
                                bias=vec[:, VC["gateb"] + ch:VC["gateb"] + ch + 1])
                            dma(gT[ch * 128:(ch + 1) * 128, t0:t0 + n], stb[sb_][:, :n], (("stb", sb_),), ())
            phase_end()

            identb = alloc([128], BF16)
            S.add("dve", "tensor_copy", (), dict(out=identb, in_=ident), (), ("identb",))
            diag = alloc([248, 128], BF16)
            for i in range(248):
                S.add("dve", "tensor_scalar", (), dict(out=diag[:, i, :], in0=identb, scalar1=vec[:, VC["dw"] + i:VC["dw"] + i + 1],
                                                      scalar2=None, op0=ALU.mult), ("identb",), (("diag", i % 8),))
            zt = [alloc([8, 542], BF16) for _ in range(2)]
            acc = alloc([8, 512], F32); sqr = [alloc([512], F32) for _ in range(2)]
            mu = alloc([512], F32); m2 = alloc([512], F32); var = alloc([512], F32); sd = alloc([512], F32); rs = alloc([512], F32)
            xc = [alloc([512], F32) for _ in range(2)]
            st8 = [alloc([8, 512], BF16) for _ in range(1)]
            sTv = sT.rearrange("(c p) t -> p c t", p=128)
            def p2load(ti):
                t0, n = T512[ti]
                dma(zt[ti % 2][:, :, :n + 30], zTv[:, :, t0:t0 + n + 30], (), (("zt", ti % 2),))
            p2load(0)
            for ti, (t0, n) in enumerate(T512):
                if ti + 1 < len(T512): p2load(ti + 1)
                zb = ti % 2
                for c in range(8):
                    pb = c % 4
                    for k in range(31):
                        mm(PS[pb][:, :n], diag[:, k * 8 + c, :], zt[zb][:, c, k:k + n], k == 0, k == 30,
                           (("zt", zb), ("diag", c)), (("ps", pb),))
                    act(acc[:, c, :n], PS[pb][:, :n], AF.Identity, (("ps", pb),), (("acc", c),),
                        bias=vec[:, VC["convb"] + c:VC["convb"] + c + 1])
                    act(sqr[c % 2][:, :n], acc[:, c, :n], AF.Square, (("acc", c),), (("sqr", c % 2),))
                    mm(PS[4][:, :n], ones_f, acc[:, c, :n], c == 0, c == 7, (("acc", c),), (("ps", 4),))
                    mm(PS[5][:, :n], ones_f, sqr[c % 2][:, :n], c == 0, c == 7, (("sqr", c % 2),), (("ps", 5),))
                S.add("dve", "tensor_scalar", (), dict(out=mu[:, :n], in0=PS[4][:, :n], scalar1=1.0 / CC, scalar2=None, op0=ALU.mult), (("ps", 4),), ("mu",))
                tt(m2[:, :n], mu[:, :n], mu[:, :n], ALU.mult, ("mu",), ("m2",))
                stt(var[:, :n], PS[5][:, :n], 1.0 / CC, m2[:, :n], ALU.mult, ALU.subtract, (("ps", 5), "m2"), ("var",))
                act(sd[:, :n], var[:, :n], AF.Ln, ("var",), ("sd",), bias=epsc)
                act(rs[:, :n], sd[:, :n], AF.Exp, ("sd",), ("rs",), scale=-0.5)
                for c in range(8):
                    xb = c % 2
                    tt(xc[xb][:, :n], acc[:, c, :n], mu[:, :n], ALU.subtract, (("acc", c), "mu"), (("xc", xb),))
                    tt(xc[xb][:, :n], xc[xb][:, :n], rs[:, :n], ALU.mult, (("xc", xb), "rs"), (("xc", xb),))
                    act(st8[0][:, c, :n], xc[xb][:, :n], AF.Silu, (("xc", xb),), (("st8", 0),),
                        bias=vec[:, VC["lnb"] + c:VC["lnb"] + c + 1], scale=vec[:, VC["lng"] + c:VC["lng"] + c + 1])
                dma(sTv[:, :, t0:t0 + n], st8[0][:, :, :n], (("st8", 0),), ())
            phase_end()

            cs = alloc([2, L], F32); perm = alloc([128], F32)
            dma(cs[:, 0, :], ccs128[0], (), ("cs",)); dma(cs[:, 1, :], ccs128[1], (), ("cs",))
            dma(perm, cp128, (), ("perm",))
            kr = alloc([2, L], F32); kb = alloc([2, L], BF16); vb = alloc([17, 256], BF16)
            dma(kr, kraw.rearrange("(c p) t -> p c t", p=128), (), ("kr",))
            dma(vb[:, 0:16, :], vtm[0:2048, :].rearrange("(k p) d -> p k d", p=128), (), ("vb",))
            dma(vb[0:16, 16, :], vtm[2048:L, :], (), ("vb",))
            tmps = (alloc([512], F32), alloc([512], F32), alloc([512], F32), alloc([512], F32), alloc([512], F32), alloc([512], F32))
            for kvh in range(2):
                headnorm_rope(kr[:, kvh, :], "kr", VC["gkg"], lambda t0, n, kvh=kvh: kb[:, kvh, t0:t0 + n], "kb", cs, perm, tmps, 6, 6)
            qr = [alloc([L], F32) for _ in range(2)]; qb = [alloc([L], BF16) for _ in range(2)]
            pt = [alloc([512], BF16) for _ in range(3)]; rd = alloc([512], F32); ot = [alloc([512], BF16) for _ in range(2)]
            pacc = None
            dma(qr[0], qraw[0:128, :], (), (("qr", 0),))
            dma(qr[1], qraw[128:256, :], (), (("qr", 1),))
            def qgen(h):
                hb = h % 2
                return headnorm_rope_gen(qr[hb], ("qr", hb), VC["gqg"], lambda t0, n, hb=hb: qb[hb][:, t0:t0 + n], ("qb", hb), cs, perm, tmps, 6, 6)
            for _ in qgen(0): pass
            for h in range(8):
                hb = h % 2; kvh = h // 4
                bg = qgen(h + 1) if h + 1 < 8 else None
                attention([(qb[hb], 128)], [(kb[:, kvh, :], 128)], lambda kc, nk, kvh=kvh: vb[:nk, kc, kvh * 128:(kvh + 1) * 128],
                          1.0 / np.sqrt(128.0), (("qb", hb), "kb", "vb"), ogT[h * 128:(h + 1) * 128, :], pt, rd, ot, pacc, bg)
                if bg is not None:
                    for _ in bg: pass
                if h + 2 < 8: dma(qr[hb], qraw[(h + 2) * 128:(h + 3) * 128, :], (), (("qr", hb),))
            phase_end()

            cqn = alloc([4, L], BF16); ckvn = alloc([4, L], BF16)
            mark = top[0]
            norm_phase(cqraw, 4, VC["mqg"], cqn, "cqn")
            norm_phase(ckvraw, 4, VC["mkvg"], ckvn, "ckvn")
            S.barrier(); top[0] = mark
            cs = alloc([2, L], F32); perm = alloc([64], F32)
            dma(cs[0:64, 0, :], ccs64[0], (), ("cs",)); dma(cs[0:64, 1, :], ccs64[1], (), ("cs",))
            dma(perm[0:64, :], cp64, (), ("perm",))
            wuq = alloc([4, 1536], BF16); wukv = alloc([4, 2048], BF16)
            dma(wuq, wview(w_uq, l), (), ("wuq",), q="pool"); dma(wukv, wview(w_ukv, l), (), ("wukv",), q="pool")
            kpb = alloc([L], BF16)
            t1 = alloc([512], F32); t2 = alloc([512], F32); xq = alloc([512], F32)
            mark2 = top[0]
            kpr = alloc([L], F32)
            dma(kpr[0:64, :], kperaw, (), ("kpr",))
            for ti, (t0, n) in enumerate(T512):
                rope(64, kpr[0:64, t0:t0 + n], kpb[0:64, t0:t0 + n], t0, n, cs, perm[0:64, :], 6, t1, t2, ("kpr", "cs", "perm"), ("kpb",))
            S.barrier(); top[0] = mark2
            qnb = [alloc([L], BF16) for _ in range(2)]; qpb = [alloc([L], BF16) for _ in range(2)]
            knb = [alloc([L], BF16) for _ in range(2)]; vhb = [alloc([17, 128], BF16) for _ in range(2)]
            pt = [alloc([512], BF16) for _ in range(3)]; rd = alloc([512], F32); ot = [alloc([512], BF16) for _ in range(2)]
            pacc = None
            def mgen(h):
                hb = h % 2
                for ti, (t0, n) in enumerate(T512):
                    tk = tuple(("cqn", c, ti) for c in range(4)); tk2 = tuple(("ckvn", c, ti) for c in range(4))
                    for c in range(4):
                        mm(PS[6][:, :n], wuq[:, c, h * 192:h * 192 + 128], cqn[:, c, t0:t0 + n], c == 0, c == 3, tk + ("wuq",), (("ps", 6),))
                    yield
                    S.add("act", "activation", (), dict(out=qnb[hb][:, t0:t0 + n], in_=PS[6][:, :n], func=AF.Copy), (("ps", 6),), (("qnb", hb),))
                    for c in range(4):
                        mm(PS[6][:64, :n], wuq[:, c, h * 192 + 128:h * 192 + 192], cqn[:, c, t0:t0 + n], c == 0, c == 3, tk + ("wuq",), (("ps", 6),))
                    yield
                    S.add("dve", "tensor_copy", (), dict(out=xq[0:64, :n], in_=PS[6][:64, :n]), (("ps", 6),), ("xq",))
                    yield
                    yield from rope_gen(64, xq[0:64, :n], qpb[hb][0:64, t0:t0 + n], t0, n, cs, perm[0:64, :], 6, t1, t2, ("xq",), (("qpb", hb),))
                    for c in range(4):
                        mm(PS[6][:, :n], wukv[:, c, h * 256:h * 256 + 128], ckvn[:, c, t0:t0 + n], c == 0, c == 3, tk2 + ("wukv",), (("ps", 6),))
                    yield
                    S.add("act", "activation", (), dict(out=knb[hb][:, t0:t0 + n], in_=PS[6][:, :n], func=AF.Copy), (("ps", 6),), (("knb", hb),))
                    yield
                for kc, (k0, nk) in enumerate(KCH):
                    tk2 = tuple(("ckvn", c, tt_) for c in range(4) for tt_ in {k0 // 512, (k0 + nk - 1) // 512})
                    for c in range(4):
                        mm(PS[6][:nk, :128], ckvn[:, c, k0:k0 + nk], wukv[:, c, h * 256 + 128:h * 256 + 256], c == 0, c == 3, tk2 + ("wukv",), (("ps", 6),))
                    yield
                    S.add("dve", "tensor_copy", (), dict(out=vhb[hb][:nk, kc, :], in_=PS[6][:nk, :128]), (("ps", 6),), (("vhb", hb),))
                    yield
            for _ in mgen(0): pass
            for h in range(8):
                hb = h % 2
                bg = mgen(h + 1) if h + 1 < 8 else None
                attention([(qnb[hb], 128), (qpb[hb], 64)], [(knb[hb], 128), (kpb, 64)], lambda kc, nk, hb=hb: vhb[hb][:nk, kc, :],
                          1.0 / np.sqrt(192.0), (("qnb", hb), ("qpb", hb), ("knb", hb), "kpb", ("vhb", hb)), omT[h * 128:(h + 1) * 128, :], pt, rd, ot, pacc, bg)
                if bg is not None:
                    for _ in bg: pass
            phase_end()

            br = [alloc([8, L], BF16) for _ in range(3)]
            for i, src in enumerate((sT, ogT, omT)):
                dma(br[i], src.rearrange("(c p) t -> p c t", p=128), (), (("br", i),))
            wm = [alloc([24, 128], BF16) for _ in range(2)]
            gt = [alloc([3, 512], BF16) for _ in range(2)]
            f0 = alloc([512], F32); f1 = alloc([512], F32)
            wsrc = [wview(w_co, l), wview(w_go, l), wview(w_mo, l)]
            gTv = gT.rearrange("(b f) t -> f b t", b=3)
            def p5w(m):
                for i in range(3):
                    dma(wm[m % 2][:, i * 8:(i + 1) * 8, :], wsrc[i][:, :, m * 128:(m + 1) * 128], (), (("wm", m % 2),), q="pool")
            p5w(0)
            it = 0
            its = [(m, ti) for m in range(16) for ti in range(len(T512))]
            def p5g(i):
                m_, ti_ = its[i]; t0_, n_ = T512[ti_]
                dma(gt[i % 2][:, :, :n_], gTv[m_ * 128:(m_ + 1) * 128, :, t0_:t0_ + n_], (), (("gt", i % 2),))
            p5g(0)
            for m in range(16):
                if m + 1 < 16: p5w(m + 1)
                for ti, (t0, n) in enumerate(T512):
                    par = it % 2; it += 1
                    if it < len(its): p5g(it)
                    for i in range(3):
                        pb = par * 3 + i
                        for c in range(8):
                            mm(PS[pb][:, :n], wm[m % 2][:, i * 8 + c, :], br[i][:, c, t0:t0 + n], c == 0, c == 7, (("br", i), ("wm", m % 2)), (("ps", pb),))
                    tt(f0[:, :n], PS[par * 3][:, :n], gt[par][:, 0, :n], ALU.mult, (("ps", par * 3), ("gt", par)), ("f0",))
                    tt(f1[:, :n], PS[par * 3 + 1][:, :n], gt[par][:, 1, :n], ALU.mult, (("ps", par * 3 + 1), ("gt", par)), ("f1",))
                    tt(f0[:, :n], f0[:, :n], f1[:, :n], ALU.add, ("f0", "f1"), ("f0",))
                    tt(f1[:, :n], PS[par * 3 + 2][:, :n], gt[par][:, 2, :n], ALU.mult, (("ps", par * 3 + 2), ("gt", par)), ("f1",))
                    tt(XT[:, m, t0:t0 + n], f0[:, :n], f1[:, :n], ALU.add, ("f0", "f1"), (("XT", m, ti),))
            phase_end()

            def resid_gemm(wl, KCt, xfn, xkeys_fn, ncol_chunks=1):
                pass

            wo = [alloc([16, 128], BF16) for _ in range(2)]
            hr = [alloc([512], F32) for _ in range(2)]; ho = [alloc([512], F32) for _ in range(2)]
            wov = wview(w_o, l)
            dma(wo[0], wov[:, :, 0:128], (), (("wo", 0),), q="pool")
            it = 0
            its = [(m, ti) for m in range(16) for ti in range(len(T512))]
            def p5h(i):
                m_, ti_ = its[i]; t0_, n_ = T512[ti_]
                dma(hr[i % 2][:, :n_], hT[m_ * 128:(m_ + 1) * 128, t0_:t0_ + n_], (), (("hr", i % 2),))
            p5h(0)
            for m in range(16):
                if m + 1 < 16: dma(wo[(m + 1) % 2], wov[:, :, (m + 1) * 128:(m + 2) * 128], (), (("wo", (m + 1) % 2),), q="pool")
                for ti, (t0, n) in enumerate(T512):
                    par = it % 2; it += 1
                    if it < len(its): p5h(it)
                    for c in range(16):
                        mm(PS[par][:, :n], wo[m % 2][:, c, :], XT[:, c, t0:t0 + n], c == 0, c == 15, (("XT", c, ti), ("wo", m % 2)), (("ps", par),))
                    tt(ho[par][:, :n], PS[par][:, :n], hr[par][:, :n], ALU.add, (("ps", par), ("hr", par)), (("ho", par),))
                    dma(hT[m * 128:(m + 1) * 128, t0:t0 + n], ho[par][:, :n], (("ho", par),), ())
            phase_end()

            norm_phase(hT, 16, VC["ffng"], XT, "XT")
            phase_end()
            wb = [alloc([16, 256], BF16) for _ in range(2)]
            tmpf = [alloc([512], F32) for _ in range(2)]; stb = [alloc([512], BF16) for _ in range(2)]
            wgv = wview(w_fg, l); wuv = wview(w_fu, l)
            def p6w(j):
                dma(wb[j % 2][:, :, 0:128], wgv[:, :, j * 128:(j + 1) * 128], (), (("wb", j % 2),), q="pool")
                dma(wb[j % 2][:, :, 128:256], wuv[:, :, j * 128:(j + 1) * 128], (), (("wb", j % 2),), q="pool")
            p6w(0)
            it = 0
            for j in range(44):
                if j + 1 < 44: p6w(j + 1)
                for ti, (t0, n) in enumerate(T512):
                    par = it % 2; it += 1
                    for mi in range(2):
                        pb = par * 2 + mi
                        for c in range(16):
                            mm(PS[pb][:, :n], wb[j % 2][:, c, mi * 128:(mi + 1) * 128], XT[:, c, t0:t0 + n], c == 0, c == 15,
                               (("XT", c, ti), ("wb", j % 2)), (("ps", pb),))
                    act(tmpf[par][:, :n], PS[par * 2][:, :n], AF.Silu, (("ps", par * 2),), (("tmpf", par),))
                    tt(stb[par][:, :n], PS[par * 2 + 1][:, :n], tmpf[par][:, :n], ALU.mult, (("ps", par * 2 + 1), ("tmpf", par)), (("stb", par),))
                    dma(aTv[:, ti, j, :n], stb[par][:, :n], (("stb", par),), ())
            phase_end()
            wd = [SB[:, xt_off:xt_off + 44 * 512 * 2].bitcast(BF16).rearrange("p (a b) -> p a b", a=44), alloc([44, 512], BF16)]
            at = [alloc([22, 512], BF16) for _ in range(2)]
            hr = [alloc([4, 512], F32) for _ in range(2)]; ho = [alloc([4, 512], F32) for _ in range(2)]
            wdv = wview(w_fd, l)
            hTv2 = hT.rearrange("(c p) t -> p c t", p=128)
            dma(wd[0], wdv[:, :, 0:512], (), (("wd", 0),), q="pool")
            it = 0; ia = 0
            steps = [(me, ti, kh) for me in range(4) for ti in range(len(T512)) for kh in range(2)]
            def p6l(i):
                me_, ti_, kh_ = steps[i]; t0_, n_ = T512[ti_]
                if kh_ == 0:
                    dma(hr[(i // 2) % 2][:, :, :n_], hTv2[:, me_ * 4:me_ * 4 + 4, t0_:t0_ + n_], (), (("hr", (i // 2) % 2),))
                dma(at[i % 2][:, :, :n_], aTv[:, ti_, kh_ * 22:(kh_ + 1) * 22, :n_], (), (("at", i % 2),))
            p6l(0)
            for me in range(4):
                if me + 1 < 4: dma(wd[(me + 1) % 2], wdv[:, :, (me + 1) * 512:(me + 2) * 512], (), (("wd", (me + 1) % 2),), q="pool")
                for ti, (t0, n) in enumerate(T512):
                    par = it % 2; it += 1
                    for kh in range(2):
                        ab = ia % 2; ia += 1
                        if ia < len(steps): p6l(ia)
                        for mi in range(4):
                            pb = par * 4 + mi
                            for c in range(22):
                                mm(PS[pb][:, :n], wd[me % 2][:, kh * 22 + c, mi * 128:(mi + 1) * 128], at[ab][:, c, :n],
                                   kh == 0 and c == 0, kh == 1 and c == 21, (("at", ab), ("wd", me % 2)), (("ps", pb),))
                    for mi in range(4):
                        pb = par * 4 + mi
                        tt(ho[par][:, mi, :n], PS[pb][:, :n], hr[par][:, mi, :n], ALU.add, (("ps", pb), ("hr", par)), (("ho", par),))
                    dma(hTv2[:, me * 4:me * 4 + 4, t0:t0 + n], ho[par][:, :, :n], (("ho", par),), ())
            phase_end()

        hin = [alloc([16, 128], F32) for _ in range(2)]; sq = alloc([16, 128], F32)
        sd = alloc([128], F32); rs = alloc([128], F32); un = alloc([16, 128], F32)
        orow = [alloc([D], F32) for _ in range(2)]
        hTv = hT.rearrange("(c p) t -> p c t", p=128)
        dma(hin[0], hTv[:, :, NM:NM + 128], (), (("hin", 0),))
        for kk in range(16):
            if kk + 1 < 16: dma(hin[(kk + 1) % 2], hTv[:, :, NM + (kk + 1) * 128:NM + (kk + 2) * 128], (), (("hin", (kk + 1) % 2),))
            b = kk % 2
            act(sq, hin[b], AF.Square, (("hin", b),), ("sq",))
            for c in range(16):
                mm(PS[7][:, :128], ones_f, sq[:, c, :], c == 0, c == 15, ("sq",), (("ps", 7),))
            act(sd, PS[7][:, :128], AF.Ln, (("ps", 7),), ("sd",), bias=epsc, scale=1.0 / D)
            act(rs, sd, AF.Exp, ("sd",), ("rs",), scale=-0.5)
            for c in range(16):
                stt(un[:, c, :], hin[b][:, c, :], vec[:, VC["fing"] + c:VC["fing"] + c + 1], rs, ALU.mult, ALU.mult, (("hin", b), "rs"), (("un", c),))
            for c4 in range(4):
                pb = c4 % 2
                for j in range(4):
                    c = c4 * 4 + j
                    S.add("pe", "transpose", (PS[pb][:, j * 128:(j + 1) * 128], un[:, c, :], ident), {}, (("un", c),), (("ps", pb),))
                S.add("dve", "tensor_copy", (), dict(out=orow[b][:, c4 * 512:(c4 + 1) * 512], in_=PS[pb][:, :]), (("ps", pb),), (("orow", b),))
            dma(out[sq_i, kk * 128:(kk + 1) * 128, :], orow[b], (("orow", b),), ())
        phase_end()

    S.emit(nc, stack)
    stack.close()
    return nc


def _consts():
    def tables(dim):
        half_blk = dim // 2
        hh = half_blk // 2
        inv = (10000.0 ** (-np.arange(hh, dtype=np.float32) / np.float32(hh))).astype(np.float32)
        rows = np.concatenate([np.zeros(NM, np.float32), np.repeat(np.arange(SEQ // 64, dtype=np.float32), 64)])
        cols = np.concatenate([np.zeros(NM, np.float32), np.tile(np.arange(64, dtype=np.float32), SEQ // 64)])
        cs = np.zeros((2, dim, L), np.float32)
        perm = np.zeros((dim, dim), np.float32)
        for d in range(dim):
            blk = d // half_blk; dd = d % half_blk; j = dd % hh
            pos = rows if blk == 0 else cols
            ang = (pos * inv[j]).astype(np.float32)
            cs[0, d] = np.cos(ang)
            s = np.sin(ang)
            if dd < hh:
                cs[1, d] = -s; src = d + hh
            else:
                cs[1, d] = s; src = d - hh
            perm[src, d] = 1.0
        return cs, perm
    cs128, p128 = tables(128)
    cs64, p64 = tables(64)
    return dict(cident=np.eye(128, dtype=np.float32), cp128=p128, cp64=p64, ccs128=cs128, ccs64=cs64)


def _vecs(p, nl):
    v = np.zeros((nl, 128, NV), np.float32)
    def put(name, arr, l):
        a = np.asarray(arr, np.float32).reshape(-1, 128).T
        v[l, :, VC[name]:VC[name] + a.shape[1]] = a
    for l in range(nl):
        put("mixg", p["mix_norm_g"][l], l); put("ffng", p["ffn_norm_g"][l], l); put("gateb", p["gate_b"][l], l)
        put("convb", p["conv_b"][l], l); put("lng", p["conv_ln_g"][l], l); put("lnb", p["conv_ln_b"][l], l)
        dw = np.asarray(p["conv_dw"][l], np.float32).reshape(31, 8, 128)
        v[l, :, VC["dw"]:VC["dw"] + 248] = dw.reshape(248, 128).T
        put("gqg", p["gqa_q_norm_g"][l], l); put("gkg", p["gqa_k_norm_g"][l], l)
        put("mqg", p["mla_q_norm_g"][l], l); put("mkvg", p["mla_kv_norm_g"][l], l)
        put("fing", p["final_norm_g"], l)
    return v


_NC_CACHE = {}


def run(p, nl, nseq, ncores):
    key = (nl, nseq)
    if key not in _NC_CACHE: _NC_CACHE[key] = build(nl, nseq)
    nc = _NC_CACHE[key]
    consts = _consts()
    f = lambda a: np.ascontiguousarray(np.asarray(a, np.float32))
    shared = dict(meta=f(p["meta_tokens"]), vecs=_vecs(p, nl), w_in=f(p["w_in"][:nl]), w_co=f(p["w_conv_out"][:nl]),
                  w_go=f(p["w_gqa_out"][:nl]), w_mo=f(p["w_mla_out"][:nl]), w_uq=f(p["w_mla_uq"][:nl]),
                  w_ukv=f(p["w_mla_ukv"][:nl]), w_o=f(p["w_out"][:nl]), w_fg=f(p["w_ffn_gate"][:nl]),
                  w_fu=f(p["w_ffn_up"][:nl]), w_fd=f(p["w_ffn_down"][:nl]), **consts)
    xs = f(p["x"])
    in_maps = [dict(shared, x=xs[i * nseq:(i + 1) * nseq]) for i in range(ncores)]
    res = run_bass_kernel_spmd(nc, in_maps, core_ids=list(range(ncores)))
    return np.concatenate([r["out"] for r in res.results], axis=0)


def kernel(**inputs):
    return run(inputs, 4, 2, 8)
```

```python
import contextlib
import numpy as np
import concourse.bass as bass
import concourse.mybir as mybir
from concourse.bass_utils import run_bass_kernel_spmd

F32, BF16, U8 = mybir.dt.float32, mybir.dt.bfloat16, mybir.dt.uint8
AF = mybir.ActivationFunctionType
ALU = mybir.AluOpType

D = 2048; SEQ = 2048; NM = 16; L = SEQ + NM; DIN = 10816; DFF = 5632; CC = 1024
EPS = 1e-6
T512 = [(i * 512, min(512, L - i * 512)) for i in range((L + 511) // 512)]
T256 = [(i * 256, min(256, L - i * 256)) for i in range((L + 255) // 256)]
KCH = [(i * 128, min(128, L - i * 128)) for i in range((L + 127) // 128)]
VC = {}
_o = 0
for _n, _w in [("mixg", 16), ("ffng", 16), ("gateb", 48), ("convb", 8), ("lng", 8), ("lnb", 8), ("dw", 248),
               ("gqg", 1), ("gkg", 1), ("mqg", 4), ("mkvg", 4), ("fing", 16)]:
    VC[_n] = _o; _o += _w
NV = _o
ND = 24


class Sched:
    def __init__(self):
        self.ops = []
        self.lastw = {}
        self.readers = {}
        self.lastop = {}
        self.dmas_since_bar = []
        self.nbar = 0

    def add(self, eng, meth, args=(), kw=None, reads=(), writes=(), dma=False):
        deps = set()
        for k in reads:
            if k in self.lastw: deps.add(self.lastw[k])
        for k in writes:
            if k in self.lastw: deps.add(self.lastw[k])
            deps.update(self.readers.get(k, {}).values())
        idx = len(self.ops)
        self.ops.append(dict(eng=eng, meth=meth, args=args, kw=kw or {}, deps=deps, dma=dma, inc=False, kind="op"))
        ek = ("dma", idx) if dma else eng
        for k in reads:
            self.readers.setdefault(k, {})[ek] = idx
        for k in writes:
            self.lastw[k] = idx; self.readers[k] = {}
        if dma: self.dmas_since_bar.append(idx)
        else: self.lastop[eng] = idx
        return idx

    def barrier(self):
        deps = set(self.lastop.values()) | set(self.dmas_since_bar)
        self.nbar += 1
        self.ops.append(dict(eng="sp", kind="bar", deps=deps, dma=False, inc=False, val=self.nbar))
        for e in ("pe", "act", "dve", "pool"):
            self.ops.append(dict(eng=e, kind="barwait", deps=set(), dma=False, inc=False, val=self.nbar))
        self.lastw = {}; self.readers = {}; self.lastop = {}; self.dmas_since_bar = []

    def emit(self, nc, stack):
        ops = self.ops
        for op in ops:
            for d in op["deps"]:
                dop = ops[d]
                if dop["dma"]: continue
                if dop["eng"] == "pe" and op["eng"] == "pe" and not op["dma"] and op["kind"] == "op": continue
                dop["inc"] = True
        esem = {e: stack.enter_context(nc.semaphore("c_" + e)) for e in ("pe", "act", "dve", "pool")}
        bsem = stack.enter_context(nc.semaphore("bar"))
        dsem = {q: [stack.enter_context(nc.semaphore("d_%s%d" % (q, i))) for i in range(ND)] for q in ("sp", "pool")}
        cnt = {e: 0 for e in esem}
        dcnt = {"sp": 0, "pool": 0}
        for op in ops:
            if op["kind"] != "op": continue
            if op["dma"]:
                q = op["eng"]; j = dcnt[q]; dcnt[q] += 1
                op["dslot"] = (q, j % ND); op["dval"] = 16 * (j // ND + 1)
            elif op["inc"]:
                cnt[op["eng"]] += 1; op["cnt"] = cnt[op["eng"]]
        streams = {e: [] for e in ("pe", "act", "dve", "pool", "sp")}
        for op in ops: streams[op["eng"]].append(op)
        engs = {"pe": nc.tensor, "act": nc.scalar, "dve": nc.vector, "pool": nc.gpsimd, "sp": nc.sync}
        final_d = {}
        for op in ops:
            if op["kind"] == "op" and op["dma"]: final_d[op["dslot"]] = op["dval"]

        def run_stream(name, E):
            known = {}
            def need(key, sem, val):
                if known.get(key, 0) >= val: return
                known[key] = val
                E.wait_ge(sem, val)
            for op in streams[name]:
                if op["kind"] == "barwait":
                    E.wait_ge(bsem, op["val"]); continue
                for d in sorted(op["deps"]):
                    dop = ops[d]
                    if dop["dma"]:
                        q, s = dop["dslot"]; need(("d", q, s), dsem[q][s], dop["dval"])
                    elif dop["inc"]:
                        if dop["eng"] == name and name == "pe" and not op["dma"] and op["kind"] == "op": continue
                        need(("e", dop["eng"]), esem[dop["eng"]], dop["cnt"])
                if op["kind"] == "bar":
                    E.sem_inc(bsem, 1); continue
                if op["dma"]:
                    q, s = op["dslot"]
                    if op["dval"] > 16: need(("d", q, s), dsem[q][s], op["dval"] - 16)
                    ins = getattr(E, op["meth"])(*op["args"], **op["kw"])
                    ins.then_inc(dsem[q][s], 16)
                else:
                    ins = getattr(E, op["meth"])(*op["args"], **op["kw"])
                    if op["inc"]: ins.then_inc(esem[op["eng"]], 1)
            if name == "sp":
                for (q, s), v in final_d.items(): need(("d", q, s), dsem[q][s], v)

        with nc.Block() as block:
            @block.tensor
            def _(e): run_stream("pe", e)
            @block.scalar
            def _(e): run_stream("act", e)
            @block.vector
            def _(e): run_stream("dve", e)
            @block.gpsimd
            def _(e): run_stream("pool", e)
            @block.sync
            def _(e): run_stream("sp", e)


def build(nl, nseq):
    nc = bass.Bass("TRN2", target_bir_lowering=False)
    def din(name, shape, dt=F32): return nc.dram_tensor(name, list(shape), dt, kind="ExternalInput").ap()
    def scr(name, shape, dt): return nc.dram_tensor(name, list(shape), dt, kind="Internal").ap()
    x = din("x", [nseq, SEQ, D]); meta = din("meta", [NM, D]); vecs = din("vecs", [nl, 128, NV])
    w_in = din("w_in", [nl, D, DIN]); w_co = din("w_co", [nl, CC, D]); w_go = din("w_go", [nl, 1024, D])
    w_mo = din("w_mo", [nl, 1024, D]); w_uq = din("w_uq", [nl, 512, 1536]); w_ukv = din("w_ukv", [nl, 512, 2048])
    w_o = din("w_o", [nl, D, D]); w_fg = din("w_fg", [nl, D, DFF]); w_fu = din("w_fu", [nl, D, DFF])
    w_fd = din("w_fd", [nl, DFF, D])
    cident = din("cident", [128, 128]); cp128 = din("cp128", [128, 128]); cp64 = din("cp64", [64, 64])
    ccs128 = din("ccs128", [2, 128, L]); ccs64 = din("ccs64", [2, 64, L])
    out = nc.dram_tensor("out", [nseq, SEQ, D], F32, kind="ExternalOutput").ap()
    hT = scr("hT", [D, L], F32); zT = scr("zT", [CC, L + 30], BF16)
    qraw = scr("qraw", [1024, L], F32); kraw = scr("kraw", [256, L], F32); vtm = scr("vtm", [L, 256], BF16)
    cqraw = scr("cqraw", [512, L], F32); ckvraw = scr("ckvraw", [512, L], F32); kperaw = scr("kperaw", [64, L], F32)
    gT = scr("gT", [6144, L], BF16); sT = scr("sT", [1024, L], BF16); ogT = scr("ogT", [1024, L], BF16)
    omT = scr("omT", [1024, L], BF16); aT = scr("aT", [128, 5 * 44 * 512], BF16)
    aTv = aT.rearrange("p (i c t) -> p i c t", i=5, c=44)

    S = Sched()
    stack = contextlib.ExitStack()
    SBYTES = 212000
    SB = stack.enter_context(nc.sbuf_tensor("sb", [128, SBYTES], U8))
    PS = [stack.enter_context(nc.psum_tensor("ps%d" % i, [128, 512], F32)) for i in range(8)]
    top = [0]
    def alloc(shape, dt):
        n = int(np.prod(shape)) * (4 if dt == F32 else 2)
        off = (top[0] + 63) // 64 * 64
        assert off + n <= SBYTES, ("sbuf overflow", off, n)
        top[0] = off + n
        ap = SB[:, off:off + n].bitcast(dt)
        if len(shape) == 2:
            ap = ap.rearrange("p (a b) -> p a b", a=shape[0])
        return ap

    ones_f = alloc([128], F32); ones_b = alloc([128], BF16); ident = alloc([128], F32)
    epsc = alloc([1], F32); vec = alloc([NV], F32)
    xt_off = (top[0] + 63) // 64 * 64
    XT = alloc([16, L], BF16)
    gtop = top[0]

    def mm(o, l, r, st, sp, reads, writes): S.add("pe", "matmul", (o, l, r), dict(start=st, stop=sp), reads, writes)
    def act(o, i, f, reads, writes, bias=None, scale=None):
        kw = {}
        if bias is not None: kw["bias"] = bias
        if scale is not None: kw["scale"] = scale
        S.add("act", "activation", (), dict(out=o, in_=i, func=f, **kw), reads, writes)
    def tt(o, a, b, op, reads, writes): S.add("dve", "tensor_tensor", (), dict(out=o, in0=a, in1=b, op=op), reads, writes)
    def stt(o, a, sc, b, op0, op1, reads, writes):
        S.add("dve", "scalar_tensor_tensor", (), dict(out=o, in0=a, scalar=sc, in1=b, op0=op0, op1=op1), reads, writes)
    def recip(o, i, reads, writes): S.add("dve", "reciprocal", (), dict(out=o, in_=i), reads, writes)
    def dma(o, i, reads, writes, q="sp"): S.add(q, "dma_start", (), dict(out=o, in_=i), reads, writes, dma=True)
    def phase_end():
        S.barrier(); top[0] = gtop

    S.add("dve", "memset", (ones_f, 1.0), {}, (), ("ones_f",))
    S.add("dve", "memset", (ones_b, 1.0), {}, (), ("ones_b",))
    S.add("dve", "memset", (epsc, EPS), {}, (), ("epsc",))
    dma(ident, cident, (), ("ident",))
    zpad = alloc([8, 16], BF16)
    S.add("dve", "memset", (zpad, 0.0), {}, (), ("zpad",))
    zTv = zT.rearrange("(c p) t -> p c t", p=128)
    dma(zTv[:, :, 0:15], zpad[:, :, 0:15], ("zpad",), ())
    dma(zTv[:, :, L + 15:L + 30], zpad[:, :, 0:15], ("zpad",), ())
    phase_end()

    def wview(w, l):
        return w[l].rearrange("(c p) m -> p c m", p=128)

    def norm_phase(src, KC, gcol, dst, dkey):
        srcv = src.rearrange("(c p) t -> p c t", p=128)
        hin = [alloc([KC, 512], F32) for _ in range(2)]
        sqs = [alloc([KC, 512], BF16) for _ in range(2)]
        sds = [alloc([512], F32) for _ in range(2)]; rss = [alloc([512], F32) for _ in range(2)]
        def load(i):
            t0, n = T512[i]
            dma(hin[i % 2][:, :, :n], srcv[:, :, t0:t0 + n], (), (("hin", i % 2),))
        load(0)
        for i, (t0, n) in enumerate(T512):
            if i + 1 < len(T512): load(i + 1)
            b = i % 2
            sq = sqs[b]; sd = sds[b]; rs = rss[b]
            act(sq[:, :, :n], hin[b][:, :, :n], AF.Square, (("hin", b),), (("sq", b),))
            pb = PS[i % 2]
            for c in range(KC):
                mm(pb[:, :n], ones_b, sq[:, c, :n], c == 0, c == KC - 1, (("sq", b),), (("ps", i % 2),))
            act(sd[:, :n], pb[:, :n], AF.Ln, (("ps", i % 2),), (("sd", b),), bias=epsc, scale=1.0 / (KC * 128))
            act(rs[:, :n], sd[:, :n], AF.Exp, (("sd", b),), (("rs", b),), scale=-0.5)
            for c in range(KC):
                stt(dst[:, c, t0:t0 + n], hin[b][:, c, :n], vec[:, gcol + c:gcol + c + 1], rs[:, :n], ALU.mult, ALU.mult,
                    (("hin", b), ("rs", b)), ((dkey, c, t0 // 512),))

    def rope(dim, xn, outap, t0, n, cs, perm, psb, tmp1, tmp2, rkeys, wkeys):
        rkeys = tuple(rkeys) + ("cs", "perm")
        mm(PS[psb][:dim, :n], perm, xn, True, True, rkeys, (("ps", psb),))
        tt(tmp1[:dim, :n], xn, cs[:dim, 0, t0:t0 + n], ALU.mult, rkeys, ("rt1",))
        tt(tmp2[:dim, :n], PS[psb][:dim, :n], cs[:dim, 1, t0:t0 + n], ALU.mult, (("ps", psb), "cs"), ("rt2",))
        tt(outap, tmp1[:dim, :n], tmp2[:dim, :n], ALU.add, ("rt1", "rt2"), wkeys)

    def rope_gen(dim, xn, outap, t0, n, cs, perm, psb, tmp1, tmp2, rkeys, wkeys):
        rkeys = tuple(rkeys) + ("cs", "perm")
        mm(PS[psb][:dim, :n], perm, xn, True, True, rkeys, (("ps", psb),))
        tt(tmp1[:dim, :n], xn, cs[:dim, 0, t0:t0 + n], ALU.mult, rkeys, ("rt1",))
        yield
        yield
        tt(tmp2[:dim, :n], PS[psb][:dim, :n], cs[:dim, 1, t0:t0 + n], ALU.mult, (("ps", psb), "cs"), ("rt2",))
        tt(outap, tmp1[:dim, :n], tmp2[:dim, :n], ALU.add, ("rt1", "rt2"), wkeys)
        yield

    def headnorm_rope_gen(raw, rkey, gcol, outap_fn, okey, cs, perm, tmps, psa, psb):
        sq, sd, rs, xn, t1, t2 = tmps
        for ti, (t0, n) in enumerate(T512):
            act(sq[:, :n], raw[:, t0:t0 + n], AF.Square, (rkey,), ("hsq",))
            yield
            yield
            mm(PS[psa][:, :n], ones_f, sq[:, :n], True, True, ("hsq",), (("ps", psa),))
            yield
            act(sd[:, :n], PS[psa][:, :n], AF.Ln, (("ps", psa),), ("hsd",), bias=epsc, scale=1.0 / 128)
            act(rs[:, :n], sd[:, :n], AF.Exp, ("hsd",), ("hrs",), scale=-0.5)
            yield
            stt(xn[:, :n], raw[:, t0:t0 + n], vec[:, gcol:gcol + 1], rs[:, :n], ALU.mult, ALU.mult, (rkey, "hrs"), ("hxn",))
            yield
            yield
            yield from rope_gen(128, xn[:, :n], outap_fn(t0, n), t0, n, cs, perm, psb, t1, t2, ("hxn",), (okey,))

    def headnorm_rope(*a):
        for _ in headnorm_rope_gen(*a): pass

    SBK = (0, 1, 7)
    def attention(qparts, kparts, vfn, scale, rkeys, dst, pt, rd, ot, pacc, bg=None):
        for ti, (t0, n) in enumerate(T512):
            bo = 2 + ti % 2; bd = 4 + ti % 2; ab = ti % 2
            def smm(kc):
                k0, nk = KCH[kc]
                for pi, ((qa, dq), (ka, dk)) in enumerate(zip(qparts, kparts)):
                    mm(PS[SBK[kc % 3]][:nk, :n], ka[:dq, k0:k0 + nk], qa[:dq, t0:t0 + n], pi == 0, pi == len(qparts) - 1,
                       rkeys, (("ps", SBK[kc % 3]),))
            smm(0); smm(1)
            for kc, (k0, nk) in enumerate(KCH):
                if kc + 2 < len(KCH): smm(kc + 2)
                pb = kc % 3; sbk = SBK[kc % 3]
                act(pt[pb][:nk, :n], PS[sbk][:nk, :n], AF.Exp, (("ps", sbk),), (("pt", pb),), scale=scale)
                mm(PS[bo][:, :n], vfn(kc, nk), pt[pb][:nk, :n], kc == 0, kc == len(KCH) - 1, (("pt", pb),) + rkeys, (("ps", bo),))
                mm(PS[bd][:, :n], ones_b[:nk, :], pt[pb][:nk, :n], kc == 0, kc == len(KCH) - 1, (("pt", pb),), (("ps", bd),))
                if bg is not None: next(bg, None)
            act(rd[:, :n], PS[bd][:, :n], AF.Ln, (("ps", bd),), ("ard0",))
            act(rd[:, :n], rd[:, :n], AF.Exp, ("ard0",), ("ard",), scale=-1.0)
            tt(ot[ti % 2][:, :n], PS[bo][:, :n], rd[:, :n], ALU.mult, (("ps", bo), "ard"), (("aot", ti % 2),))
            dma(dst[:, t0:t0 + n], ot[ti % 2][:, :n], (("aot", ti % 2),), ())

    for sq_i in range(nseq):
        xin = [alloc([D], F32) for _ in range(2)]
        hst = [alloc([16, 128], F32) for _ in range(2)]
        hTv = hT.rearrange("(c p) t -> p c t", p=128)
        def p0load(k):
            k0, n = KCH[k]; b = k % 2
            if k == 0:
                dma(xin[b][0:NM, :], meta[:, :], (), (("xin", b),))
                dma(xin[b][NM:128, :], x[sq_i, 0:128 - NM, :], (), (("xin", b),))
            else:
                dma(xin[b][:n, :], x[sq_i, k0 - NM:k0 - NM + n, :], (), (("xin", b),))
        p0load(0)
        for k, (k0, n) in enumerate(KCH):
            if k + 1 < len(KCH): p0load(k + 1)
            b = k % 2
            for c4 in range(4):
                pb = c4 % 2
                for j in range(4):
                    c = c4 * 4 + j
                    S.add("pe", "transpose", (PS[pb][:, j * 128:j * 128 + n], xin[b][:n, c * 128:(c + 1) * 128], ident[:n, :n]), {},
                          (("xin", b), "ident"), (("ps", pb),))
                src = PS[pb][:, :].rearrange("p (j t) -> p j t", j=4)[:, :, :n]
                S.add("dve", "tensor_copy", (), dict(out=hst[b][:, c4 * 4:c4 * 4 + 4, :n], in_=src), (("ps", pb),), (("hst", b),))
            dma(hTv[:, :, k0:k0 + n], hst[b][:, :, :n], (("hst", b),), ())
        phase_end()

        for l in range(nl):
            dma(vec, vecs[l], (), ("vec",))
            S.barrier()
            norm_phase(hT, 16, VC["mixg"], XT, "XT")
            phase_end()
            wv = wview(w_in, l)
            groups = []
            for j in range(8):
                groups.append(("glu", [(128 * j, 128), (1024 + 128 * j, 128)], j))
            for j in range(4): groups.append(("raw", [(2048 + 256 * j, 256)], (qraw, 256 * j)))
            groups.append(("raw", [(3072, 256)], (kraw, 0)))
            groups.append(("vtm", [(3328, 256)], None))
            for j in range(2): groups.append(("raw", [(3584 + 256 * j, 256)], (cqraw, 256 * j)))
            for j in range(2): groups.append(("raw", [(4096 + 256 * j, 256)], (ckvraw, 256 * j)))
            groups.append(("raw64", [(4608, 64)], (kperaw, 0)))
            for j in range(24): groups.append(("gate", [(4672 + 256 * j, 256)], j))
            wb = [alloc([16, 256], BF16) for _ in range(2)]
            tmpf = [alloc([512], F32) for _ in range(2)]
            stb = [alloc([512], BF16) for _ in range(4)]
            stf = [alloc([512], F32) for _ in range(4)]
            vst = [alloc([256], BF16) for _ in range(2)]
            def wload(gi):
                kind, cols, info = groups[gi]; b = gi % 2; o = 0
                for (c0, w) in cols:
                    dma(wb[b][:, :, o:o + w], wv[:, :, c0:c0 + w], (), (("wb", b),), q="pool"); o += w
            wload(0)
            cnt = [0]
            for gi, (kind, cols, info) in enumerate(groups):
                if gi + 1 < len(groups): wload(gi + 1)
                b = gi % 2; W = wb[b]
                if kind == "vtm":
                    for kc, (k0, nk) in enumerate(KCH):
                        pb = kc % 2
                        for c in range(16):
                            mm(PS[pb][:nk, :256], XT[:, c, k0:k0 + nk], W[:, c, 0:256], c == 0, c == 15,
                               (("XT", c, k0 // 512), ("XT", c, (k0 + nk - 1) // 512), ("wb", b)), (("ps", pb),))
                        S.add("act", "activation", (), dict(out=vst[pb][:nk, :], in_=PS[pb][:nk, :256], func=AF.Copy),
                              (("ps", pb),), (("vst", pb),))
                        dma(vtm[k0:k0 + nk, :], vst[pb][:nk, :], (("vst", pb),), ())
                    continue
                nch = 1 if kind == "raw64" else 2
                wd = 64 if kind == "raw64" else 128
                for ti, (t0, n) in enumerate(T512):
                    par = ti % 2
                    for mi in range(nch):
                        pb = par * 2 + mi
                        for c in range(16):
                            mm(PS[pb][:wd, :n], W[:, c, mi * 128:mi * 128 + wd], XT[:, c, t0:t0 + n], c == 0, c == 15,
                               (("XT", c, ti), ("wb", b)), (("ps", pb),))
                    if kind == "glu":
                        j = info; sb_ = cnt[0] % 4; cnt[0] += 1
                        act(tmpf[par][:, :n], PS[par * 2 + 1][:, :n], AF.Sigmoid, (("ps", par * 2 + 1),), (("tmpf", par),))
                        tt(stb[sb_][:, :n], PS[par * 2][:, :n], tmpf[par][:, :n], ALU.mult, (("ps", par * 2), ("tmpf", par)), (("stb", sb_),))
                        dma(zT[j * 128:(j + 1) * 128, 15 + t0:15 + t0 + n], stb[sb_][:, :n], (("stb", sb_),), ())
                    elif kind in ("raw", "raw64"):
                        dst, r0 = info
                        for mi in range(nch):
                            sb_ = cnt[0] % 4; cnt[0] += 1; pb = par * 2 + mi
                            if mi == 0:
                                S.add("act", "activation", (), dict(out=stf[sb_][:wd, :n], in_=PS[pb][:wd, :n], func=AF.Copy), (("ps", pb),), (("stf", sb_),))
                            else:
                                S.add("dve", "tensor_copy", (), dict(out=stf[sb_][:wd, :n], in_=PS[pb][:wd, :n]), (("ps", pb),), (("stf", sb_),))
                            dma(dst[r0 + mi * 128:r0 + mi * 128 + wd, t0:t0 + n], stf[sb_][:wd, :n], (("stf", sb_),), ())
                    elif kind == "gate":
                        j = info
                        for mi in range(2):
                            sb_ = cnt[0] % 4; cnt[0] += 1; pb = par * 2 + mi; ch = 2 * j + mi
                            act(stb[sb_][:, :n], PS[pb][:, :n], AF.Sigmoid, (("ps", pb),), (("stb", sb_),),
                                bias=vec[:, VC["gateb"] + ch:VC["gateb"] + ch + 1])
                            dma(gT[ch * 128:(ch + 1) * 128, t0:t0 + n], stb[sb_][:, :n], (("stb", sb_),), ())
            phase_end()

            identb = alloc([128], BF16)
            S.add("dve", "tensor_copy", (), dict(out=identb, in_=ident), (), ("identb",))
            diag = alloc([248, 128], BF16)
            for i in range(248):
                S.add("dve", "tensor_scalar", (), dict(out=diag[:, i, :], in0=identb, scalar1=vec[:, VC["dw"] + i:VC["dw"] + i + 1],
                                                      scalar2=None, op0=ALU.mult), ("identb",), (("diag", i % 8),))
            zt = [alloc([8, 542], BF16) for _ in range(2)]
            accs = [alloc([8, 512], F32) for _ in range(2)]; sqr = [alloc([512], F32) for _ in range(3)]
            mu = alloc([512], F32); m2 = alloc([512], F32); var = alloc([512], F32); sd = alloc([512], F32); rs = alloc([512], F32)
            xc = [alloc([512], F32) for _ in range(2)]
            st8 = [alloc([8, 512], BF16) for _ in range(1)]
            sTv = sT.rearrange("(c p) t -> p c t", p=128)
            def p2load(ti):
                t0, n = T512[ti]
                dma(zt[ti % 2][:, :, :n + 30], zTv[:, :, t0:t0 + n + 30], (), (("zt", ti % 2),))
            p2load(0)
            for ti, (t0, n) in enumerate(T512):
                if ti + 1 < len(T512): p2load(ti + 1)
                zb = ti % 2; acc = accs[zb]
                def stats(c):
                    mm(PS[4 + 2 * zb][:, :n], ones_f, acc[:, c, :n], c == 0, c == 7, (("acc", zb, c),), (("ps", 4 + 2 * zb),))
                    mm(PS[5 + 2 * zb][:, :n], ones_f, sqr[c % 3][:, :n], c == 0, c == 7, (("sqr", c % 3),), (("ps", 5 + 2 * zb),))
                for c in range(8):
                    pb = c % 4
                    for k in range(31):
                        mm(PS[pb][:, :n], diag[:, k * 8 + c, :], zt[zb][:, c, k:k + n], k == 0, k == 30,
                           (("zt", zb), ("diag", c)), (("ps", pb),))
                    act(acc[:, c, :n], PS[pb][:, :n], AF.Identity, (("ps", pb),), (("acc", zb, c),),
                        bias=vec[:, VC["convb"] + c:VC["convb"] + c + 1])
                    act(sqr[c % 3][:, :n], acc[:, c, :n], AF.Square, (("acc", zb, c),), (("sqr", c % 3),))
                    if c > 0: stats(c - 1)
                stats(7)
                S.add("dve", "tensor_scalar", (), dict(out=mu[:, :n], in0=PS[4 + 2 * zb][:, :n], scalar1=1.0 / CC, scalar2=None, op0=ALU.mult), (("ps", 4 + 2 * zb),), ("mu",))
                tt(m2[:, :n], mu[:, :n], mu[:, :n], ALU.mult, ("mu",), ("m2",))
                stt(var[:, :n], PS[5 + 2 * zb][:, :n], 1.0 / CC, m2[:, :n], ALU.mult, ALU.subtract, (("ps", 5 + 2 * zb), "m2"), ("var",))
                act(sd[:, :n], var[:, :n], AF.Ln, ("var",), ("sd",), bias=epsc)
                act(rs[:, :n], sd[:, :n], AF.Exp, ("sd",), ("rs",), scale=-0.5)
                for c in range(8):
                    xb = c % 2
                    tt(xc[xb][:, :n], acc[:, c, :n], mu[:, :n], ALU.subtract, (("acc", zb, c), "mu"), (("xc", xb),))
                    tt(xc[xb][:, :n], xc[xb][:, :n], rs[:, :n], ALU.mult, (("xc", xb), "rs"), (("xc", xb),))
                    act(st8[0][:, c, :n], xc[xb][:, :n], AF.Silu, (("xc", xb),), (("st8", 0),),
                        bias=vec[:, VC["lnb"] + c:VC["lnb"] + c + 1], scale=vec[:, VC["lng"] + c:VC["lng"] + c + 1])
                dma(sTv[:, :, t0:t0 + n], st8[0][:, :, :n], (("st8", 0),), ())
            phase_end()

            cs = alloc([2, L], F32); perm = alloc([128], F32)
            dma(cs[:, 0, :], ccs128[0], (), ("cs",)); dma(cs[:, 1, :], ccs128[1], (), ("cs",))
            dma(perm, cp128, (), ("perm",))
            kr = alloc([2, L], F32); kb = alloc([2, L], BF16); vb = alloc([17, 256], BF16)
            dma(kr, kraw.rearrange("(c p) t -> p c t", p=128), (), ("kr",))
            dma(vb[:, 0:16, :], vtm[0:2048, :].rearrange("(k p) d -> p k d", p=128), (), ("vb",))
            dma(vb[0:16, 16, :], vtm[2048:L, :], (), ("vb",))
            tmps = (alloc([512], F32), alloc([512], F32), alloc([512], F32), alloc([512], F32), alloc([512], F32), alloc([512], F32))
            for kvh in range(2):
                headnorm_rope(kr[:, kvh, :], "kr", VC["gkg"], lambda t0, n, kvh=kvh: kb[:, kvh, t0:t0 + n], "kb", cs, perm, tmps, 6, 6)
            qr = [alloc([L], F32) for _ in range(2)]; qb = [alloc([L], BF16) for _ in range(2)]
            pt = [alloc([512], BF16) for _ in range(3)]; rd = alloc([512], F32); ot = [alloc([512], BF16) for _ in range(2)]
            pacc = None
            dma(qr[0], qraw[0:128, :], (), (("qr", 0),))
            dma(qr[1], qraw[128:256, :], (), (("qr", 1),))
            def qgen(h):
                hb = h % 2
                return headnorm_rope_gen(qr[hb], ("qr", hb), VC["gqg"], lambda t0, n, hb=hb: qb[hb][:, t0:t0 + n], ("qb", hb), cs, perm, tmps, 6, 6)
            for _ in qgen(0): pass
            for h in range(8):
                hb = h % 2; kvh = h // 4
                bg = qgen(h + 1) if h + 1 < 8 else None
                attention([(qb[hb], 128)], [(kb[:, kvh, :], 128)], lambda kc, nk, kvh=kvh: vb[:nk, kc, kvh * 128:(kvh + 1) * 128],
                          1.0 / np.sqrt(128.0), (("qb", hb), "kb", "vb"), ogT[h * 128:(h + 1) * 128, :], pt, rd, ot, pacc, bg)
                if bg is not None:
                    for _ in bg: pass
                if h + 2 < 8: dma(qr[hb], qraw[(h + 2) * 128:(h + 3) * 128, :], (), (("qr", hb),))
            phase_end()

            cqn = alloc([4, L], BF16); ckvn = alloc([4, L], BF16)
            mark = top[0]
            norm_phase(cqraw, 4, VC["mqg"], cqn, "cqn")
            norm_phase(ckvraw, 4, VC["mkvg"], ckvn, "ckvn")
            S.barrier(); top[0] = mark
            cs = alloc([2, L], F32); perm = alloc([64], F32)
            dma(cs[0:64, 0, :], ccs64[0], (), ("cs",)); dma(cs[0:64, 1, :], ccs64[1], (), ("cs",))
            dma(perm[0:64, :], cp64, (), ("perm",))
            wuq = alloc([4, 1536], BF16); wukv = alloc([4, 2048], BF16)
            dma(wuq, wview(w_uq, l), (), ("wuq",), q="pool"); dma(wukv, wview(w_ukv, l), (), ("wukv",), q="pool")
            kpb = alloc([L], BF16)
            t1 = alloc([512], F32); t2 = alloc([512], F32); xq = alloc([512], F32)
            mark2 = top[0]
            kpr = alloc([L], F32)
            dma(kpr[0:64, :], kperaw, (), ("kpr",))
            for ti, (t0, n) in enumerate(T512):
                rope(64, kpr[0:64, t0:t0 + n], kpb[0:64, t0:t0 + n], t0, n, cs, perm[0:64, :], 6, t1, t2, ("kpr", "cs", "perm"), ("kpb",))
            S.barrier(); top[0] = mark2
            qnb = [alloc([L], BF16) for _ in range(2)]; qpb = [alloc([L], BF16) for _ in range(2)]
            knb = [alloc([L], BF16) for _ in range(2)]; vhb = [alloc([17, 128], BF16) for _ in range(2)]
            pt = [alloc([512], BF16) for _ in range(3)]; rd = alloc([512], F32); ot = [alloc([512], BF16) for _ in range(2)]
            pacc = None
            def mgen(h):
                hb = h % 2
                for ti, (t0, n) in enumerate(T512):
                    tk = tuple(("cqn", c, ti) for c in range(4)); tk2 = tuple(("ckvn", c, ti) for c in range(4))
                    for c in range(4):
                        mm(PS[6][:, :n], wuq[:, c, h * 192:h * 192 + 128], cqn[:, c, t0:t0 + n], c == 0, c == 3, tk + ("wuq",), (("ps", 6),))
                    yield
                    S.add("act", "activation", (), dict(out=qnb[hb][:, t0:t0 + n], in_=PS[6][:, :n], func=AF.Copy), (("ps", 6),), (("qnb", hb),))
                    for c in range(4):
                        mm(PS[6][:64, :n], wuq[:, c, h * 192 + 128:h * 192 + 192], cqn[:, c, t0:t0 + n], c == 0, c == 3, tk + ("wuq",), (("ps", 6),))
                    yield
                    S.add("dve", "tensor_copy", (), dict(out=xq[0:64, :n], in_=PS[6][:64, :n]), (("ps", 6),), ("xq",))
                    yield
                    yield from rope_gen(64, xq[0:64, :n], qpb[hb][0:64, t0:t0 + n], t0, n, cs, perm[0:64, :], 6, t1, t2, ("xq",), (("qpb", hb),))
                    for c in range(4):
                        mm(PS[6][:, :n], wukv[:, c, h * 256:h * 256 + 128], ckvn[:, c, t0:t0 + n], c == 0, c == 3, tk2 + ("wukv",), (("ps", 6),))
                    yield
                    S.add("act", "activation", (), dict(out=knb[hb][:, t0:t0 + n], in_=PS[6][:, :n], func=AF.Copy), (("ps", 6),), (("knb", hb),))
                    yield
                for kc, (k0, nk) in enumerate(KCH):
                    tk2 = tuple(("ckvn", c, tt_) for c in range(4) for tt_ in {k0 // 512, (k0 + nk - 1) // 512})
                    for c in range(4):
                        mm(PS[6][:nk, :128], ckvn[:, c, k0:k0 + nk], wukv[:, c, h * 256 + 128:h * 256 + 256], c == 0, c == 3, tk2 + ("wukv",), (("ps", 6),))
                    yield
                    S.add("dve", "tensor_copy", (), dict(out=vhb[hb][:nk, kc, :], in_=PS[6][:nk, :128]), (("ps", 6),), (("vhb", hb),))
                    yield
            for _ in mgen(0): pass
            for h in range(8):
                hb = h % 2
                bg = mgen(h + 1) if h + 1 < 8 else None
                attention([(qnb[hb], 128), (qpb[hb], 64)], [(knb[hb], 128), (kpb, 64)], lambda kc, nk, hb=hb: vhb[hb][:nk, kc, :],
                          1.0 / np.sqrt(192.0), (("qnb", hb), ("qpb", hb), ("knb", hb), "kpb", ("vhb", hb)), omT[h * 128:(h + 1) * 128, :], pt, rd, ot, pacc, bg)
                if bg is not None:
                    for _ in bg: pass
            phase_end()

            br = [alloc([8, L], BF16) for _ in range(3)]
            for i, src in enumerate((sT, ogT, omT)):
                dma(br[i], src.rearrange("(c p) t -> p c t", p=128), (), (("br", i),))
            wm = [alloc([24, 128], BF16) for _ in range(2)]
            gt = [alloc([3, 512], BF16) for _ in range(2)]
            f0 = alloc([512], F32); f1 = alloc([512], F32)
            wsrc = [wview(w_co, l), wview(w_go, l), wview(w_mo, l)]
            gTv = gT.rearrange("(b f) t -> f b t", b=3)
            def p5w(m):
                for i in range(3):
                    dma(wm[m % 2][:, i * 8:(i + 1) * 8, :], wsrc[i][:, :, m * 128:(m + 1) * 128], (), (("wm", m % 2),), q="pool")
            p5w(0)
            it = 0
            its = [(m, ti) for m in range(16) for ti in range(len(T512))]
            def p5g(i):
                m_, ti_ = its[i]; t0_, n_ = T512[ti_]
                dma(gt[i % 2][:, :, :n_], gTv[m_ * 128:(m_ + 1) * 128, :, t0_:t0_ + n_], (), (("gt", i % 2),))
            p5g(0)
            for m in range(16):
                if m + 1 < 16: p5w(m + 1)
                for ti, (t0, n) in enumerate(T512):
                    par = it % 2; it += 1
                    if it < len(its): p5g(it)
                    for i in range(3):
                        pb = par * 3 + i
                        for c in range(8):
                            mm(PS[pb][:, :n], wm[m % 2][:, i * 8 + c, :], br[i][:, c, t0:t0 + n], c == 0, c == 7, (("br", i), ("wm", m % 2)), (("ps", pb),))
                    tt(f0[:, :n], PS[par * 3][:, :n], gt[par][:, 0, :n], ALU.mult, (("ps", par * 3), ("gt", par)), ("f0",))
                    tt(f1[:, :n], PS[par * 3 + 1][:, :n], gt[par][:, 1, :n], ALU.mult, (("ps", par * 3 + 1), ("gt", par)), ("f1",))
                    tt(f0[:, :n], f0[:, :n], f1[:, :n], ALU.add, ("f0", "f1"), ("f0",))
                    tt(f1[:, :n], PS[par * 3 + 2][:, :n], gt[par][:, 2, :n], ALU.mult, (("ps", par * 3 + 2), ("gt", par)), ("f1",))
                    tt(XT[:, m, t0:t0 + n], f0[:, :n], f1[:, :n], ALU.add, ("f0", "f1"), (("XT", m, ti),))
            phase_end()

            def resid_gemm(wl, KCt, xfn, xkeys_fn, ncol_chunks=1):
                pass

            wo = [alloc([16, 128], BF16) for _ in range(2)]
            hr = [alloc([512], F32) for _ in range(2)]; ho = [alloc([512], F32) for _ in range(2)]
            wov = wview(w_o, l)
            dma(wo[0], wov[:, :, 0:128], (), (("wo", 0),), q="pool")
            it = 0
            its = [(m, ti) for m in range(16) for ti in range(len(T512))]
            def p5h(i):
                m_, ti_ = its[i]; t0_, n_ = T512[ti_]
                dma(hr[i % 2][:, :n_], hT[m_ * 128:(m_ + 1) * 128, t0_:t0_ + n_], (), (("hr", i % 2),))
            p5h(0)
            for m in range(16):
                if m + 1 < 16: dma(wo[(m + 1) % 2], wov[:, :, (m + 1) * 128:(m + 2) * 128], (), (("wo", (m + 1) % 2),), q="pool")
                for ti, (t0, n) in enumerate(T512):
                    par = it % 2; it += 1
                    if it < len(its): p5h(it)
                    for c in range(16):
                        mm(PS[par][:, :n], wo[m % 2][:, c, :], XT[:, c, t0:t0 + n], c == 0, c == 15, (("XT", c, ti), ("wo", m % 2)), (("ps", par),))
                    tt(ho[par][:, :n], PS[par][:, :n], hr[par][:, :n], ALU.add, (("ps", par), ("hr", par)), (("ho", par),))
                    dma(hT[m * 128:(m + 1) * 128, t0:t0 + n], ho[par][:, :n], (("ho", par),), ())
            phase_end()

            norm_phase(hT, 16, VC["ffng"], XT, "XT")
            phase_end()
            wb = [alloc([16, 256], BF16) for _ in range(2)]
            tmpf = [alloc([512], F32) for _ in range(2)]; stb = [alloc([512], BF16) for _ in range(2)]
            wgv = wview(w_fg, l); wuv = wview(w_fu, l)
            def p6w(j):
                dma(wb[j % 2][:, :, 0:128], wgv[:, :, j * 128:(j + 1) * 128], (), (("wb", j % 2),), q="pool")
                dma(wb[j % 2][:, :, 128:256], wuv[:, :, j * 128:(j + 1) * 128], (), (("wb", j % 2),), q="pool")
            p6w(0)
            it = 0
            for j in range(44):
                if j + 1 < 44: p6w(j + 1)
                for ti, (t0, n) in enumerate(T512):
                    par = it % 2; it += 1
                    for mi in range(2):
                        pb = par * 2 + mi
                        for c in range(16):
                            mm(PS[pb][:, :n], wb[j % 2][:, c, mi * 128:(mi + 1) * 128], XT[:, c, t0:t0 + n], c == 0, c == 15,
                               (("XT", c, ti), ("wb", j % 2)), (("ps", pb),))
                    act(tmpf[par][:, :n], PS[par * 2][:, :n], AF.Silu, (("ps", par * 2),), (("tmpf", par),))
                    tt(stb[par][:, :n], PS[par * 2 + 1][:, :n], tmpf[par][:, :n], ALU.mult, (("ps", par * 2 + 1), ("tmpf", par)), (("stb", par),))
                    dma(aTv[:, ti, j, :n], stb[par][:, :n], (("stb", par),), ())
            phase_end()
            wd = [SB[:, xt_off:xt_off + 44 * 512 * 2].bitcast(BF16).rearrange("p (a b) -> p a b", a=44), alloc([44, 512], BF16)]
            at = [alloc([22, 512], BF16) for _ in range(2)]
            hr = [alloc([4, 512], F32) for _ in range(2)]; ho = [alloc([4, 512], F32) for _ in range(2)]
            wdv = wview(w_fd, l)
            hTv2 = hT.rearrange("(c p) t -> p c t", p=128)
            dma(wd[0], wdv[:, :, 0:512], (), (("wd", 0),), q="pool")
            it = 0; ia = 0
            steps = [(me, ti, kh) for me in range(4) for ti in range(len(T512)) for kh in range(2)]
            def p6l(i):
                me_, ti_, kh_ = steps[i]; t0_, n_ = T512[ti_]
                if kh_ == 0:
                    dma(hr[(i // 2) % 2][:, :, :n_], hTv2[:, me_ * 4:me_ * 4 + 4, t0_:t0_ + n_], (), (("hr", (i // 2) % 2),))
                dma(at[i % 2][:, :, :n_], aTv[:, ti_, kh_ * 22:(kh_ + 1) * 22, :n_], (), (("at", i % 2),))
            p6l(0)
            for me in range(4):
                if me + 1 < 4: dma(wd[(me + 1) % 2], wdv[:, :, (me + 1) * 512:(me + 2) * 512], (), (("wd", (me + 1) % 2),), q="pool")
                for ti, (t0, n) in enumerate(T512):
                    par = it % 2; it += 1
                    for kh in range(2):
                        ab = ia % 2; ia += 1
                        if ia < len(steps): p6l(ia)
                        for mi in range(4):
                            pb = par * 4 + mi
                            for c in range(22):
                                mm(PS[pb][:, :n], wd[me % 2][:, kh * 22 + c, mi * 128:(mi + 1) * 128], at[ab][:, c, :n],
                                   kh == 0 and c == 0, kh == 1 and c == 21, (("at", ab), ("wd", me % 2)), (("ps", pb),))
                    for mi in range(4):
                        pb = par * 4 + mi
                        tt(ho[par][:, mi, :n], PS[pb][:, :n], hr[par][:, mi, :n], ALU.add, (("ps", pb), ("hr", par)), (("ho", par),))
                    dma(hTv2[:, me * 4:me * 4 + 4, t0:t0 + n], ho[par][:, :, :n], (("ho", par),), ())
            phase_end()

        hin = [alloc([16, 128], F32) for _ in range(2)]; sq = alloc([16, 128], F32)
        sd = alloc([128], F32); rs = alloc([128], F32); un = alloc([16, 128], F32)
        orow = [alloc([D], F32) for _ in range(2)]
        hTv = hT.rearrange("(c p) t -> p c t", p=128)
        dma(hin[0], hTv[:, :, NM:NM + 128], (), (("hin", 0),))
        for kk in range(16):
            if kk + 1 < 16: dma(hin[(kk + 1) % 2], hTv[:, :, NM + (kk + 1) * 128:NM + (kk + 2) * 128], (), (("hin", (kk + 1) % 2),))
            b = kk % 2
            act(sq, hin[b], AF.Square, (("hin", b),), ("sq",))
            for c in range(16):
                mm(PS[7][:, :128], ones_f, sq[:, c, :], c == 0, c == 15, ("sq",), (("ps", 7),))
            act(sd, PS[7][:, :128], AF.Ln, (("ps", 7),), ("sd",), bias=epsc, scale=1.0 / D)
            act(rs, sd, AF.Exp, ("sd",), ("rs",), scale=-0.5)
            for c in range(16):
                stt(un[:, c, :], hin[b][:, c, :], vec[:, VC["fing"] + c:VC["fing"] + c + 1], rs, ALU.mult, ALU.mult, (("hin", b), "rs"), (("un", c),))
            for c4 in range(4):
                pb = c4 % 2
                for j in range(4):
                    c = c4 * 4 + j
                    S.add("pe", "transpose", (PS[pb][:, j * 128:(j + 1) * 128], un[:, c, :], ident), {}, (("un", c),), (("ps", pb),))
                S.add("dve", "tensor_copy", (), dict(out=orow[b][:, c4 * 512:(c4 + 1) * 512], in_=PS[pb][:, :]), (("ps", pb),), (("orow", b),))
            dma(out[sq_i, kk * 128:(kk + 1) * 128, :], orow[b], (("orow", b),), ())
        phase_end()

    S.emit(nc, stack)
    stack.close()
    return nc


def _consts():
    def tables(dim):
        half_blk = dim // 2
        hh = half_blk // 2
        inv = (10000.0 ** (-np.arange(hh, dtype=np.float32) / np.float32(hh))).astype(np.float32)
        rows = np.concatenate([np.zeros(NM, np.float32), np.repeat(np.arange(SEQ // 64, dtype=np.float32), 64)])
        cols = np.concatenate([np.zeros(NM, np.float32), np.tile(np.arange(64, dtype=np.float32), SEQ // 64)])
        cs = np.zeros((2, dim, L), np.float32)
        perm = np.zeros((dim, dim), np.float32)
        for d in range(dim):
            blk = d // half_blk; dd = d % half_blk; j = dd % hh
            pos = rows if blk == 0 else cols
            ang = (pos * inv[j]).astype(np.float32)
            cs[0, d] = np.cos(ang)
            s = np.sin(ang)
            if dd < hh:
                cs[1, d] = -s; src = d + hh
            else:
                cs[1, d] = s; src = d - hh
            perm[src, d] = 1.0
        return cs, perm
    cs128, p128 = tables(128)
    cs64, p64 = tables(64)
    return dict(cident=np.eye(128, dtype=np.float32), cp128=p128, cp64=p64, ccs128=cs128, ccs64=cs64)


def _vecs(p, nl):
    v = np.zeros((nl, 128, NV), np.float32)
    def put(name, arr, l):
        a = np.asarray(arr, np.float32).reshape(-1, 128).T
        v[l, :, VC[name]:VC[name] + a.shape[1]] = a
    for l in range(nl):
        put("mixg", p["mix_norm_g"][l], l); put("ffng", p["ffn_norm_g"][l], l); put("gateb", p["gate_b"][l], l)
        put("convb", p["conv_b"][l], l); put("lng", p["conv_ln_g"][l], l); put("lnb", p["conv_ln_b"][l], l)
        dw = np.asarray(p["conv_dw"][l], np.float32).reshape(31, 8, 128)
        v[l, :, VC["dw"]:VC["dw"] + 248] = dw.reshape(248, 128).T
        put("gqg", p["gqa_q_norm_g"][l], l); put("gkg", p["gqa_k_norm_g"][l], l)
        put("mqg", p["mla_q_norm_g"][l], l); put("mkvg", p["mla_kv_norm_g"][l], l)
        put("fing", p["final_norm_g"], l)
    return v


_NC_CACHE = {}


def run(p, nl, nseq, ncores):
    key = (nl, nseq)
    if key not in _NC_CACHE: _NC_CACHE[key] = build(nl, nseq)
    nc = _NC_CACHE[key]
    consts = _consts()
    f = lambda a: np.ascontiguousarray(np.asarray(a, np.float32))
    shared = dict(meta=f(p["meta_tokens"]), vecs=_vecs(p, nl), w_in=f(p["w_in"][:nl]), w_co=f(p["w_conv_out"][:nl]),
                  w_go=f(p["w_gqa_out"][:nl]), w_mo=f(p["w_mla_out"][:nl]), w_uq=f(p["w_mla_uq"][:nl]),
                  w_ukv=f(p["w_mla_ukv"][:nl]), w_o=f(p["w_out"][:nl]), w_fg=f(p["w_ffn_gate"][:nl]),
                  w_fu=f(p["w_ffn_up"][:nl]), w_fd=f(p["w_ffn_down"][:nl]), **consts)
    xs = f(p["x"])
    in_maps = [dict(shared, x=xs[i * nseq:(i + 1) * nseq]) for i in range(ncores)]
    res = run_bass_kernel_spmd(nc, in_maps, core_ids=list(range(ncores)))
    return np.concatenate([r["out"] for r in res.results], axis=0)


def kernel(**inputs):
    return run(inputs, 4, 2, 8)
```

```python
import contextlib
import numpy as np
import concourse.bass as bass
import concourse.mybir as mybir
from concourse.bass_utils import run_bass_kernel_spmd

F32, BF16, U8 = mybir.dt.float32, mybir.dt.bfloat16, mybir.dt.uint8
AF = mybir.ActivationFunctionType
ALU = mybir.AluOpType

D = 2048; SEQ = 2048; NM = 16; L = SEQ + NM; DIN = 10816; DFF = 5632; CC = 1024
EPS = 1e-6
T512 = [(i * 512, min(512, L - i * 512)) for i in range((L + 511) // 512)]
T256 = [(i * 256, min(256, L - i * 256)) for i in range((L + 255) // 256)]
KCH = [(i * 128, min(128, L - i * 128)) for i in range((L + 127) // 128)]
VC = {}
_o = 0
for _n, _w in [("mixg", 16), ("ffng", 16), ("gateb", 48), ("convb", 8), ("lng", 8), ("lnb", 8), ("dw", 248),
               ("gqg", 1), ("gkg", 1), ("mqg", 4), ("mkvg", 4), ("fing", 16)]:
    VC[_n] = _o; _o += _w
NV = _o
ND = 24


class Sched:
    def __init__(self):
        self.ops = []
        self.lastw = {}
        self.readers = {}
        self.lastop = {}
        self.dmas_since_bar = []
        self.nbar = 0

    def add(self, eng, meth, args=(), kw=None, reads=(), writes=(), dma=False):
        deps = set()
        for k in reads:
            if k in self.lastw: deps.add(self.lastw[k])
        for k in writes:
            if k in self.lastw: deps.add(self.lastw[k])
            deps.update(self.readers.get(k, {}).values())
        idx = len(self.ops)
        self.ops.append(dict(eng=eng, meth=meth, args=args, kw=kw or {}, deps=deps, dma=dma, inc=False, kind="op"))
        ek = ("dma", idx) if dma else eng
        for k in reads:
            self.readers.setdefault(k, {})[ek] = idx
        for k in writes:
            self.lastw[k] = idx; self.readers[k] = {}
        if dma: self.dmas_since_bar.append(idx)
        else: self.lastop[eng] = idx
        return idx

    def barrier(self):
        deps = set(self.lastop.values()) | set(self.dmas_since_bar)
        self.nbar += 1
        self.ops.append(dict(eng="sp", kind="bar", deps=deps, dma=False, inc=False, val=self.nbar))
        for e in ("pe", "act", "dve", "pool"):
            self.ops.append(dict(eng=e, kind="barwait", deps=set(), dma=False, inc=False, val=self.nbar))
        self.lastw = {}; self.readers = {}; self.lastop = {}; self.dmas_since_bar = []

    def emit(self, nc, stack):
        ops = self.ops
        for op in ops:
            for d in op["deps"]:
                dop = ops[d]
                if dop["dma"]: continue
                if dop["eng"] == "pe" and op["eng"] == "pe" and not op["dma"] and op["kind"] == "op": continue
                dop["inc"] = True
        esem = {e: stack.enter_context(nc.semaphore("c_" + e)) for e in ("pe", "act", "dve", "pool")}
        bsem = stack.enter_context(nc.semaphore("bar"))
        dsem = {q: [stack.enter_context(nc.semaphore("d_%s%d" % (q, i))) for i in range(ND)] for q in ("sp", "pool")}
        cnt = {e: 0 for e in esem}
        dcnt = {"sp": 0, "pool": 0}
        for op in ops:
            if op["kind"] != "op": continue
            if op["dma"]:
                q = op["eng"]; j = dcnt[q]; dcnt[q] += 1
                op["dslot"] = (q, j % ND); op["dval"] = 16 * (j // ND + 1)
            elif op["inc"]:
                cnt[op["eng"]] += 1; op["cnt"] = cnt[op["eng"]]
        streams = {e: [] for e in ("pe", "act", "dve", "pool", "sp")}
        for op in ops: streams[op["eng"]].append(op)
        engs = {"pe": nc.tensor, "act": nc.scalar, "dve": nc.vector, "pool": nc.gpsimd, "sp": nc.sync}
        final_d = {}
        for op in ops:
            if op["kind"] == "op" and op["dma"]: final_d[op["dslot"]] = op["dval"]

        def run_stream(name, E):
            known = {}
            def need(key, sem, val):
                if known.get(key, 0) >= val: return
                known[key] = val
                E.wait_ge(sem, val)
            for op in streams[name]:
                if op["kind"] == "barwait":
                    E.wait_ge(bsem, op["val"]); continue
                for d in sorted(op["deps"]):
                    dop = ops[d]
                    if dop["dma"]:
                        q, s = dop["dslot"]; need(("d", q, s), dsem[q][s], dop["dval"])
                    elif dop["inc"]:
                        if dop["eng"] == name and name == "pe" and not op["dma"] and op["kind"] == "op": continue
                        need(("e", dop["eng"]), esem[dop["eng"]], dop["cnt"])
                if op["kind"] == "bar":
                    E.sem_inc(bsem, 1); continue
                if op["dma"]:
                    q, s = op["dslot"]
                    if op["dval"] > 16: need(("d", q, s), dsem[q][s], op["dval"] - 16)
                    ins = getattr(E, op["meth"])(*op["args"], **op["kw"])
                    ins.then_inc(dsem[q][s], 16)
                else:
                    ins = getattr(E, op["meth"])(*op["args"], **op["kw"])
                    if op["inc"]: ins.then_inc(esem[op["eng"]], 1)
            if name == "sp":
                for (q, s), v in final_d.items(): need(("d", q, s), dsem[q][s], v)

        with nc.Block() as block:
            @block.tensor
            def _(e): run_stream("pe", e)
            @block.scalar
            def _(e): run_stream("act", e)
            @block.vector
            def _(e): run_stream("dve", e)
            @block.gpsimd
            def _(e): run_stream("pool", e)
            @block.sync
            def _(e): run_stream("sp", e)


def build(nl, nseq):
    nc = bass.Bass("TRN2", target_bir_lowering=False)
    def din(name, shape, dt=F32): return nc.dram_tensor(name, list(shape), dt, kind="ExternalInput").ap()
    def scr(name, shape, dt): return nc.dram_tensor(name, list(shape), dt, kind="Internal").ap()
    x = din("x", [nseq, SEQ, D]); meta = din("meta", [NM, D]); vecs = din("vecs", [nl, 128, NV])
    w_in = din("w_in", [nl, D, DIN]); w_co = din("w_co", [nl, CC, D]); w_go = din("w_go", [nl, 1024, D])
    w_mo = din("w_mo", [nl, 1024, D]); w_uq = din("w_uq", [nl, 512, 1536]); w_ukv = din("w_ukv", [nl, 512, 2048])
    w_o = din("w_o", [nl, D, D]); w_fg = din("w_fg", [nl, D, DFF]); w_fu = din("w_fu", [nl, D, DFF])
    w_fd = din("w_fd", [nl, DFF, D])
    cident = din("cident", [128, 128]); cp128 = din("cp128", [128, 128]); cp64 = din("cp64", [64, 64])
    ccs128 = din("ccs128", [2, 128, L]); ccs64 = din("ccs64", [2, 64, L])
    out = nc.dram_tensor("out", [nseq, SEQ, D], F32, kind="ExternalOutput").ap()
    hT = scr("hT", [D, L], F32); zT = scr("zT", [CC, L + 30], BF16)
    qraw = scr("qraw", [1024, L], F32); kraw = scr("kraw", [256, L], F32); vtm = scr("vtm", [L, 256], BF16)
    cqraw = scr("cqraw", [512, L], F32); ckvraw = scr("ckvraw", [512, L], F32); kperaw = scr("kperaw", [64, L], F32)
    gT = scr("gT", [6144, L], BF16); sT = scr("sT", [1024, L], BF16); ogT = scr("ogT", [1024, L], BF16)
    omT = scr("omT", [1024, L], BF16); aT = scr("aT", [128, 5 * 44 * 512], BF16)
    aTv = aT.rearrange("p (i c t) -> p i c t", i=5, c=44)

    S = Sched()
    stack = contextlib.ExitStack()
    SBYTES = 212000
    SB = stack.enter_context(nc.sbuf_tensor("sb", [128, SBYTES], U8))
    PS = [stack.enter_context(nc.psum_tensor("ps%d" % i, [128, 512], F32)) for i in range(8)]
    top = [0]
    def alloc(shape, dt):
        n = int(np.prod(shape)) * (4 if dt == F32 else 2)
        off = (top[0] + 63) // 64 * 64
        assert off + n <= SBYTES, ("sbuf overflow", off, n)
        top[0] = off + n
        ap = SB[:, off:off + n].bitcast(dt)
        if len(shape) == 2:
            ap = ap.rearrange("p (a b) -> p a b", a=shape[0])
        return ap

    ones_f = alloc([128], F32); ones_b = alloc([128], BF16); ident = alloc([128], F32)
    epsc = alloc([1], F32); vec = alloc([NV], F32)
    xt_off = (top[0] + 63) // 64 * 64
    XT = alloc([16, L], BF16)
    gtop = top[0]

    def mm(o, l, r, st, sp, reads, writes): S.add("pe", "matmul", (o, l, r), dict(start=st, stop=sp), reads, writes)
    def act(o, i, f, reads, writes, bias=None, scale=None):
        kw = {}
        if bias is not None: kw["bias"] = bias
        if scale is not None: kw["scale"] = scale
        S.add("act", "activation", (), dict(out=o, in_=i, func=f, **kw), reads, writes)
    def tt(o, a, b, op, reads, writes): S.add("dve", "tensor_tensor", (), dict(out=o, in0=a, in1=b, op=op), reads, writes)
    def stt(o, a, sc, b, op0, op1, reads, writes):
        S.add("dve", "scalar_tensor_tensor", (), dict(out=o, in0=a, scalar=sc, in1=b, op0=op0, op1=op1), reads, writes)
    def recip(o, i, reads, writes): S.add("dve", "reciprocal", (), dict(out=o, in_=i), reads, writes)
    def dma(o, i, reads, writes, q="sp"): S.add(q, "dma_start", (), dict(out=o, in_=i), reads, writes, dma=True)
    def phase_end():
        S.barrier(); top[0] = gtop

    S.add("dve", "memset", (ones_f, 1.0), {}, (), ("ones_f",))
    S.add("dve", "memset", (ones_b, 1.0), {}, (), ("ones_b",))
    S.add("dve", "memset", (epsc, EPS), {}, (), ("epsc",))
    dma(ident, cident, (), ("ident",))
    zpad = alloc([8, 16], BF16)
    S.add("dve", "memset", (zpad, 0.0), {}, (), ("zpad",))
    zTv = zT.rearrange("(c p) t -> p c t", p=128)
    dma(zTv[:, :, 0:15], zpad[:, :, 0:15], ("zpad",), ())
    dma(zTv[:, :, L + 15:L + 30], zpad[:, :, 0:15], ("zpad",), ())
    phase_end()

    def wview(w, l):
        return w[l].rearrange("(c p) m -> p c m", p=128)

    def norm_phase(src, KC, gcol, dst, dkey):
        srcv = src.rearrange("(c p) t -> p c t", p=128)
        hin = [alloc([KC, 512], F32) for _ in range(2)]
        sqs = [alloc([KC, 512], BF16) for _ in range(2)]
        sds = [alloc([512], F32) for _ in range(2)]; rss = [alloc([512], F32) for _ in range(2)]
        def load(i):
            t0, n = T512[i]
            dma(hin[i % 2][:, :, :n], srcv[:, :, t0:t0 + n], (), (("hin", i % 2),))
        load(0)
        for i, (t0, n) in enumerate(T512):
            if i + 1 < len(T512): load(i + 1)
            b = i % 2
            sq = sqs[b]; sd = sds[b]; rs = rss[b]
            act(sq[:, :, :n], hin[b][:, :, :n], AF.Square, (("hin", b),), (("sq", b),))
            pb = PS[i % 2]
            for c in range(KC):
                mm(pb[:, :n], ones_b, sq[:, c, :n], c == 0, c == KC - 1, (("sq", b),), (("ps", i % 2),))
            act(sd[:, :n], pb[:, :n], AF.Ln, (("ps", i % 2),), (("sd", b),), bias=epsc, scale=1.0 / (KC * 128))
            act(rs[:, :n], sd[:, :n], AF.Exp, (("sd", b),), (("rs", b),), scale=-0.5)
            for c in range(KC):
                stt(dst[:, c, t0:t0 + n], hin[b][:, c, :n], vec[:, gcol + c:gcol + c + 1], rs[:, :n], ALU.mult, ALU.mult,
                    (("hin", b), ("rs", b)), ((dkey, c, t0 // 512),))

    def rope(dim, xn, outap, t0, n, cs, perm, psb, tmp1, tmp2, rkeys, wkeys):
        rkeys = tuple(rkeys) + ("cs", "perm")
        mm(PS[psb][:dim, :n], perm, xn, True, True, rkeys, (("ps", psb),))
        tt(tmp1[:dim, :n], xn, cs[:dim, 0, t0:t0 + n], ALU.mult, rkeys, ("rt1",))
        tt(tmp2[:dim, :n], PS[psb][:dim, :n], cs[:dim, 1, t0:t0 + n], ALU.mult, (("ps", psb), "cs"), ("rt2",))
        tt(outap, tmp1[:dim, :n], tmp2[:dim, :n], ALU.add, ("rt1", "rt2"), wkeys)

    def rope_gen(dim, xn, outap, t0, n, cs, perm, psb, tmp1, tmp2, rkeys, wkeys):
        rkeys = tuple(rkeys) + ("cs", "perm")
        mm(PS[psb][:dim, :n], perm, xn, True, True, rkeys, (("ps", psb),))
        tt(tmp1[:dim, :n], xn, cs[:dim, 0, t0:t0 + n], ALU.mult, rkeys, ("rt1",))
        yield
        yield
        tt(tmp2[:dim, :n], PS[psb][:dim, :n], cs[:dim, 1, t0:t0 + n], ALU.mult, (("ps", psb), "cs"), ("rt2",))
        tt(outap, tmp1[:dim, :n], tmp2[:dim, :n], ALU.add, ("rt1", "rt2"), wkeys)
        yield

    def headnorm_rope_gen(raw, rkey, gcol, outap_fn, okey, cs, perm, tmps, psa, psb):
        sq, sd, rs, xn, t1, t2 = tmps
        for ti, (t0, n) in enumerate(T512):
            act(sq[:, :n], raw[:, t0:t0 + n], AF.Square, (rkey,), ("hsq",))
            yield
            yield
            mm(PS[psa][:, :n], ones_f, sq[:, :n], True, True, ("hsq",), (("ps", psa),))
            yield
            act(sd[:, :n], PS[psa][:, :n], AF.Ln, (("ps", psa),), ("hsd",), bias=epsc, scale=1.0 / 128)
            act(rs[:, :n], sd[:, :n], AF.Exp, ("hsd",), ("hrs",), scale=-0.5)
            yield
            stt(xn[:, :n], raw[:, t0:t0 + n], vec[:, gcol:gcol + 1], rs[:, :n], ALU.mult, ALU.mult, (rkey, "hrs"), ("hxn",))
            yield
            yield
            yield from rope_gen(128, xn[:, :n], outap_fn(t0, n), t0, n, cs, perm, psb, t1, t2, ("hxn",), (okey,))

    def headnorm_rope(*a):
        for _ in headnorm_rope_gen(*a): pass

    SBK = (0, 1, 7)
    def attention(qparts, kparts, vfn, scale, rkeys, dst, pt, rd, ot, pacc, bg=None):
        for ti, (t0, n) in enumerate(T512):
            bo = 2 + ti % 2; bd = 4 + ti % 2; ab = ti % 2
            def smm(kc):
                k0, nk = KCH[kc]
                for pi, ((qa, dq), (ka, dk)) in enumerate(zip(qparts, kparts)):
                    mm(PS[SBK[kc % 3]][:nk, :n], ka[:dq, k0:k0 + nk], qa[:dq, t0:t0 + n], pi == 0, pi == len(qparts) - 1,
                       rkeys, (("ps", SBK[kc % 3]),))
            smm(0); smm(1)
            for kc, (k0, nk) in enumerate(KCH):
                if kc + 2 < len(KCH): smm(kc + 2)
                pb = kc % 3; sbk = SBK[kc % 3]
                act(pt[pb][:nk, :n], PS[sbk][:nk, :n], AF.Exp, (("ps", sbk),), (("pt", pb),), scale=scale)
                mm(PS[bo][:, :n], vfn(kc, nk), pt[pb][:nk, :n], kc == 0, kc == len(KCH) - 1, (("pt", pb),) + rkeys, (("ps", bo),))
                mm(PS[bd][:, :n], ones_b[:nk, :], pt[pb][:nk, :n], kc == 0, kc == len(KCH) - 1, (("pt", pb),), (("ps", bd),))
                if bg is not None: next(bg, None)
            act(rd[:, :n], PS[bd][:, :n], AF.Ln, (("ps", bd),), ("ard0",))
            act(rd[:, :n], rd[:, :n], AF.Exp, ("ard0",), ("ard",), scale=-1.0)
            tt(ot[ti % 2][:, :n], PS[bo][:, :n], rd[:, :n], ALU.mult, (("ps", bo), "ard"), (("aot", ti % 2),))
            dma(dst[:, t0:t0 + n], ot[ti % 2][:, :n], (("aot", ti % 2),), ())

    for sq_i in range(nseq):
        xin = [alloc([D], F32) for _ in range(2)]
        hst = [alloc([16, 128], F32) for _ in range(2)]
        hTv = hT.rearrange("(c p) t -> p c t", p=128)
        def p0load(k):
            k0, n = KCH[k]; b = k % 2
            if k == 0:
                dma(xin[b][0:NM, :], meta[:, :], (), (("xin", b),))
                dma(xin[b][NM:128, :], x[sq_i, 0:128 - NM, :], (), (("xin", b),))
            else:
                dma(xin[b][:n, :], x[sq_i, k0 - NM:k0 - NM + n, :], (), (("xin", b),))
        p0load(0)
        for k, (k0, n) in enumerate(KCH):
            if k + 1 < len(KCH): p0load(k + 1)
            b = k % 2
            for c4 in range(4):
                pb = c4 % 2
                for j in range(4):
                    c = c4 * 4 + j
                    S.add("pe", "transpose", (PS[pb][:, j * 128:j * 128 + n], xin[b][:n, c * 128:(c + 1) * 128], ident[:n, :n]), {},
                          (("xin", b), "ident"), (("ps", pb),))
                src = PS[pb][:, :].rearrange("p (j t) -> p j t", j=4)[:, :, :n]
                S.add("dve", "tensor_copy", (), dict(out=hst[b][:, c4 * 4:c4 * 4 + 4, :n], in_=src), (("ps", pb),), (("hst", b),))
            dma(hTv[:, :, k0:k0 + n], hst[b][:, :, :n], (("hst", b),), ())
        phase_end()

        for l in range(nl):
            dma(vec, vecs[l], (), ("vec",))
            S.barrier()
            norm_phase(hT, 16, VC["mixg"], XT, "XT")
            phase_end()
            wv = wview(w_in, l)
            groups = []
            for j in range(8):
                groups.append(("glu", [(128 * j, 128), (1024 + 128 * j, 128)], j))
            for j in range(4): groups.append(("raw", [(2048 + 256 * j, 256)], (qraw, 256 * j)))
            groups.append(("raw", [(3072, 256)], (kraw, 0)))
            groups.append(("vtm", [(3328, 256)], None))
            for j in range(2): groups.append(("raw", [(3584 + 256 * j, 256)], (cqraw, 256 * j)))
            for j in range(2): groups.append(("raw", [(4096 + 256 * j, 256)], (ckvraw, 256 * j)))
            groups.append(("raw64", [(4608, 64)], (kperaw, 0)))
            for j in range(24): groups.append(("gate", [(4672 + 256 * j, 256)], j))
            wb = [alloc([16, 256], BF16) for _ in range(2)]
            tmpf = [alloc([512], F32) for _ in range(2)]
            stb = [alloc([512], BF16) for _ in range(4)]
            stf = [alloc([512], F32) for _ in range(4)]
            vst = [alloc([256], BF16) for _ in range(2)]
            def wload(gi):
                kind, cols, info = groups[gi]; b = gi % 2; o = 0
                for (c0, w) in cols:
                    dma(wb[b][:, :, o:o + w], wv[:, :, c0:c0 + w], (), (("wb", b),), q="pool"); o += w
            wload(0)
            cnt = [0]
            for gi, (kind, cols, info) in enumerate(groups):
                if gi + 1 < len(groups): wload(gi + 1)
                b = gi % 2; W = wb[b]
                if kind == "vtm":
                    for kc, (k0, nk) in enumerate(KCH):
                        pb = kc % 2
                        for c in range(16):
                            mm(PS[pb][:nk, :256], XT[:, c, k0:k0 + nk], W[:, c, 0:256], c == 0, c == 15,
                               (("XT", c, k0 // 512), ("XT", c, (k0 + nk - 1) // 512), ("wb", b)), (("ps", pb),))
                        S.add("act", "activation", (), dict(out=vst[pb][:nk, :], in_=PS[pb][:nk, :256], func=AF.Copy),
                              (("ps", pb),), (("vst", pb),))
                        dma(vtm[k0:k0 + nk, :], vst[pb][:nk, :], (("vst", pb),), ())
                    continue
                nch = 1 if kind == "raw64" else 2
                wd = 64 if kind == "raw64" else 128
                for ti, (t0, n) in enumerate(T512):
                    par = ti % 2
                    for mi in range(nch):
                        pb = par * 2 + mi
                        for c in range(16):
                            mm(PS[pb][:wd, :n], W[:, c, mi * 128:mi * 128 + wd], XT[:, c, t0:t0 + n], c == 0, c == 15,
                               (("XT", c, ti), ("wb", b)), (("ps", pb),))
                    if kind == "glu":
                        j = info; sb_ = cnt[0] % 4; cnt[0] += 1
                        act(tmpf[par][:, :n], PS[par * 2 + 1][:, :n], AF.Sigmoid, (("ps", par * 2 + 1),), (("tmpf", par),))
                        tt(stb[sb_][:, :n], PS[par * 2][:, :n], tmpf[par][:, :n], ALU.mult, (("ps", par * 2), ("tmpf", par)), (("stb", sb_),))
                        dma(zT[j * 128:(j + 1) * 128, 15 + t0:15 + t0 + n], stb[sb_][:, :n], (("stb", sb_),), ())
                    elif kind in ("raw", "raw64"):
                        dst, r0 = info
                        for mi in range(nch):
                            sb_ = cnt[0] % 4; cnt[0] += 1; pb = par * 2 + mi
                            if mi == 0:
                                S.add("act", "activation", (), dict(out=stf[sb_][:wd, :n], in_=PS[pb][:wd, :n], func=AF.Copy), (("ps", pb),), (("stf", sb_),))
                            else:
                                S.add("dve", "tensor_copy", (), dict(out=stf[sb_][:wd, :n], in_=PS[pb][:wd, :n]), (("ps", pb),), (("stf", sb_),))
                            dma(dst[r0 + mi * 128:r0 + mi * 128 + wd, t0:t0 + n], stf[sb_][:wd, :n], (("stf", sb_),), ())
                    elif kind == "gate":
                        j = info
                        for mi in range(2):
                            sb_ = cnt[0] % 4; cnt[0] += 1; pb = par * 2 + mi; ch = 2 * j + mi
                            act(stb[sb_][:, :n], PS[pb][:, :n], AF.Sigmoid, (("ps", pb),), (("stb", sb_),),
                                bias=vec[:, VC["gateb"] + ch:VC["gateb"] + ch + 1])
                            dma(gT[ch * 128:(ch + 1) * 128, t0:t0 + n], stb[sb_][:, :n], (("stb", sb_),), ())
            phase_end()

            identb = alloc([128], BF16)
            S.add("dve", "tensor_copy", (), dict(out=identb, in_=ident), (), ("identb",))
            diag = alloc([248, 128], BF16)
            for i in [k_ * 8 + c_ for c_ in range(8) for k_ in range(31)]:
                S.add("dve", "tensor_scalar", (), dict(out=diag[:, i, :], in0=identb, scalar1=vec[:, VC["dw"] + i:VC["dw"] + i + 1],
                                                      scalar2=None, op0=ALU.mult), ("identb",), (("diag", i % 8),))
            zt = [alloc([8, 542], BF16) for _ in range(2)]
            accs = [alloc([8, 512], F32) for _ in range(2)]; sqr = [alloc([512], F32) for _ in range(3)]
            mu = alloc([512], F32); m2 = alloc([512], F32); var = alloc([512], F32); sd = alloc([512], F32); rs = alloc([512], F32)
            xc = [alloc([512], F32) for _ in range(2)]
            st8 = [alloc([8, 512], BF16) for _ in range(1)]
            sTv = sT.rearrange("(c p) t -> p c t", p=128)
            def p2load(ti):
                t0, n = T512[ti]
                dma(zt[ti % 2][:, :, :n + 30], zTv[:, :, t0:t0 + n + 30], (), (("zt", ti % 2),))
            p2load(0)
            for ti, (t0, n) in enumerate(T512):
                if ti + 1 < len(T512): p2load(ti + 1)
                zb = ti % 2; acc = accs[zb]
                def stats(c):
                    mm(PS[4 + 2 * zb][:, :n], ones_f, acc[:, c, :n], c == 0, c == 7, (("acc", zb, c),), (("ps", 4 + 2 * zb),))
                    mm(PS[5 + 2 * zb][:, :n], ones_f, sqr[c % 3][:, :n], c == 0, c == 7, (("sqr", c % 3),), (("ps", 5 + 2 * zb),))
                for c in range(8):
                    pb = c % 4
                    for k in range(31):
                        mm(PS[pb][:, :n], diag[:, k * 8 + c, :], zt[zb][:, c, k:k + n], k == 0, k == 30,
                           (("zt", zb), ("diag", c)), (("ps", pb),))
                    act(acc[:, c, :n], PS[pb][:, :n], AF.Identity, (("ps", pb),), (("acc", zb, c),),
                        bias=vec[:, VC["convb"] + c:VC["convb"] + c + 1])
                    act(sqr[c % 3][:, :n], acc[:, c, :n], AF.Square, (("acc", zb, c),), (("sqr", c % 3),))
                    if c > 0: stats(c - 1)
                stats(7)
                S.add("dve", "tensor_scalar", (), dict(out=mu[:, :n], in0=PS[4 + 2 * zb][:, :n], scalar1=1.0 / CC, scalar2=None, op0=ALU.mult), (("ps", 4 + 2 * zb),), ("mu",))
                tt(m2[:, :n], mu[:, :n], mu[:, :n], ALU.mult, ("mu",), ("m2",))
                stt(var[:, :n], PS[5 + 2 * zb][:, :n], 1.0 / CC, m2[:, :n], ALU.mult, ALU.subtract, (("ps", 5 + 2 * zb), "m2"), ("var",))
                act(sd[:, :n], var[:, :n], AF.Ln, ("var",), ("sd",), bias=epsc)
                act(rs[:, :n], sd[:, :n], AF.Exp, ("sd",), ("rs",), scale=-0.5)
                for c in range(8):
                    xb = c % 2
                    tt(xc[xb][:, :n], acc[:, c, :n], mu[:, :n], ALU.subtract, (("acc", zb, c), "mu"), (("xc", xb),))
                    tt(xc[xb][:, :n], xc[xb][:, :n], rs[:, :n], ALU.mult, (("xc", xb), "rs"), (("xc", xb),))
                    act(st8[0][:, c, :n], xc[xb][:, :n], AF.Silu, (("xc", xb),), (("st8", 0),),
                        bias=vec[:, VC["lnb"] + c:VC["lnb"] + c + 1], scale=vec[:, VC["lng"] + c:VC["lng"] + c + 1])
                dma(sTv[:, :, t0:t0 + n], st8[0][:, :, :n], (("st8", 0),), ())
            phase_end()

            cs = alloc([2, L], F32); perm = alloc([128], F32)
            dma(cs[:, 0, :], ccs128[0], (), ("cs",)); dma(cs[:, 1, :], ccs128[1], (), ("cs",))
            dma(perm, cp128, (), ("perm",))
            kr = alloc([2, L], F32); kb = alloc([2, L], BF16); vb = alloc([17, 256], BF16)
            dma(kr, kraw.rearrange("(c p) t -> p c t", p=128), (), ("kr",))
            dma(vb[:, 0:16, :], vtm[0:2048, :].rearrange("(k p) d -> p k d", p=128), (), ("vb",))
            dma(vb[0:16, 16, :], vtm[2048:L, :], (), ("vb",))
            tmps = (alloc([512], F32), alloc([512], F32), alloc([512], F32), alloc([512], F32), alloc([512], F32), alloc([512], F32))
            for kvh in range(2):
                headnorm_rope(kr[:, kvh, :], "kr", VC["gkg"], lambda t0, n, kvh=kvh: kb[:, kvh, t0:t0 + n], "kb", cs, perm, tmps, 6, 6)
            qr = [alloc([L], F32) for _ in range(2)]; qb = [alloc([L], BF16) for _ in range(2)]
            pt = [alloc([512], BF16) for _ in range(3)]; rd = alloc([512], F32); ot = [alloc([512], BF16) for _ in range(2)]
            pacc = None
            dma(qr[0], qraw[0:128, :], (), (("qr", 0),))
            dma(qr[1], qraw[128:256, :], (), (("qr", 1),))
            def qgen(h):
                hb = h % 2
                return headnorm_rope_gen(qr[hb], ("qr", hb), VC["gqg"], lambda t0, n, hb=hb: qb[hb][:, t0:t0 + n], ("qb", hb), cs, perm, tmps, 6, 6)
            for _ in qgen(0): pass
            for h in range(8):
                hb = h % 2; kvh = h // 4
                bg = qgen(h + 1) if h + 1 < 8 else None
                attention([(qb[hb], 128)], [(kb[:, kvh, :], 128)], lambda kc, nk, kvh=kvh: vb[:nk, kc, kvh * 128:(kvh + 1) * 128],
                          1.0 / np.sqrt(128.0), (("qb", hb), "kb", "vb"), ogT[h * 128:(h + 1) * 128, :], pt, rd, ot, pacc, bg)
                if bg is not None:
                    for _ in bg: pass
                if h + 2 < 8: dma(qr[hb], qraw[(h + 2) * 128:(h + 3) * 128, :], (), (("qr", hb),))
            phase_end()

            cqn = alloc([4, L], BF16); ckvn = alloc([4, L], BF16)
            cs = alloc([2, L], F32); perm = alloc([64], F32)
            dma(cs[0:64, 0, :], ccs64[0], (), ("cs",)); dma(cs[0:64, 1, :], ccs64[1], (), ("cs",))
            dma(perm[0:64, :], cp64, (), ("perm",))
            wuq = alloc([4, 1536], BF16); wukv = alloc([4, 2048], BF16)
            dma(wuq, wview(w_uq, l), (), ("wuq",), q="pool"); dma(wukv, wview(w_ukv, l), (), ("wukv",), q="pool")
            kpb = alloc([L], BF16)
            t1 = alloc([512], F32); t2 = alloc([512], F32); xq = alloc([512], F32)
            mark2 = top[0]
            kpr = alloc([L], F32)
            dma(kpr[0:64, :], kperaw, (), ("kpr",))
            m3 = top[0]
            norm_phase(cqraw, 4, VC["mqg"], cqn, "cqn")
            top[0] = m3
            norm_phase(ckvraw, 4, VC["mkvg"], ckvn, "ckvn")
            for ti, (t0, n) in enumerate(T512):
                rope(64, kpr[0:64, t0:t0 + n], kpb[0:64, t0:t0 + n], t0, n, cs, perm[0:64, :], 6, t1, t2, ("kpr", "cs", "perm"), ("kpb",))
            S.barrier(); top[0] = mark2
            qnb = [alloc([L], BF16) for _ in range(2)]; qpb = [alloc([L], BF16) for _ in range(2)]
            knb = [alloc([L], BF16) for _ in range(2)]; vhb = [alloc([17, 128], BF16) for _ in range(2)]
            pt = [alloc([512], BF16) for _ in range(3)]; rd = alloc([512], F32); ot = [alloc([512], BF16) for _ in range(2)]
            pacc = None
            def mgen(h):
                hb = h % 2
                for ti, (t0, n) in enumerate(T512):
                    tk = tuple(("cqn", c, ti) for c in range(4)); tk2 = tuple(("ckvn", c, ti) for c in range(4))
                    for c in range(4):
                        mm(PS[6][:, :n], wuq[:, c, h * 192:h * 192 + 128], cqn[:, c, t0:t0 + n], c == 0, c == 3, tk + ("wuq",), (("ps", 6),))
                    yield
                    S.add("act", "activation", (), dict(out=qnb[hb][:, t0:t0 + n], in_=PS[6][:, :n], func=AF.Copy), (("ps", 6),), (("qnb", hb),))
                    for c in range(4):
                        mm(PS[6][:64, :n], wuq[:, c, h * 192 + 128:h * 192 + 192], cqn[:, c, t0:t0 + n], c == 0, c == 3, tk + ("wuq",), (("ps", 6),))
                    yield
                    S.add("dve", "tensor_copy", (), dict(out=xq[0:64, :n], in_=PS[6][:64, :n]), (("ps", 6),), ("xq",))
                    yield
                    yield from rope_gen(64, xq[0:64, :n], qpb[hb][0:64, t0:t0 + n], t0, n, cs, perm[0:64, :], 6, t1, t2, ("xq",), (("qpb", hb),))
                    for c in range(4):
                        mm(PS[6][:, :n], wukv[:, c, h * 256:h * 256 + 128], ckvn[:, c, t0:t0 + n], c == 0, c == 3, tk2 + ("wukv",), (("ps", 6),))
                    yield
                    S.add("act", "activation", (), dict(out=knb[hb][:, t0:t0 + n], in_=PS[6][:, :n], func=AF.Copy), (("ps", 6),), (("knb", hb),))
                    yield
                for kc, (k0, nk) in enumerate(KCH):
                    tk2 = tuple(("ckvn", c, tt_) for c in range(4) for tt_ in {k0 // 512, (k0 + nk - 1) // 512})
                    for c in range(4):
                        mm(PS[6][:nk, :128], ckvn[:, c, k0:k0 + nk], wukv[:, c, h * 256 + 128:h * 256 + 256], c == 0, c == 3, tk2 + ("wukv",), (("ps", 6),))
                    yield
                    S.add("dve", "tensor_copy", (), dict(out=vhb[hb][:nk, kc, :], in_=PS[6][:nk, :128]), (("ps", 6),), (("vhb", hb),))
                    yield
            for _ in mgen(0): pass
            for h in range(8):
                hb = h % 2
                bg = mgen(h + 1) if h + 1 < 8 else None
                attention([(qnb[hb], 128), (qpb[hb], 64)], [(knb[hb], 128), (kpb, 64)], lambda kc, nk, hb=hb: vhb[hb][:nk, kc, :],
                          1.0 / np.sqrt(192.0), (("qnb", hb), ("qpb", hb), ("knb", hb), "kpb", ("vhb", hb)), omT[h * 128:(h + 1) * 128, :], pt, rd, ot, pacc, bg)
                if bg is not None:
                    for _ in bg: pass
            phase_end()

            br = [alloc([8, L], BF16) for _ in range(3)]
            for ti_, (t0_, n_) in enumerate(T512):
                for i, src in enumerate((sT, ogT, omT)):
                    dma(br[i][:, :, t0_:t0_ + n_], src.rearrange("(c p) t -> p c t", p=128)[:, :, t0_:t0_ + n_], (), (("br", i, ti_),))
            wm = [alloc([24, 128], BF16) for _ in range(2)]
            gt = [alloc([3, 512], BF16) for _ in range(2)]
            f0 = alloc([512], F32); f1 = alloc([512], F32)
            wsrc = [wview(w_co, l), wview(w_go, l), wview(w_mo, l)]
            gTv = gT.rearrange("(b f) t -> f b t", b=3)
            def p5w(m):
                for i in range(3):
                    dma(wm[m % 2][:, i * 8:(i + 1) * 8, :], wsrc[i][:, :, m * 128:(m + 1) * 128], (), (("wm", m % 2),), q="pool")
            p5w(0)
            it = 0
            its = [(m, ti) for m in range(16) for ti in range(len(T512))]
            def p5g(i):
                m_, ti_ = its[i]; t0_, n_ = T512[ti_]
                dma(gt[i % 2][:, :, :n_], gTv[m_ * 128:(m_ + 1) * 128, :, t0_:t0_ + n_], (), (("gt", i % 2),))
            p5g(0)
            for m in range(16):
                if m + 1 < 16: p5w(m + 1)
                for ti, (t0, n) in enumerate(T512):
                    par = it % 2; it += 1
                    if it < len(its): p5g(it)
                    for i in range(3):
                        pb = par * 3 + i
                        for c in range(8):
                            mm(PS[pb][:, :n], wm[m % 2][:, i * 8 + c, :], br[i][:, c, t0:t0 + n], c == 0, c == 7, (("br", i, ti), ("wm", m % 2)), (("ps", pb),))
                    tt(f0[:, :n], PS[par * 3][:, :n], gt[par][:, 0, :n], ALU.mult, (("ps", par * 3), ("gt", par)), ("f0",))
                    tt(f1[:, :n], PS[par * 3 + 1][:, :n], gt[par][:, 1, :n], ALU.mult, (("ps", par * 3 + 1), ("gt", par)), ("f1",))
                    tt(f0[:, :n], f0[:, :n], f1[:, :n], ALU.add, ("f0", "f1"), ("f0",))
                    tt(f1[:, :n], PS[par * 3 + 2][:, :n], gt[par][:, 2, :n], ALU.mult, (("ps", par * 3 + 2), ("gt", par)), ("f1",))
                    tt(XT[:, m, t0:t0 + n], f0[:, :n], f1[:, :n], ALU.add, ("f0", "f1"), (("XT", m, ti),))
            phase_end()

            def resid_gemm(wl, KCt, xfn, xkeys_fn, ncol_chunks=1):
                pass

            wo = [alloc([16, 128], BF16) for _ in range(2)]
            hr = [alloc([512], F32) for _ in range(2)]; ho = [alloc([512], F32) for _ in range(2)]
            wov = wview(w_o, l)
            dma(wo[0], wov[:, :, 0:128], (), (("wo", 0),), q="pool")
            it = 0
            its = [(m, ti) for m in range(16) for ti in range(len(T512))]
            def p5h(i):
                m_, ti_ = its[i]; t0_, n_ = T512[ti_]
                dma(hr[i % 2][:, :n_], hT[m_ * 128:(m_ + 1) * 128, t0_:t0_ + n_], (), (("hr", i % 2),))
            p5h(0)
            for m in range(16):
                if m + 1 < 16: dma(wo[(m + 1) % 2], wov[:, :, (m + 1) * 128:(m + 2) * 128], (), (("wo", (m + 1) % 2),), q="pool")
                for ti, (t0, n) in enumerate(T512):
                    par = it % 2; it += 1
                    if it < len(its): p5h(it)
                    for c in range(16):
                        mm(PS[par][:, :n], wo[m % 2][:, c, :], XT[:, c, t0:t0 + n], c == 0, c == 15, (("XT", c, ti), ("wo", m % 2)), (("ps", par),))
                    tt(ho[par][:, :n], PS[par][:, :n], hr[par][:, :n], ALU.add, (("ps", par), ("hr", par)), (("ho", par),))
                    dma(hT[m * 128:(m + 1) * 128, t0:t0 + n], ho[par][:, :n], (("ho", par),), ())
            phase_end()

            norm_phase(hT, 16, VC["ffng"], XT, "XT")
            phase_end()
            wb = [alloc([16, 256], BF16) for _ in range(2)]
            tmpf = [alloc([512], F32) for _ in range(2)]; stb = [alloc([512], BF16) for _ in range(2)]
            wgv = wview(w_fg, l); wuv = wview(w_fu, l)
            def p6w(j):
                dma(wb[j % 2][:, :, 0:128], wgv[:, :, j * 128:(j + 1) * 128], (), (("wb", j % 2),), q="pool")
                dma(wb[j % 2][:, :, 128:256], wuv[:, :, j * 128:(j + 1) * 128], (), (("wb", j % 2),), q="pool")
            p6w(0)
            it = 0
            for j in range(44):
                if j + 1 < 44: p6w(j + 1)
                for ti, (t0, n) in enumerate(T512):
                    par = it % 2; it += 1
                    for mi in range(2):
                        pb = par * 2 + mi
                        for c in range(16):
                            mm(PS[pb][:, :n], wb[j % 2][:, c, mi * 128:(mi + 1) * 128], XT[:, c, t0:t0 + n], c == 0, c == 15,
                               (("XT", c, ti), ("wb", j % 2)), (("ps", pb),))
                    act(tmpf[par][:, :n], PS[par * 2][:, :n], AF.Silu, (("ps", par * 2),), (("tmpf", par),))
                    tt(stb[par][:, :n], PS[par * 2 + 1][:, :n], tmpf[par][:, :n], ALU.mult, (("ps", par * 2 + 1), ("tmpf", par)), (("stb", par),))
                    dma(aTv[:, ti, j, :n], stb[par][:, :n], (("stb", par),), ())
            phase_end()
            wd = [SB[:, xt_off:xt_off + 44 * 512 * 2].bitcast(BF16).rearrange("p (a b) -> p a b", a=44), alloc([44, 512], BF16)]
            at = [alloc([22, 512], BF16) for _ in range(2)]
            hr = [alloc([4, 512], F32) for _ in range(2)]; ho = [alloc([4, 512], F32) for _ in range(2)]
            wdv = wview(w_fd, l)
            hTv2 = hT.rearrange("(c p) t -> p c t", p=128)
            def wdload(me_):
                for pc in range(4):
                    dma(wd[me_ % 2][:, pc * 11:(pc + 1) * 11, :], wdv[:, pc * 11:(pc + 1) * 11, me_ * 512:(me_ + 1) * 512], (), (("wd", me_ % 2, pc),), q="pool")
            wdload(0)
            it = 0; ia = 0
            steps = [(me, ti, kh) for me in range(4) for ti in range(len(T512)) for kh in range(2)]
            def p6l(i):
                me_, ti_, kh_ = steps[i]; t0_, n_ = T512[ti_]
                if kh_ == 0:
                    dma(hr[(i // 2) % 2][:, :, :n_], hTv2[:, me_ * 4:me_ * 4 + 4, t0_:t0_ + n_], (), (("hr", (i // 2) % 2),))
                dma(at[i % 2][:, :, :n_], aTv[:, ti_, kh_ * 22:(kh_ + 1) * 22, :n_], (), (("at", i % 2),))
            p6l(0)
            for me in range(4):
                if me + 1 < 4: wdload(me + 1)
                for ti, (t0, n) in enumerate(T512):
                    par = it % 2; it += 1
                    for kh in range(2):
                        ab = ia % 2; ia += 1
                        if ia < len(steps): p6l(ia)
                        for mi in range(4):
                            pb = par * 4 + mi
                            for c in range(22):
                                mm(PS[pb][:, :n], wd[me % 2][:, kh * 22 + c, mi * 128:(mi + 1) * 128], at[ab][:, c, :n],
                                   kh == 0 and c == 0, kh == 1 and c == 21, (("at", ab), ("wd", me % 2, (kh * 22 + c) // 11)), (("ps", pb),))
                    for mi in range(4):
                        pb = par * 4 + mi
                        tt(ho[par][:, mi, :n], PS[pb][:, :n], hr[par][:, mi, :n], ALU.add, (("ps", pb), ("hr", par)), (("ho", par),))
                    dma(hTv2[:, me * 4:me * 4 + 4, t0:t0 + n], ho[par][:, :, :n], (("ho", par),), ())
            phase_end()

        hin = [alloc([16, 128], F32) for _ in range(2)]; sq = alloc([16, 128], F32)
        sd = alloc([128], F32); rs = alloc([128], F32); un = alloc([16, 128], F32)
        orow = [alloc([D], F32) for _ in range(2)]
        hTv = hT.rearrange("(c p) t -> p c t", p=128)
        dma(hin[0], hTv[:, :, NM:NM + 128], (), (("hin", 0),))
        for kk in range(16):
            if kk + 1 < 16: dma(hin[(kk + 1) % 2], hTv[:, :, NM + (kk + 1) * 128:NM + (kk + 2) * 128], (), (("hin", (kk + 1) % 2),))
            b = kk % 2
            act(sq, hin[b], AF.Square, (("hin", b),), ("sq",))
            for c in range(16):
                mm(PS[7][:, :128], ones_f, sq[:, c, :], c == 0, c == 15, ("sq",), (("ps", 7),))
            act(sd, PS[7][:, :128], AF.Ln, (("ps", 7),), ("sd",), bias=epsc, scale=1.0 / D)
            act(rs, sd, AF.Exp, ("sd",), ("rs",), scale=-0.5)
            for c in range(16):
                stt(un[:, c, :], hin[b][:, c, :], vec[:, VC["fing"] + c:VC["fing"] + c + 1], rs, ALU.mult, ALU.mult, (("hin", b), "rs"), (("un", c),))
            for c4 in range(4):
                pb = c4 % 2
                for j in range(4):
                    c = c4 * 4 + j
                    S.add("pe", "transpose", (PS[pb][:, j * 128:(j + 1) * 128], un[:, c, :], ident), {}, (("un", c),), (("ps", pb),))
                S.add("dve", "tensor_copy", (), dict(out=orow[b][:, c4 * 512:(c4 + 1) * 512], in_=PS[pb][:, :]), (("ps", pb),), (("orow", b),))
            dma(out[sq_i, kk * 128:(kk + 1) * 128, :], orow[b], (("orow", b),), ())
        phase_end()

    S.emit(nc, stack)
    stack.close()
    return nc


def _consts():
    def tables(dim):
        half_blk = dim // 2
        hh = half_blk // 2
        inv = (10000.0 ** (-np.arange(hh, dtype=np.float32) / np.float32(hh))).astype(np.float32)
        rows = np.concatenate([np.zeros(NM, np.float32), np.repeat(np.arange(SEQ // 64, dtype=np.float32), 64)])
        cols = np.concatenate([np.zeros(NM, np.float32), np.tile(np.arange(64, dtype=np.float32), SEQ // 64)])
        cs = np.zeros((2, dim, L), np.float32)
        perm = np.zeros((dim, dim), np.float32)
        for d in range(dim):
            blk = d // half_blk; dd = d % half_blk; j = dd % hh
            pos = rows if blk == 0 else cols
            ang = (pos * inv[j]).astype(np.float32)
            cs[0, d] = np.cos(ang)
            s = np.sin(ang)
            if dd < hh:
                cs[1, d] = -s; src = d + hh
            else:
                cs[1, d] = s; src = d - hh
            perm[src, d] = 1.0
        return cs, perm
    cs128, p128 = tables(128)
    cs64, p64 = tables(64)
    return dict(cident=np.eye(128, dtype=np.float32), cp128=p128, cp64=p64, ccs128=cs128, ccs64=cs64)


def _vecs(p, nl):
    v = np.zeros((nl, 128, NV), np.float32)
    def put(name, arr, l):
        a = np.asarray(arr, np.float32).reshape(-1, 128).T
        v[l, :, VC[name]:VC[name] + a.shape[1]] = a
    for l in range(nl):
        put("mixg", p["mix_norm_g"][l], l); put("ffng", p["ffn_norm_g"][l], l); put("gateb", p["gate_b"][l], l)
        put("convb", p["conv_b"][l], l); put("lng", p["conv_ln_g"][l], l); put("lnb", p["conv_ln_b"][l], l)
        dw = np.asarray(p["conv_dw"][l], np.float32).reshape(31, 8, 128)
        v[l, :, VC["dw"]:VC["dw"] + 248] = dw.reshape(248, 128).T
        put("gqg", p["gqa_q_norm_g"][l], l); put("gkg", p["gqa_k_norm_g"][l], l)
        put("mqg", p["mla_q_norm_g"][l], l); put("mkvg", p["mla_kv_norm_g"][l], l)
        put("fing", p["final_norm_g"], l)
    return v


_NC_CACHE = {}


def run(p, nl, nseq, ncores):
    key = (nl, nseq)
    if key not in _NC_CACHE: _NC_CACHE[key] = build(nl, nseq)
    nc = _NC_CACHE[key]
    consts = _consts()
    f = lambda a: np.ascontiguousarray(np.asarray(a, np.float32))
    shared = dict(meta=f(p["meta_tokens"]), vecs=_vecs(p, nl), w_in=f(p["w_in"][:nl]), w_co=f(p["w_conv_out"][:nl]),
                  w_go=f(p["w_gqa_out"][:nl]), w_mo=f(p["w_mla_out"][:nl]), w_uq=f(p["w_mla_uq"][:nl]),
                  w_ukv=f(p["w_mla_ukv"][:nl]), w_o=f(p["w_out"][:nl]), w_fg=f(p["w_ffn_gate"][:nl]),
                  w_fu=f(p["w_ffn_up"][:nl]), w_fd=f(p["w_ffn_down"][:nl]), **consts)
    xs = f(p["x"])
    in_maps = [dict(shared, x=xs[i * nseq:(i + 1) * nseq]) for i in range(ncores)]
    res = run_bass_kernel_spmd(nc, in_maps, core_ids=list(range(ncores)))
    return np.concatenate([r["out"] for r in res.results], axis=0)


def kernel(**inputs):
    return run(inputs, 4, 2, 8)
```
